# Optimizing a Trainium2 kernel written in Bass

```python
import jax, jax.numpy as jnp
from jax import lax
import numpy as np

D_MODEL = 1024
BATCH = 8
SEQ = 4096
DEPTH = 2

CHUNK = 64
D_MIX = D_MODEL

GLA_DV = 96
GLA_DK = 48
GLA_WIDTH = 3 * D_MODEL // 8
GLA_HEADS = GLA_WIDTH // GLA_DV
GLA_KEY_WIDTH = GLA_HEADS * GLA_DK
GLA_GATE_RANK = 16
GLA_GATE_TAU = 16.0

CONV_WIDTH = D_MODEL // 4
CONV_KERNEL = 31

ATT_HEAD_DIM = 64
ATT_WIDTH = 3 * D_MODEL // 8
ATT_HEADS = ATT_WIDTH // ATT_HEAD_DIM
ATT_LEFT_CHUNKS = 8
ATT_BAND_CHUNKS = ATT_LEFT_CHUNKS + 1
ATT_BAND = ATT_BAND_CHUNKS * CHUNK
MAX_REL_DIST = 128
N_REL = 2 * MAX_REL_DIST + 1

D_FF = 4 * D_MODEL

EPS = 1e-6
NEG_INF = -1e30

IN_SIZES = (
    GLA_KEY_WIDTH,
    GLA_KEY_WIDTH,
    GLA_WIDTH,
    GLA_WIDTH,
    GLA_GATE_RANK,
    2 * CONV_WIDTH,
    ATT_WIDTH,
    ATT_WIDTH,
    ATT_WIDTH,
)
D_IN = int(sum(IN_SIZES))
IN_SPLITS = [int(s) for s in np.cumsum(IN_SIZES)[:-1]]

kernel_name = "hybrid_gla_conformer_chunkattn_encoder"


def rmsnorm(x, g):
    xf = x.astype(jnp.float32)
    y = xf * lax.rsqrt(jnp.mean(xf * xf, axis=-1, keepdims=True) + EPS)
    return (y * g.astype(jnp.float32)).astype(x.dtype)


def gla_mixer(q, k, v, g, gate_lr, w_gate, b_gate, out_norm):
    dtype = v.dtype
    B, S = q.shape[:2]
    nc = S // CHUNK
    f32 = jnp.float32
    shp_k = (B, nc, CHUNK, GLA_HEADS, GLA_DK)
    qf = q.astype(f32).reshape(shp_k) * (GLA_DK ** -0.5)
    kf = k.astype(f32).reshape(shp_k)
    vf = v.astype(f32).reshape(B, nc, CHUNK, GLA_HEADS, GLA_DV)
    z = gate_lr.astype(f32) @ w_gate.astype(f32) + b_gate.astype(f32)
    log_a = (jax.nn.log_sigmoid(z) / GLA_GATE_TAU).reshape(shp_k)
    cum = jnp.cumsum(log_a, axis=2)
    end = cum[:, :, -1:]
    k_dec = kf * jnp.exp(end - cum)
    chunk_decay = jnp.exp(end[:, :, 0])
    kv = jnp.einsum('bnchk,bnchv->bnhkv', k_dec, vf)

    def step(state, inp):
        a, kv_c = inp
        state = a[..., None] * state + kv_c
        return state, state

    init = jnp.zeros((B, GLA_HEADS, GLA_DK, GLA_DV), f32)
    _, states = lax.scan(step, init, (jnp.moveaxis(chunk_decay, 1, 0), jnp.moveaxis(kv, 1, 0)))
    o = jnp.einsum('bnchk,nbhkv->bnchv', qf, states)
    o = o.reshape(B, S, GLA_HEADS, GLA_DV)
    o = o * lax.rsqrt(jnp.mean(o * o, axis=-1, keepdims=True) + EPS)
    o = o.reshape(B, S, GLA_WIDTH) * out_norm.astype(f32)
    o = o * jax.nn.silu(g.astype(f32))
    return o.astype(dtype)


def conv_mixer(u, w_dw, b_dw, ln_g, ln_b):
    dtype = u.dtype
    a, b = jnp.split(u, 2, axis=-1)
    h = a * jax.nn.sigmoid(b)
    h = lax.conv_general_dilated(
        h, w_dw.reshape(CONV_KERNEL, 1, CONV_WIDTH).astype(h.dtype),
        window_strides=(1,), padding=((CONV_KERNEL - 1, 0),),
        dimension_numbers=('NWC', 'WIO', 'NWC'), feature_group_count=CONV_WIDTH)
    hf = h.astype(jnp.float32) + b_dw.astype(jnp.float32)
    mu = jnp.mean(hf, axis=-1, keepdims=True)
    var = jnp.mean(jnp.square(hf - mu), axis=-1, keepdims=True)
    hf = (hf - mu) * lax.rsqrt(var + EPS) * ln_g.astype(jnp.float32) + ln_b.astype(jnp.float32)
    return jax.nn.silu(hf).astype(dtype)


def chunk_attention(q, k, v, rel_bias):
    dtype = v.dtype
    B, S = q.shape[:2]
    nc = S // CHUNK
    shp = (B, nc, CHUNK, ATT_HEADS, ATT_HEAD_DIM)
    qc, kc, vc = q.reshape(shp), k.reshape(shp), v.reshape(shp)
    pad = ((0, 0), (ATT_LEFT_CHUNKS, 0), (0, 0), (0, 0), (0, 0))
    kp, vp = jnp.pad(kc, pad), jnp.pad(vc, pad)
    k_band = jnp.concatenate([kp[:, w:w + nc] for w in range(ATT_BAND_CHUNKS)], axis=2)
    v_band = jnp.concatenate([vp[:, w:w + nc] for w in range(ATT_BAND_CHUNKS)], axis=2)
    scores = jnp.einsum('bnqhd,bnkhd->bnhqk', qc, k_band).astype(jnp.float32) * (ATT_HEAD_DIM ** -0.5)
    q_pos = np.arange(CHUNK)[:, None]
    k_pos = np.arange(ATT_BAND)[None, :] - ATT_LEFT_CHUNKS * CHUNK
    rel_idx = np.clip(q_pos - k_pos, -MAX_REL_DIST, MAX_REL_DIST) + MAX_REL_DIST
    bias = rel_bias.astype(jnp.float32)[:, rel_idx]
    key_chunk = np.arange(nc)[:, None] - ATT_LEFT_CHUNKS + np.repeat(np.arange(ATT_BAND_CHUNKS), CHUNK)[None, :]
    valid = key_chunk >= 0
    scores = jnp.where(valid[None, :, None, None, :], scores + bias[None, None], NEG_INF)
    p = jax.nn.softmax(scores, axis=-1).astype(dtype)
    o = jnp.einsum('bnhqk,bnkhd->bnqhd', p, v_band)
    return o.reshape(B, S, ATT_WIDTH)


def sq_relu_mlp(x, w_up, w_down):
    h = jax.nn.relu(x @ w_up)
    return (h * h) @ w_down


def setup_inputs(seed: int = 0) -> dict:
    key = jax.random.key(seed)
    ks = jax.random.split(key, 20)
    f32 = jnp.float32
    nrm = lambda k, shape, s: (jax.random.normal(k, shape, f32) * s)
    return {
        "x": nrm(ks[0], (BATCH, SEQ, D_MODEL), 1.0),
        "norm_mix": 1.0 + nrm(ks[1], (DEPTH, D_MODEL), 0.01),
        "w_in": nrm(ks[2], (DEPTH, D_MODEL, D_IN), D_MODEL ** -0.5),
        "w_gla_gate": nrm(ks[3], (DEPTH, GLA_GATE_RANK, GLA_KEY_WIDTH), GLA_GATE_RANK ** -0.5),
        "b_gla_gate": nrm(ks[4], (DEPTH, GLA_KEY_WIDTH), 0.1),
        "gla_norm": 1.0 + nrm(ks[5], (DEPTH, GLA_WIDTH), 0.01),
        "w_dw": nrm(ks[6], (DEPTH, CONV_KERNEL, CONV_WIDTH), CONV_KERNEL ** -0.5),
        "b_dw": nrm(ks[7], (DEPTH, CONV_WIDTH), 0.01),
        "conv_ln_g": 1.0 + nrm(ks[8], (DEPTH, CONV_WIDTH), 0.01),
        "conv_ln_b": nrm(ks[9], (DEPTH, CONV_WIDTH), 0.01),
        "rel_bias": nrm(ks[10], (DEPTH, ATT_HEADS, N_REL), 0.1),
        "w_out": nrm(ks[11], (DEPTH, D_MIX, D_MODEL), D_MIX ** -0.5),
        "norm_ffn": 1.0 + nrm(ks[12], (DEPTH, D_MODEL), 0.01),
        "w_up": nrm(ks[13], (DEPTH, D_MODEL, D_FF), D_MODEL ** -0.5),
        "w_down": nrm(ks[14], (DEPTH, D_FF, D_MODEL), D_FF ** -0.5),
        "norm_final": 1.0 + nrm(ks[15], (D_MODEL,), 0.01),
    }


def reference(x, norm_mix, w_in, w_gla_gate, b_gla_gate, gla_norm, w_dw, b_dw,
              conv_ln_g, conv_ln_b, rel_bias, w_out, norm_ffn, w_up, w_down, norm_final):
    h = x
    for l in range(DEPTH):
        xn = rmsnorm(h, norm_mix[l])
        proj = xn @ w_in[l]
        (g_q, g_k, g_v, g_g, g_lr, c_u, a_q, a_k, a_v) = jnp.split(proj, IN_SPLITS, axis=-1)
        o_gla = gla_mixer(g_q, g_k, g_v, g_g, g_lr, w_gla_gate[l], b_gla_gate[l], gla_norm[l])
        o_conv = conv_mixer(c_u, w_dw[l], b_dw[l], conv_ln_g[l], conv_ln_b[l])
        o_att = chunk_attention(a_q, a_k, a_v, rel_bias[l])
        mixed = jnp.concatenate([o_gla, o_conv, o_att], axis=-1)
        h = h + mixed @ w_out[l]
        h = h + sq_relu_mlp(rmsnorm(h, norm_ffn[l]), w_up[l], w_down[l])
    return rmsnorm(h, norm_final)
```

```python
import numpy as np
from contextlib import ExitStack
import concourse.bass as bass
import concourse.mybir as mybir
from concourse.bass_utils import run_bass_kernel_spmd

F32 = mybir.dt.float32
BF16 = mybir.dt.bfloat16
AF = mybir.ActivationFunctionType
ALU = mybir.AluOpType
AX = mybir.AxisListType

D = 1024
SEQ = 4096
DEPTH = 2
T = 256
NT = SEQ // T
DIN = 2832
DFF = 4096
EPS = 1e-6
GQ, GK, GV, GG, LR, CA, CB, AQ, AK, AV = 0, 192, 384, 768, 1152, 1168, 1424, 1680, 2064, 2448
NEG = -30000.0

C_ID, C_TRI, C_ONES, C_IND, C_J = 0, 128, 256, 384, 386
NCST = 514


def make_consts():
    c = np.zeros((128, NCST), np.float32)
    c[:, C_ID:C_ID + 128] = np.eye(128, dtype=np.float32)
    s = np.arange(128)[:, None]
    t = np.arange(128)[None, :]
    c[:, C_TRI:C_TRI + 128] = np.where((s > t) & (s // 64 == t // 64), -1.0 / 16.0, 0.0)
    c[:, C_ONES:C_ONES + 128] = 1.0
    c[:, C_IND:C_IND + 2] = np.where(s // 64 == np.arange(2)[None, :], -1.0 / 16.0, 0.0)
    c[:, C_J:C_J + 128] = np.eye(128, dtype=np.float32)[::-1]
    return c


class Sched:
    def __init__(self):
        self.ops = []
        self.lw = {}
        self.rd = {}

    def add(self, eng, fn, reads=(), writes=(), dma=None):
        raw = set()
        for k in reads:
            raw |= self.lw.get(k, set())
        oth = set()
        for k in writes:
            oth |= self.lw.get(k, set())
            oth |= set(self.rd.get(k, {}).values())
        i = len(self.ops)
        self.ops.append(dict(eng=eng, fn=fn, raw=raw, oth=oth - raw, dma=dma, need=False))
        sig = ('d', dma) if dma is not None else ('e', eng)
        for k in reads:
            self.rd.setdefault(k, {})[sig] = i
        for k in writes:
            self.lw[k] = {i}
            self.rd[k] = {}
        return i

    def fence(self, src, dst):
        acc = set()
        for k in src:
            acc |= self.lw.get(k, set()) | set(self.rd.get(k, {}).values())
        for k in dst:
            self.lw[k] = self.lw.get(k, set()) | acc

    def emit(self, nc, es):
        ops = self.ops
        CAP = 30000
        for o in ops:
            deps = set()
            for j in o['raw']:
                pj = ops[j]
                if pj['dma'] is not None or pj['eng'] != o['eng'] or o['eng'] != 'pe':
                    deps.add(j)
            for j in o['oth']:
                pj = ops[j]
                if pj['dma'] is not None or pj['eng'] != o['eng'] or o['dma'] is not None:
                    deps.add(j)
            o['deps'] = deps
            for j in deps:
                ops[j]['need'] = True
        cnt = {}
        dcnt = {}
        engsems = {}
        dsems = {}

        def get_esem(e, idx):
            key = (e, idx)
            if key not in engsems:
                engsems[key] = es.enter_context(nc.semaphore("s_%s_%d" % (e, idx)))
            return engsems[key]

        for o in ops:
            if o['dma'] is not None:
                k = o['dma']
                if k not in dsems:
                    dsems[k] = es.enter_context(nc.semaphore("d_%d" % len(dsems)))
                dcnt[k] = dcnt.get(k, 0) + 16
                o['sem'] = dsems[k]
                o['val'] = dcnt[k]
                o['need'] = True
            elif o['need']:
                e = o['eng']
                c = cnt.get(e, 0)
                o['sem'] = get_esem(e, c // CAP)
                o['val'] = c % CAP + 1
                cnt[e] = c + 1
        self.nsem = len(engsems) + len(dsems)
        per = {e: [] for e in ('pe', 'act', 'dve', 'pool', 'sp')}
        for o in ops:
            per[o['eng']].append(o)

        def run(eng_obj, lst):
            waited = {}
            for o in lst:
                for j in sorted(o['deps']):
                    pj = ops[j]
                    sid = id(pj['sem'])
                    if waited.get(sid, 0) >= pj['val']:
                        continue
                    waited[sid] = pj['val']
                    eng_obj.wait_ge(pj['sem'], pj['val'])
                inst = o['fn'](eng_obj)
                if o['dma'] is not None:
                    inst.then_inc(o['sem'], 16)
                elif o['need']:
                    inst.then_inc(o['sem'], 1)
            last = {}
            for o in lst:
                if o['dma'] is not None:
                    last[id(o['sem'])] = (o['sem'], o['val'])
            for sid, (sm, v) in last.items():
                if waited.get(sid, 0) < v:
                    eng_obj.wait_ge(sm, v)

        block = es.enter_context(nc.Block())

        @block.sync
        def _(e):
            run(e, per['sp'])

        @block.gpsimd
        def _(e):
            run(e, per['pool'])

        @block.scalar
        def _(e):
            run(e, per['act'])

        @block.vector
        def _(e):
            run(e, per['dve'])

        @block.tensor
        def _(e):
            run(e, per['pe'])


def build(ntiles=NT, stop_after=None):
    nc = bass.Bass("TRN2", target_bir_lowering=False)
    es = ExitStack()
    dr = {}

    def din(name, shape):
        dr[name] = nc.dram_tensor(name, list(shape), F32, kind="ExternalInput").ap()
        return dr[name]

    x = din("x", [SEQ, D])
    norm_mix = din("norm_mix", [DEPTH, D])
    w_in = din("w_in", [DEPTH, D, DIN])
    w_gla_gate = din("w_gla_gate", [DEPTH, 16, 192])
    b_gla_gate = din("b_gla_gate", [DEPTH, 192])
    gla_norm = din("gla_norm", [DEPTH, 384])
    w_dw = din("w_dw", [DEPTH, 31, 256])
    b_dw = din("b_dw", [DEPTH, 256])
    conv_ln_g = din("conv_ln_g", [DEPTH, 256])
    conv_ln_b = din("conv_ln_b", [DEPTH, 256])
    rel_bias = din("rel_bias", [DEPTH, 6, 257])
    w_out = din("w_out", [DEPTH, D, D])
    norm_ffn = din("norm_ffn", [DEPTH, D])
    w_up = din("w_up", [DEPTH, D, DFF])
    w_down = din("w_down", [DEPTH, DFF, D])
    norm_final = din("norm_final", [D])
    cst = din("cst", [128, NCST])
    out = nc.dram_tensor("out", [SEQ, D], F32, kind="ExternalOutput").ap()
    hs = [nc.dram_tensor("hs%d" % i, [NT, 128, 8 * T], F32, kind="Internal").ap() for i in range(2)]
    xs = nc.dram_tensor("xs", [NT, 128, 8 * T], BF16, kind="Internal").ap()
    ext = nc.dram_tensor("ext", [6, 768], F32, kind="Internal").ap()

    PBYTES = 78 * 1024
    HB = 64 * 1024
    TOT = PBYTES + 2 * HB
    arena_t = es.enter_context(nc.sbuf_tensor("arena", [128, TOT // 4], F32))
    ps = [es.enter_context(nc.psum_tensor("ps%d" % b, [128, 512], F32)) for b in range(8)]

    def view(off, dtype, shape, parts=128, p0=0):
        n = int(np.prod(shape))
        esz = 4 if dtype == F32 else 2
        nb = n * esz
        assert off % 4 == 0
        a = arena_t[p0:p0 + parts, off // 4:(off + nb + 3) // 4]
        if dtype != F32:
            a = a.bitcast(dtype)
        a = a[:, 0:n]
        if len(shape) == 2:
            a = a.rearrange("p (a b) -> p a b", a=shape[0])
        elif len(shape) == 3:
            a = a.rearrange("p (a b c) -> p a b c", a=shape[0], b=shape[1])
        return a

    class Carver:
        def __init__(self, base, limit):
            self.off = base
            self.limit = limit

        def get(self, dtype, shape, parts=128, p0=0):
            n = int(np.prod(shape))
            nb = n * (4 if dtype == F32 else 2)
            nb = (nb + 31) // 32 * 32
            o = self.off
            self.off += nb
            assert self.off <= self.limit, ("arena overflow", self.off, self.limit)
            return view(o, dtype, shape, parts, p0)

    P = Carver(0, PBYTES)
    cst_sb = P.get(F32, [NCST])
    ident_f = cst_sb[:, C_ID:C_ID + 128]
    tri_f = cst_sb[:, C_TRI:C_TRI + 128]
    ones_f = cst_sb[:, C_ONES:C_ONES + 128]
    ind_f = cst_sb[:, C_IND:C_IND + 2]
    J_f = cst_sb[:, C_J:C_J + 128]
    ident_b = P.get(BF16, [128])
    ones_b = P.get(BF16, [128])
    gvec = P.get(F32, [5, 8])
    hT = [P.get(F32, [8, T]) for _ in range(2)]
    xnT = [P.get(BF16, [8, T]) for _ in range(2)]
    mixsq = P.get(BF16, [8, T])
    rstd_b = P.get(F32, [T])
    hidT = P.get(BF16, [16, T])
    relu_t = [P.get(F32, [T]) for _ in range(2)]
    otile = P.get(F32, [2, D])
    gla_k = P.get(F32, [2, 192])
    gla_v = P.get(BF16, [2, 384])
    gla_g = P.get(F32, [2, 384])
    gqT = [P.get(BF16, [T], parts=48) for _ in range(4)]
    aqT = P.get(BF16, [3, T])
    btmp = P.get(F32, [640])
    gnb = P.get(F32, [384])
    go = P.get(F32, [384])

    S = Sched()
    sp_q = 'sp'

    def dma(q, out_ap, in_ap, reads, writes, key, **kw):
        def fn(e, out_ap=out_ap, in_ap=in_ap, kw=kw):
            return e.dma_start(out=out_ap, in_=in_ap, **kw)
        return S.add(q, fn, reads, writes, dma=key)

    def pe_mms(lst, reads, writes):
        def fn(e, lst=lst):
            ins = None
            for (o, l, r, st, sp) in lst:
                ins = e.matmul(o, lhsT=l, rhs=r, start=st, stop=sp)
            return ins
        return S.add('pe', fn, reads, writes)

    def pe_tr(lst, reads, writes):
        def fn(e, lst=lst):
            ins = None
            for (o, i, idn) in lst:
                ins = e.transpose(o, i, idn)
            return ins
        return S.add('pe', fn, reads, writes)

    def act(out_ap, in_ap, func, reads, writes, **kw):
        def fn(e, out_ap=out_ap, in_ap=in_ap, func=func, kw=kw):
            return e.activation(out=out_ap, in_=in_ap, func=func, **kw)
        return S.add('act', fn, reads, writes)

    def ts(eng, out_ap, in0, s1, s2, op0, op1, reads, writes):
        def fn(e, out_ap=out_ap, in0=in0, s1=s1, s2=s2, op0=op0, op1=op1):
            if op1 is None:
                return e.tensor_scalar(out=out_ap, in0=in0, scalar1=s1, scalar2=None, op0=op0)
            return e.tensor_scalar(out=out_ap, in0=in0, scalar1=s1, scalar2=s2, op0=op0, op1=op1)
        return S.add(eng, fn, reads, writes)

    def tt(eng, out_ap, in0, in1, op, reads, writes):
        def fn(e, out_ap=out_ap, in0=in0, in1=in1, op=op):
            return e.tensor_tensor(out=out_ap, in0=in0, in1=in1, op=op)
        return S.add(eng, fn, reads, writes)

    def stt(out_ap, in0, sc, in1, op0, op1, reads, writes):
        def fn(e, out_ap=out_ap, in0=in0, sc=sc, in1=in1, op0=op0, op1=op1):
            return e.scalar_tensor_tensor(out=out_ap, in0=in0, scalar=sc, in1=in1, op0=op0, op1=op1)
        return S.add('dve', fn, reads, writes)

    def cp(eng, out_ap, in_ap, reads, writes):
        if eng == 'act':
            return act(out_ap, in_ap, AF.Copy, reads, writes)
        def fn(e, out_ap=out_ap, in_ap=in_ap):
            return e.tensor_copy(out=out_ap, in_=in_ap)
        return S.add(eng, fn, reads, writes)

    def memset(eng, ap, val, writes):
        def fn(e, ap=ap, val=val):
            return e.memset(ap, val)
        return S.add(eng, fn, (), writes)

    def psb(b):
        return ps[b][:]

    def psbf(b):
        return ps[b][:].bitcast(BF16)

    dma(sp_q, cst_sb, cst[:, :], (), ['cst'], 'cst')
    cp('dve', ident_b, ident_f, ['cst'], ['ident_b'])
    cp('dve', ones_b, ones_f, ['cst'], ['ones_b'])
    gsrc = [norm_mix[0], norm_ffn[0], norm_mix[1], norm_ffn[1], norm_final]
    for i, g in enumerate(gsrc):
        gv_ = g.rearrange("(k p o) -> k p o", p=128, o=1)
        for k in range(8):
            dma(sp_q, gvec[:, i, k:k + 1], gv_[k], (), ['gvec'], 'gvec')
    ts('dve', gvec, gvec, 32.0, None, ALU.mult, None, ['gvec'], ['gvec'])

    WK = [[('W', 0, p) for p in range(12)], [('W', 1, p) for p in range(12)]]

    def hoff(h):
        return PBYTES + h * HB

    def load_w_A(l, h):
        base = hoff(h)
        win = view(base, BF16, [8, DIN])
        wout = view(base + 8 * DIN * 2, BF16, [8, D])
        src_in = w_in[l].rearrange("(k p) n -> p k n", p=128)
        src_out = w_out[l].rearrange("(k p) n -> p k n", p=128)
        for k in range(8):
            dma('pool', win[:, k, :], src_in[:, k, :], (), [WK[h][k]], ('W', h))
        for k in range(0, 8, 4):
            dma('pool', wout[:, k:k + 4, :], src_out[:, k:k + 4, :], (), [WK[h][8 + k // 4]], ('W', h))
        return win, wout

    def load_w_B(l, half, h):
        base = hoff(h)
        wup = view(base, BF16, [8, 2048])
        wdn = view(base + 8 * 2048 * 2, BF16, [16, D])
        src_up = w_up[l].rearrange("(k p) n -> p k n", p=128)
        src_dn = w_down[l].rearrange("(j p) n -> p j n", p=128)
        for k in range(8):
            dma('pool', wup[:, k, :], src_up[:, k, half * 2048:(half + 1) * 2048], (), [WK[h][k]], ('W', h))
        for j in range(0, 16, 4):
            dma('pool', wdn[:, j:j + 4, :], src_dn[:, half * 16 + j:half * 16 + j + 4, :], (), [WK[h][8 + j // 4]],
                ('W', h))
        return wup, wdn

    def rms_stats(buf, gi, dst_is_f32_inplace=False):
        act(mixsq, hT[buf], AF.Square, [('hT', buf)], ['mixsq'])
        lst = [(psb(7)[:, 0:T], ones_b, mixsq[:, k, :], k == 0, k == 7) for k in range(8)]
        pe_mms(lst, ['mixsq', 'ones_b'], [('ps', 7)])
        act(rstd_b, psb(7)[:, 0:T], AF.Ln, [('ps', 7)], ['rstd'], bias=float(D * EPS))
        act(rstd_b, rstd_b, AF.Exp, ['rstd'], ['rstd'], scale=-0.5)
        for k in range(8):
            if dst_is_f32_inplace:
                stt(hT[buf][:, k, :], hT[buf][:, k, :], gvec[:, gi, k:k + 1], rstd_b, ALU.mult, ALU.mult,
                    [('hT', buf), 'rstd', 'gvec'], [('hT', buf)])
            else:
                stt(xnT[buf][:, k, :], hT[buf][:, k, :], gvec[:, gi, k:k + 1], rstd_b, ALU.mult, ALU.mult,
                    [('hT', buf), 'rstd', 'gvec'], [('xnT', buf)])

    state = dict(hs_cur=None)

    def phase_A(l, hw, hsrc, hdst):
        hb = 1 - hw
        AKEY = lambda n: ('A', l, n)
        win, wout = state['wA']
        Hc = Carver(hoff(hb), hoff(hb) + HB)
        dg = Hc.get(BF16, [2, 31, 128])
        biasb = Hc.get(BF16, [6, 640])
        kT = Hc.get(BF16, [3, 768])
        vr = Hc.get(BF16, [6, 6, 65])
        pT = [Hc.get(BF16, [640]) for _ in range(2)]
        lrT = Hc.get(F32, [T], parts=32)
        sig = [Hc.get(F32, [T]) for _ in range(2)]
        hglu = Hc.get(BF16, [2, 32 + T])
        cy = Hc.get(F32, [2, T])
        cysq = Hc.get(F32, [2, T])
        cm = Hc.get(F32, [T])
        cmsq = Hc.get(F32, [T])
        cvar = Hc.get(F32, [T])
        crs = Hc.get(F32, [T])
        cd = Hc.get(F32, [2, T])
        cpar = Hc.get(F32, [3, 2])
        wT = Hc.get(F32, [2, 31])
        wtmp = Hc.get(F32, [256], parts=32)
        g_e = Hc.get(F32, [192])
        g_sp = Hc.get(F32, [192])
        g_ed = Hc.get(F32, [192])
        kdec = Hc.get(BF16, [192])
        Sst = Hc.get(F32, [4, 96], parts=48)
        Sbf = [Hc.get(BF16, [4, 96], parts=48) for _ in range(2)]
        dec = Hc.get(F32, [4, 2], parts=48)
        gsq = Hc.get(F32, [384])
        gms = Hc.get(F32, [4])
        gon = Hc.get(BF16, [384])
        wg = Hc.get(F32, [192], parts=32)
        rden = Hc.get(F32, [6])
        aob = Hc.get(BF16, [384])
        a_keys = [AKEY(n) for n in ('dg', 'bias', 'kT0', 'kT1', 'kT2', 'kT3', 'kT4', 'kT5', 'v0', 'v1', 'v2', 'v3',
                                    'v4', 'v5', 'pT0', 'pT1', 'lrT', 'sig0', 'sig1', 'hglu', 'cy', 'cysq', 'cm',
                                    'cmsq', 'cvar', 'crs', 'cd', 'cpar', 'wT', 'wtmp', 'g_e', 'g_sp', 'g_ed',
                                    'kdec', 'S0', 'S1', 'S2', 'S3', 'Sbf0', 'Sbf1', 'dec', 'go', 'gsq', 'gms', 'gon', 'gnb', 'wg',
                                    'rden', 'aob', 'btmp', 'vones')]
        S.fence(WK[hb], a_keys)

        for i, src in enumerate((b_dw[l], conv_ln_g[l], conv_ln_b[l])):
            sv_ = src.rearrange("(b p o) -> b p o", p=128, o=1)
            for blk in range(2):
                dma(sp_q, cpar[:, i, blk:blk + 1], sv_[blk], (), [AKEY('cpar')], AKEY('cpar'))
        dma(sp_q, wtmp[0:31, :], w_dw[l], (), [AKEY('wtmp')], AKEY('wtmp'))
        for blk in range(2):
            pe_tr([(psb(6)[:, blk * 32:blk * 32 + 31], wtmp[0:31, blk * 128:(blk + 1) * 128], ident_f[0:31, 0:31])],
                  [AKEY('wtmp'), 'cst'], [('ps', 6)])
        cp('dve', wT, psb(6)[:, 0:64].rearrange("p (b j) -> p b j", b=2)[:, :, 0:31], [('ps', 6)], [AKEY('wT')])
        for blk in range(2):
            for j in range(31):
                ts('pool', dg[:, blk, j, :], ident_b, wT[:, blk, j:j + 1], None, ALU.mult, None,
                   [AKEY('wT'), 'ident_b'], [AKEY('dg')])
        memset('pool', hglu[:, :, 0:32], 0.0, [AKEY('hglu')])
        dma(sp_q, wg[0:16, :], w_gla_gate[l], (), [AKEY('wg')], AKEY('wg'))
        dma(sp_q, wg[16:17, :], b_gla_gate[l].rearrange("(o n) -> o n", o=1), (), [AKEY('wg')], AKEY('wg'))
        dma(sp_q, gnb, gla_norm[l].partition_broadcast(128), (), [AKEY('gnb')], AKEY('gnb'))
        ts('dve', gnb, gnb, float(np.sqrt(96.0)), None, ALU.mult, None, [AKEY('gnb')], [AKEY('gnb')])
        memset('pool', lrT, 1.0, [AKEY('lrT')])
        memset('pool', Sst, 0.0, [AKEY('S%d' % h) for h in range(4)])
        memset('pool', vr[:, :, :, 64:65], 1.0, [AKEY('vones')])
        dma(sp_q, ext[:, 0:256], rel_bias[l][:, 1:257], (), ['ext'], 'ext')
        dma(sp_q, btmp[0:6, 512:513], rel_bias[l][:, 256:257], (), [AKEY('btmp')], AKEY('btmp'),
            allow_slow_non_contiguous=True)
        cp('dve', btmp[0:6, 0:512], btmp[0:6, 512:513].to_broadcast([6, 512]), [AKEY('btmp')], [AKEY('btmp')])
        dma(sp_q, ext[:, 256:768], btmp[0:6, 0:512], [AKEY('btmp')], ['ext'], 'ext')
        for h in range(6):
            src = bass.AP(tensor=ext.tensor, offset=ext.offset + h * 768, ap=[[1, 128], [1, 640]])
            dma(sp_q, btmp, src, ['ext'], [AKEY('btmp')], AKEY('btmp'))
            pe_mms([(psb(4)[:, 0:512], J_f, btmp[:, 0:512], True, True),
                    (psb(5)[:, 0:128], J_f, btmp[:, 512:640], True, True)], [AKEY('btmp'), 'cst'],
                   [('ps', 4), ('ps', 5)])
            cp('dve', biasb[:, h, 0:512], psb(4)[:, 0:512], [('ps', 4)], [AKEY('bias')])
            cp('dve', biasb[:, h, 512:640], psb(5)[:, 0:128], [('ps', 5)], [AKEY('bias')])
        memset('pool', biasb[64:128, :, 0:64], NEG, [AKEY('bias')])
        memset('pool', biasb[0:64, :, 576:640], NEG, [AKEY('bias')])

        def load_h(i, buf):
            if l == 0:
                return None
            dma(sp_q, hT[buf].rearrange("p k t -> p (k t)"), hsrc[i], [('hs', id(hsrc), i)], [('hT', buf)],
                ('hT', buf))

        def load_x(i):
            dma(sp_q, otile, x[i * T:(i + 1) * T, :].rearrange("(s p) d -> p s d", p=128), (), ['otile'], 'otile')

        if l > 0:
            load_h(0, 0)
        else:
            load_x(0)
        for i in range(ntiles):
            buf = i % 2
            if l == 0:
                for s in range(2):
                    for kk in range(0, 8, 4):
                        b = 4 + (kk // 4)
                        pe_tr([(psb(b)[:, j * 128:(j + 1) * 128], otile[:, s, (kk + j) * 128:(kk + j + 1) * 128], ident_f)
                               for j in range(4)], ['otile', 'cst'], [('ps', b)])
                        cp('act' if kk == 0 else 'dve', hT[buf][:, kk:kk + 4, s * 128:(s + 1) * 128],
                           psb(b).rearrange("p (j t) -> p j t", j=4), [('ps', b)], [('hT', buf)])
                if i + 1 < ntiles:
                    load_x(i + 1)
            elif i + 1 < ntiles:
                load_h(i + 1, 1 - buf)
            rms_stats(buf, 2 * l)
            XN = ('xnT', buf)
            xn = xnT[buf]
            pe_mms([(psb(0)[0:16, 0:T], win[:, k, LR:LR + 16], xn[:, k, :], k == 0, k == 7) for k in range(8)],
                   [XN] + WK[hw], [('ps', 0)])
            cp('dve', lrT[0:16, :], psb(0)[0:16, 0:T], [('ps', 0)], [AKEY('lrT')])
            for h in range(4):
                b = 1 + (h % 2)
                pe_mms([(psb(b)[0:48, 0:T], win[:, k, GQ + 48 * h:GQ + 48 * h + 48], xn[:, k, :], k == 0, k == 7)
                        for k in range(8)], [XN] + WK[hw], [('ps', b)])
                act(gqT[h], psb(b)[0:48, 0:T], AF.Copy, [('ps', b)], [('gqT', h)], scale=float(48 ** -0.5))
            for blk in range(2):
                pe_mms([(psb(0)[:, 0:T], win[:, k, CA + 128 * blk:CA + 128 * blk + 128], xn[:, k, :], k == 0, k == 7)
                        for k in range(8)], [XN] + WK[hw], [('ps', 0)])
                pe_mms([(psb(3)[:, 0:T], win[:, k, CB + 128 * blk:CB + 128 * blk + 128], xn[:, k, :], k == 0, k == 7)
                        for k in range(8)], [XN] + WK[hw], [('ps', 3)])
                act(sig[blk], psb(3)[:, 0:T], AF.Sigmoid, [('ps', 3)], [AKEY('sig%d' % blk)])
                tt('dve', hglu[:, blk, 32:32 + T], psb(0)[:, 0:T], sig[blk], ALU.mult,
                   [('ps', 0), AKEY('sig%d' % blk)], [AKEY('hglu')])
            slot0 = (2 * i) % 6
            for pr in range(3):
                b = 1 + (pr % 2)
                pe_mms([(psb(b)[:, 0:T], win[:, k, AQ + 128 * pr:AQ + 128 * pr + 128], xn[:, k, :], k == 0, k == 7)
                        for k in range(8)], [XN] + WK[hw], [('ps', b)])
                act(aqT[:, pr, :], psb(b)[:, 0:T], AF.Copy, [('ps', b)], ['aqT'], scale=0.125)
            for pr in range(3):
                b = 0 if pr % 2 == 0 else 3
                pe_mms([(psb(b)[:, 0:T], win[:, k, AK + 128 * pr:AK + 128 * pr + 128], xn[:, k, :], k == 0, k == 7)
                        for k in range(8)], [XN] + WK[hw], [('ps', b)])
                cp('dve', kT[:, pr, slot0 * 128:slot0 * 128 + T], psb(b)[:, 0:T], [('ps', b)],
                   [AKEY('kT%d' % slot0), AKEY('kT%d' % (slot0 + 1))])
            for s in range(2):
                g = 2 * i + s
                slot = g % 6
                xs_ = lambda k: xn[:, k, s * 128:(s + 1) * 128]
                pe_mms([(psb(1)[:, 0:192], xs_(k), win[:, k, GK:GK + 192], k == 0, k == 7) for k in range(8)],
                       [XN] + WK[hw], [('ps', 1)])
                cp('dve', gla_k[:, s, :], psb(1)[:, 0:192], [('ps', 1)], [('gla_k', s)])
                pe_mms([(psb(2)[:, 0:384], xs_(k), win[:, k, GV:GV + 384], k == 0, k == 7) for k in range(8)],
                       [XN] + WK[hw], [('ps', 2)])
                cp('act', gla_v[:, s, :], psb(2)[:, 0:384], [('ps', 2)], [('gla_v', s)])
                pe_mms([(psb(0)[:, 0:384], xs_(k), win[:, k, GG:GG + 384], k == 0, k == 7) for k in range(8)],
                       [XN] + WK[hw], [('ps', 0)])
                act(gla_g[:, s, :], psb(0)[:, 0:384], AF.Silu, [('ps', 0)], [('gla_g', s)])
                pe_mms([(psb(3)[:, 0:384], xs_(k), win[:, k, AV:AV + 384], k == 0, k == 7) for k in range(8)],
                       [XN] + WK[hw], [('ps', 3)])
                cp('dve', vr[:, slot, :, 0:64], psb(3)[:, 0:384].rearrange("p (h d) -> p h d", h=6), [('ps', 3)],
                   [AKEY('v%d' % slot)])
            for s in range(2):
                pe_mms([(psb(1)[:, 0:192], lrT[0:17, s * 128:(s + 1) * 128], wg[0:17, :], True, True)],
                       [AKEY('lrT'), AKEY('wg')], [('ps', 1)])
                act(g_e, psb(1)[:, 0:192], AF.Exp, [('ps', 1)], [AKEY('g_e')], scale=-1.0)
                act(g_sp, g_e, AF.Ln, [AKEY('g_e')], [AKEY('g_sp')], bias=1.0)
                pe_mms([(psb(1)[:, 0:192], tri_f, g_sp, True, True)], [AKEY('g_sp'), 'cst'], [('ps', 1)])
                pe_mms([(psb(2)[0:48, 2 * h:2 * h + 2], g_sp[:, 48 * h:48 * h + 48], ind_f, True, True)
                        for h in range(4)], [AKEY('g_sp'), 'cst'], [('ps', 2)])
                act(g_ed, psb(1)[:, 0:192], AF.Exp, [('ps', 1)], [AKEY('g_ed')])
                act(dec, psb(2)[0:48, 0:8].rearrange("p (h c) -> p h c", h=4), AF.Exp, [('ps', 2)], [AKEY('dec')])
                tt('dve', kdec, gla_k[:, s, :], g_ed, ALU.mult, [('gla_k', s), AKEY('g_ed')], [AKEY('kdec')])
                for c in range(2):
                    sb = c
                    pe_mms([(psb(2)[0:48, 96 * h:96 * h + 96], kdec[c * 64:(c + 1) * 64, 48 * h:48 * h + 48],
                             gla_v[c * 64:(c + 1) * 64, s, 96 * h:96 * h + 96], True, True) for h in range(4)],
                           [AKEY('kdec'), ('gla_v', s)], [('ps', 2)])
                    for h in range(4):
                        stt(Sst[:, h, :], Sst[:, h, :], dec[:, h, c:c + 1], psb(2)[0:48, 96 * h:96 * h + 96],
                            ALU.mult, ALU.add, [AKEY('S%d' % h), AKEY('dec'), ('ps', 2)], [AKEY('S%d' % h)])
                    cp('act', Sbf[sb], Sst, [AKEY('S%d' % h) for h in range(4)], [AKEY('Sbf%d' % sb)])
                    pe_mms([(psb(3)[c * 64:(c + 1) * 64, 96 * h:96 * h + 96],
                             gqT[h][:, s * 128 + c * 64:s * 128 + c * 64 + 64], Sbf[sb][:, h, :], True, True)
                            for h in range(4)], [('gqT', 0), ('gqT', 1), ('gqT', 2), ('gqT', 3), AKEY('Sbf%d' % sb)],
                           [('ps', 3)])
                cp('act', go, psb(3)[:, 0:384], [('ps', 3)], [AKEY('go')])
                tt('pool', gsq, go, go, ALU.mult, [AKEY('go')], [AKEY('gsq')])

                def red(e, gms=gms, gsq=gsq):
                    return e.tensor_reduce(out=gms, in_=gsq.rearrange("p (h v) -> p h v", h=4), axis=AX.X, op=ALU.add)
                S.add('dve', red, [AKEY('gsq')], [AKEY('gms')])
                act(gms, gms, AF.Ln, [AKEY('gms')], [AKEY('gms')], bias=float(96 * EPS))
                act(gms, gms, AF.Exp, [AKEY('gms')], [AKEY('gms')], scale=-0.5)
                tt('dve', go, go, gnb, ALU.mult, [AKEY('go'), AKEY('gnb')], [AKEY('go')])
                tt('dve', go, go, gla_g[:, s, :], ALU.mult, [AKEY('go'), ('gla_g', s)], [AKEY('go')])
                tt('dve', gon.rearrange("p (h v) -> p h v", h=4), go.rearrange("p (h v) -> p h v", h=4),
                   gms.unsqueeze(2).to_broadcast([128, 4, 96]), ALU.mult, [AKEY('go'), AKEY('gms')], [AKEY('gon')])
                pe_tr([(psbf(1)[:, j * 128:(j + 1) * 128], gon[:, j * 128:(j + 1) * 128], ident_b) for j in range(3)],
                      [AKEY('gon'), 'ident_b'], [('ps', 1)])
                cp('dve', mixsq[:, 0:3, s * 128:(s + 1) * 128], psbf(1)[:, 0:384].rearrange("p (j t) -> p j t", j=3),
                   [('ps', 1)], ['mixsq'])
            for blk in range(2):
                b = 4 + blk
                pe_mms([(psb(b)[:, 0:T], dg[:, blk, j, :], hglu[:, blk, 2 + j:2 + j + T], j == 0, j == 30)
                        for j in range(31)], [AKEY('dg'), AKEY('hglu')], [('ps', b)])
                act(cy[:, blk, :], psb(b)[:, 0:T], AF.Identity, [('ps', b), AKEY('cpar')], [AKEY('cy')],
                    bias=cpar[:, 0, blk:blk + 1])
                act(cysq[:, blk, :], psb(b)[:, 0:T], AF.Square, [('ps', b), AKEY('cpar')], [AKEY('cysq')],
                    bias=cpar[:, 0, blk:blk + 1])
            cp('pool', hglu[:, :, 0:32], hglu[:, :, T:T + 32], [AKEY('hglu')], [AKEY('hglu')])
            pe_mms([(psb(4)[:, 0:T], ones_f, cy[:, 0, :], True, False), (psb(4)[:, 0:T], ones_f, cy[:, 1, :], False, True),
                    (psb(4)[:, T:2 * T], ones_f, cysq[:, 0, :], True, False),
                    (psb(4)[:, T:2 * T], ones_f, cysq[:, 1, :], False, True)],
                   [AKEY('cy'), AKEY('cysq'), 'cst'], [('ps', 4)])
            ts('dve', cm, psb(4)[:, 0:T], 1.0 / 256.0, None, ALU.mult, None, [('ps', 4)], [AKEY('cm')])
            tt('dve', cmsq, cm, cm, ALU.mult, [AKEY('cm')], [AKEY('cmsq')])
            stt(cvar, psb(4)[:, T:2 * T], 1.0 / 256.0, cmsq, ALU.mult, ALU.subtract, [('ps', 4), AKEY('cmsq')],
                [AKEY('cvar')])
            act(crs, cvar, AF.Ln, [AKEY('cvar')], [AKEY('crs')], bias=float(EPS))
            act(crs, crs, AF.Exp, [AKEY('crs')], [AKEY('crs')], scale=-0.5)
            for blk in range(2):
                tt('pool', cd[:, blk, :], cy[:, blk, :], cm, ALU.subtract, [AKEY('cy'), AKEY('cm')], [AKEY('cd')])
                tt('pool', cd[:, blk, :], cd[:, blk, :], crs, ALU.mult, [AKEY('cd'), AKEY('crs')], [AKEY('cd')])
                act(mixsq[:, 3 + blk, :], cd[:, blk, :], AF.Silu, [AKEY('cd'), AKEY('cpar')], ['mixsq'],
                    scale=cpar[:, 1, blk:blk + 1], bias=cpar[:, 2, blk:blk + 1])
            for s in range(2):
                g = 2 * i + s
                nj = min(5, g + 1)
                for h in range(6):
                    pr, p0 = h // 2, 64 * (h % 2)
                    sb = h % 2
                    bA, bB = (4, 5) if sb == 0 else (6, 7)
                    lst = [(psb(bA)[:, 0:512], ident_b, biasb[:, h, 0:512], True, False)]
                    for j in range(min(nj, 4)):
                        sl = (g - j) % 6
                        lst.append((psb(bA)[:, j * 128:(j + 1) * 128], kT[p0:p0 + 64, pr, sl * 128:(sl + 1) * 128],
                                    aqT[p0:p0 + 64, pr, s * 128:(s + 1) * 128], False, j == min(nj, 4) - 1))
                    rd = ['ident_b', AKEY('bias'), 'aqT'] + [AKEY('kT%d' % ((g - j) % 6)) for j in range(nj)]
                    wr = [('ps', bA)]
                    if nj == 5:
                        sl = (g - 4) % 6
                        lst.append((psb(bB)[:, 0:128], ident_b, biasb[:, h, 512:640], True, False))
                        lst.append((psb(bB)[:, 0:128], kT[p0:p0 + 64, pr, sl * 128:(sl + 1) * 128],
                                    aqT[p0:p0 + 64, pr, s * 128:(s + 1) * 128], False, True))
                        wr.append(('ps', bB))
                    pe_mms(lst, rd, wr)
                    na = min(nj, 4) * 128
                    act(pT[sb][:, 0:na], psb(bA)[:, 0:na], AF.Exp, [('ps', bA)], [AKEY('pT%d' % sb)])
                    if nj == 5:
                        act(pT[sb][:, 512:640], psb(bB)[:, 0:128], AF.Exp, [('ps', bB)], [AKEY('pT%d' % sb)])
                    pe_mms([(psb(3)[:, 65 * h:65 * h + 65], pT[sb][:, j * 128:(j + 1) * 128], vr[:, (g - j) % 6, h, :],
                             j == 0, j == nj - 1) for j in range(nj)],
                           [AKEY('pT%d' % sb), AKEY('vones')] + [AKEY('v%d' % ((g - j) % 6)) for j in range(nj)],
                           [('ps', 3)])
                o3 = psb(3)[:, 0:390].rearrange("p (h d) -> p h d", h=6)

                def rcp(e, rden=rden, o3=o3):
                    return e.reciprocal(out=rden.unsqueeze(2), in_=o3[:, :, 64:65])
                S.add('dve', rcp, [('ps', 3)], [AKEY('rden')])
                tt('dve', aob.rearrange("p (h d) -> p h d", h=6), o3[:, :, 0:64],
                   rden.unsqueeze(2).to_broadcast([128, 6, 64]), ALU.mult, [('ps', 3), AKEY('rden')], [AKEY('aob')])
                pe_tr([(psbf(2)[:, j * 128:(j + 1) * 128], aob[:, j * 128:(j + 1) * 128], ident_b) for j in range(3)],
                      [AKEY('aob'), 'ident_b'], [('ps', 2)])
                cp('dve', mixsq[:, 5:8, s * 128:(s + 1) * 128], psbf(2)[:, 0:384].rearrange("p (j t) -> p j t", j=3),
                   [('ps', 2)], ['mixsq'])
            for m in range(8):
                b = m % 2
                pe_mms([(psb(b)[:, 0:T], wout[:, k, m * 128:(m + 1) * 128], mixsq[:, k, :], k == 0, k == 7)
                        for k in range(8)], ['mixsq'] + WK[hw], [('ps', b)])
                tt('dve', hT[buf][:, m, :], hT[buf][:, m, :], psb(b)[:, 0:T], ALU.add, [('hT', buf), ('ps', b)],
                   [('hT', buf)])
            dma(sp_q, hdst[i], hT[buf].rearrange("p k t -> p (k t)"), [('hT', buf)], [('hs', id(hdst), i)],
                ('hTst', buf))
        return a_keys

    def phase_B(l, half, hw, hsrc, hdst, final=False):
        wup, wdn = state['wB']
        if half == 0:
            dma(sp_q, hT[0].rearrange("p k t -> p (k t)"), hsrc[0], [('hs', id(hsrc), 0)], [('hT', 0)], ('hT', 0))
        else:
            dma(sp_q, hT[0].rearrange("p k t -> p (k t)"), hsrc[0], [('hs', id(hsrc), 0)], [('hT', 0)], ('hT', 0))
            dma(sp_q, xnT[0].rearrange("p k t -> p (k t)"), xs[0], [('xs', 0)], [('xnT', 0)], ('xnT', 0))
        for i in range(ntiles):
            buf = i % 2
            if i + 1 < ntiles:
                nb = 1 - buf
                dma(sp_q, hT[nb].rearrange("p k t -> p (k t)"), hsrc[i + 1], [('hs', id(hsrc), i + 1)], [('hT', nb)],
                    ('hT', nb))
                if half == 1:
                    dma(sp_q, xnT[nb].rearrange("p k t -> p (k t)"), xs[i + 1], [('xs', i + 1)], [('xnT', nb)],
                        ('xnT', nb))
            if half == 0:
                rms_stats(buf, 2 * l + 1)
                dma(sp_q, xs[i], xnT[buf].rearrange("p k t -> p (k t)"), [('xnT', buf)], [('xs', i)], ('xnTst', buf))
            XN = ('xnT', buf)
            xn = xnT[buf]
            for j in range(16):
                b = j % 4
                pe_mms([(psb(b)[:, 0:T], wup[:, k, j * 128:(j + 1) * 128], xn[:, k, :], k == 0, k == 7)
                        for k in range(8)], [XN] + WK[hw], [('ps', b)])
                r = relu_t[j % 2]
                act(r, psb(b)[:, 0:T], AF.Relu, [('ps', b)], [('relu', j % 2)])
                tt('pool' if j % 2 == 0 else 'dve', hidT[:, j, :], r, r, ALU.mult, [('relu', j % 2)], [('hid', j)])
            for m in range(8):
                b = 4 + (m % 4)
                pe_mms([(psb(b)[:, 0:T], wdn[:, j, m * 128:(m + 1) * 128], hidT[:, j, :], j == 0, j == 15)
                        for j in range(16)], [('hid', j) for j in range(16)] + WK[hw], [('ps', b)])
                tt('dve', hT[buf][:, m, :], hT[buf][:, m, :], psb(b)[:, 0:T], ALU.add, [('hT', buf), ('ps', b)],
                   [('hT', buf)])
            if not final:
                dma(sp_q, hdst[i], hT[buf].rearrange("p k t -> p (k t)"), [('hT', buf)], [('hs', id(hdst), i)],
                    ('hTst', buf))
            else:
                rms_stats(buf, 4, dst_is_f32_inplace=True)
                for s in range(2):
                    for kk in range(0, 8, 4):
                        b = 0 + (kk // 4)
                        pe_tr([(psb(b)[:, j * 128:(j + 1) * 128], hT[buf][:, kk + j, s * 128:(s + 1) * 128], ident_f)
                               for j in range(4)], [('hT', buf), 'cst'], [('ps', b)])
                        cp('act' if kk == 0 else 'dve', otile[:, s, kk * 128:(kk + 4) * 128], psb(b), [('ps', b)],
                           ['otile'])
                dma(sp_q, out[i * T:(i + 1) * T, :].rearrange("(s p) d -> p s d", p=128), otile, ['otile'], ['out'],
                    'otile_st')

    def dump_dbg(hsbuf):
        dbg = nc.dram_tensor("dbg", [NT, 128, 8 * T], F32, kind="ExternalOutput").ap()
        for i in range(ntiles):
            dma(sp_q, hT[0].rearrange("p k t -> p (k t)"), hsbuf[i], [('hs', id(hsbuf), i)], [('hT', 0)], ('hT', 0))
            dma(sp_q, dbg[i], hT[0].rearrange("p k t -> p (k t)"), [('hT', 0)], ['dbg'], 'dbg')

    cur = 0
    wA = load_w_A(0, 0)
    for l in range(DEPTH):
        hw = l % 2
        state['wA'] = wA
        if l == 0:
            akeys = phase_A(l, hw, None, hs[0])
            cur = 0
        else:
            akeys = phase_A(l, hw, hs[cur], hs[1 - cur])
            cur = 1 - cur
        if stop_after == (l, 'A'):
            dump_dbg(hs[cur])
            break
        S.fence(akeys, WK[1 - hw])
        wB1 = load_w_B(l, 0, 1 - hw)
        wB2 = load_w_B(l, 1, hw)
        state['wB'] = wB1
        phase_B(l, 0, 1 - hw, hs[cur], hs[1 - cur])
        cur = 1 - cur
        if stop_after == (l, 'B1'):
            dump_dbg(hs[cur])
            break
        if l + 1 < DEPTH:
            wA = load_w_A(l + 1, 1 - hw)
        state['wB'] = wB2
        fin = (l == DEPTH - 1 and stop_after is None)
        phase_B(l, 1, hw, hs[cur], hs[1 - cur], final=fin)
        cur = 1 - cur
        if stop_after == (l, 'B2'):
            dump_dbg(hs[cur])
            break

    S.emit(nc, es)
    es.close()
    return nc


_NC_CACHE = {}


def kernel(**inputs):
    key = 'full'
    if key not in _NC_CACHE:
        _NC_CACHE[key] = build()
    nc = _NC_CACHE[key]
    cstv = make_consts()
    names = ["norm_mix", "w_in", "w_gla_gate", "b_gla_gate", "gla_norm", "w_dw", "b_dw", "conv_ln_g", "conv_ln_b",
             "rel_bias", "w_out", "norm_ffn", "w_up", "w_down", "norm_final"]
    shared = {n: np.ascontiguousarray(np.asarray(inputs[n], dtype=np.float32)) for n in names}
    xfull = np.asarray(inputs["x"], dtype=np.float32)
    in_maps = []
    for c in range(8):
        m = dict(shared)
        m["x"] = np.ascontiguousarray(xfull[c])
        m["cst"] = cstv
        in_maps.append(m)
    res = run_bass_kernel_spmd(nc, in_maps, core_ids=list(range(8)))
    return np.stack([np.asarray(r["out"], dtype=np.float32) for r in res.results], axis=0)
```

```python
import numpy as np
from contextlib import ExitStack
import concourse.bass as bass
import concourse.mybir as mybir
from concourse.bass_utils import run_bass_kernel_spmd

F32 = mybir.dt.float32
BF16 = mybir.dt.bfloat16
AF = mybir.ActivationFunctionType
ALU = mybir.AluOpType
AX = mybir.AxisListType

D = 1024
SEQ = 4096
DEPTH = 2
T = 256
NT = SEQ // T
DIN = 2832
DFF = 4096
EPS = 1e-6
GQ, GK, GV, GG, LR, CA, CB, AQ, AK, AV = 0, 192, 384, 768, 1152, 1168, 1424, 1680, 2064, 2448
NEG = -30000.0

C_ID, C_TRI, C_ONES, C_IND, C_J = 0, 128, 256, 384, 386
NCST = 514


def make_consts():
    c = np.zeros((128, NCST), np.float32)
    c[:, C_ID:C_ID + 128] = np.eye(128, dtype=np.float32)
    s = np.arange(128)[:, None]
    t = np.arange(128)[None, :]
    c[:, C_TRI:C_TRI + 128] = np.where((s > t) & (s // 64 == t // 64), -1.0 / 16.0, 0.0)
    c[:, C_ONES:C_ONES + 128] = 1.0
    c[:, C_IND:C_IND + 2] = np.where(s // 64 == np.arange(2)[None, :], -1.0 / 16.0, 0.0)
    c[:, C_J:C_J + 128] = np.eye(128, dtype=np.float32)[::-1]
    return c


class Sched:
    def __init__(self):
        self.ops = []
        self.lw = {}
        self.rd = {}
        self.rec = None

    def record(self):
        assert self.rec is None
        self.rec = []

    def stop(self):
        r = self.rec
        self.rec = None
        return r

    def replay(self, items):
        assert self.rec is None
        for it in items:
            self.add(*it)

    def add(self, eng, fn, reads=(), writes=(), dma=None):
        if self.rec is not None:
            self.rec.append((eng, fn, tuple(reads), tuple(writes), dma))
            return -1
        raw = set()
        for k in reads:
            raw |= self.lw.get(k, set())
        oth = set()
        for k in writes:
            oth |= self.lw.get(k, set())
            oth |= set(self.rd.get(k, {}).values())
        i = len(self.ops)
        self.ops.append(dict(eng=eng, fn=fn, raw=raw, oth=oth - raw, dma=dma, need=False))
        sig = ('d', dma) if dma is not None else ('e', eng)
        for k in reads:
            self.rd.setdefault(k, {})[sig] = i
        for k in writes:
            self.lw[k] = {i}
            self.rd[k] = {}
        return i

    def fence(self, src, dst):
        acc = set()
        for k in src:
            acc |= self.lw.get(k, set()) | set(self.rd.get(k, {}).values())
        for k in dst:
            self.lw[k] = self.lw.get(k, set()) | acc

    def emit(self, nc, es):
        ops = self.ops
        CAP = 30000
        for o in ops:
            deps = set()
            for j in o['raw']:
                pj = ops[j]
                if pj['dma'] is not None or pj['eng'] != o['eng'] or o['eng'] != 'pe':
                    deps.add(j)
            for j in o['oth']:
                pj = ops[j]
                if pj['dma'] is not None or pj['eng'] != o['eng'] or o['dma'] is not None:
                    deps.add(j)
            o['deps'] = deps
            for j in deps:
                ops[j]['need'] = True
        cnt = {}
        dcnt = {}
        engsems = {}
        dsems = {}

        def get_esem(e, idx):
            key = (e, idx)
            if key not in engsems:
                engsems[key] = es.enter_context(nc.semaphore("s_%s_%d" % (e, idx)))
            return engsems[key]

        for o in ops:
            if o['dma'] is not None:
                k = o['dma']
                if k not in dsems:
                    dsems[k] = es.enter_context(nc.semaphore("d_%d" % len(dsems)))
                dcnt[k] = dcnt.get(k, 0) + 16
                o['sem'] = dsems[k]
                o['val'] = dcnt[k]
                o['need'] = True
            elif o['need']:
                e = o['eng']
                c = cnt.get(e, 0)
                o['sem'] = get_esem(e, c // CAP)
                o['val'] = c % CAP + 1
                cnt[e] = c + 1
        self.nsem = len(engsems) + len(dsems)
        per = {e: [] for e in ('pe', 'act', 'dve', 'pool', 'sp')}
        for o in ops:
            per[o['eng']].append(o)

        def run(eng_obj, lst):
            waited = {}
            for o in lst:
                for j in sorted(o['deps']):
                    pj = ops[j]
                    sid = id(pj['sem'])
                    if waited.get(sid, 0) >= pj['val']:
                        continue
                    waited[sid] = pj['val']
                    eng_obj.wait_ge(pj['sem'], pj['val'])
                inst = o['fn'](eng_obj)
                if o['dma'] is not None:
                    inst.then_inc(o['sem'], 16)
                elif o['need']:
                    inst.then_inc(o['sem'], 1)
            last = {}
            for o in lst:
                if o['dma'] is not None:
                    last[id(o['sem'])] = (o['sem'], o['val'])
            for sid, (sm, v) in last.items():
                if waited.get(sid, 0) < v:
                    eng_obj.wait_ge(sm, v)

        block = es.enter_context(nc.Block())

        @block.sync
        def _(e):
            run(e, per['sp'])

        @block.gpsimd
        def _(e):
            run(e, per['pool'])

        @block.scalar
        def _(e):
            run(e, per['act'])

        @block.vector
        def _(e):
            run(e, per['dve'])

        @block.tensor
        def _(e):
            run(e, per['pe'])


def build(ntiles=NT, stop_after=None):
    nc = bass.Bass("TRN2", target_bir_lowering=False)
    es = ExitStack()
    dr = {}

    def din(name, shape):
        dr[name] = nc.dram_tensor(name, list(shape), F32, kind="ExternalInput").ap()
        return dr[name]

    x = din("x", [SEQ, D])
    norm_mix = din("norm_mix", [DEPTH, D])
    w_in = din("w_in", [DEPTH, D, DIN])
    w_gla_gate = din("w_gla_gate", [DEPTH, 16, 192])
    b_gla_gate = din("b_gla_gate", [DEPTH, 192])
    gla_norm = din("gla_norm", [DEPTH, 384])
    w_dw = din("w_dw", [DEPTH, 31, 256])
    b_dw = din("b_dw", [DEPTH, 256])
    conv_ln_g = din("conv_ln_g", [DEPTH, 256])
    conv_ln_b = din("conv_ln_b", [DEPTH, 256])
    rel_bias = din("rel_bias", [DEPTH, 6, 257])
    w_out = din("w_out", [DEPTH, D, D])
    norm_ffn = din("norm_ffn", [DEPTH, D])
    w_up = din("w_up", [DEPTH, D, DFF])
    w_down = din("w_down", [DEPTH, DFF, D])
    norm_final = din("norm_final", [D])
    cst = din("cst", [128, NCST])
    out = nc.dram_tensor("out", [SEQ, D], F32, kind="ExternalOutput").ap()
    hs = [nc.dram_tensor("hs%d" % i, [NT, 128, 8 * T], F32, kind="Internal").ap() for i in range(2)]
    xs = nc.dram_tensor("xs", [NT, 128, 8 * T], BF16, kind="Internal").ap()
    ext = nc.dram_tensor("ext", [6, 768], F32, kind="Internal").ap()

    PBYTES = 78 * 1024
    HB = 64 * 1024
    TOT = PBYTES + 2 * HB
    arena_t = es.enter_context(nc.sbuf_tensor("arena", [128, TOT // 4], F32))
    ps = [es.enter_context(nc.psum_tensor("ps%d" % b, [128, 512], F32)) for b in range(8)]

    def view(off, dtype, shape, parts=128, p0=0):
        n = int(np.prod(shape))
        esz = 4 if dtype == F32 else 2
        nb = n * esz
        assert off % 4 == 0
        a = arena_t[p0:p0 + parts, off // 4:(off + nb + 3) // 4]
        if dtype != F32:
            a = a.bitcast(dtype)
        a = a[:, 0:n]
        if len(shape) == 2:
            a = a.rearrange("p (a b) -> p a b", a=shape[0])
        elif len(shape) == 3:
            a = a.rearrange("p (a b c) -> p a b c", a=shape[0], b=shape[1])
        return a

    class Carver:
        def __init__(self, base, limit):
            self.off = base
            self.limit = limit

        def get(self, dtype, shape, parts=128, p0=0):
            n = int(np.prod(shape))
            nb = n * (4 if dtype == F32 else 2)
            nb = (nb + 31) // 32 * 32
            o = self.off
            self.off += nb
            assert self.off <= self.limit, ("arena overflow", self.off, self.limit)
            return view(o, dtype, shape, parts, p0)

    P = Carver(0, PBYTES)
    cst_sb = P.get(F32, [NCST])
    ident_f = cst_sb[:, C_ID:C_ID + 128]
    tri_f = cst_sb[:, C_TRI:C_TRI + 128]
    ones_f = cst_sb[:, C_ONES:C_ONES + 128]
    ind_f = cst_sb[:, C_IND:C_IND + 2]
    J_f = cst_sb[:, C_J:C_J + 128]
    ident_b = P.get(BF16, [128])
    ones_b = P.get(BF16, [128])
    gvec = P.get(F32, [5, 8])
    hT = [P.get(F32, [8, T]) for _ in range(2)]
    xnT = [P.get(BF16, [8, T]) for _ in range(2)]
    hsq = P.get(BF16, [8, T])
    mixT = P.get(BF16, [8, T])
    rstd_b = P.get(F32, [T])
    otile = P.get(F32, [2, D])
    ab_base = P.off
    PA = Carver(ab_base, PBYTES)
    gla_k = [PA.get(F32, [2, 192]) for _ in range(2)]
    gla_v = [PA.get(BF16, [2, 384]) for _ in range(2)]
    gla_g = [PA.get(F32, [2, 384]) for _ in range(2)]
    gqT = [[PA.get(BF16, [T], parts=48) for _ in range(4)] for _ in range(2)]
    aqT = [PA.get(BF16, [3, T]) for _ in range(2)]
    lrT = [PA.get(F32, [T], parts=32) for _ in range(2)]
    hglu = [PA.get(BF16, [2, 32 + T]) for _ in range(2)]
    btmp = PA.get(F32, [640])
    gnb = PA.get(F32, [384])
    go = PA.get(F32, [384])
    PA_KEYS = ([('gla_k', p, s_) for p in range(2) for s_ in range(2)] + [('gla_v', p, s_) for p in range(2) for s_ in range(2)]
               + [('gla_g', p, s_) for p in range(2) for s_ in range(2)] + [('gqT', p, h) for p in range(2) for h in range(4)]
               + [('aqT', p) for p in range(2)] + [('lrT', p) for p in range(2)] + [('hglu', p) for p in range(2)]
               + ['btmp', 'gnb', 'go'])
    PB = Carver(ab_base, PBYTES)
    hidT = [PB.get(BF16, [16, T]) for _ in range(2)]
    relu_t = [PB.get(F32, [T]) for _ in range(2)]
    PB_KEYS = [('hid', p, j) for p in range(2) for j in range(16)] + [('relu', n) for n in range(2)]

    S = Sched()
    sp_q = 'sp'

    def dma(q, out_ap, in_ap, reads, writes, key, **kw):
        def fn(e, out_ap=out_ap, in_ap=in_ap, kw=kw):
            return e.dma_start(out=out_ap, in_=in_ap, **kw)
        return S.add(q, fn, reads, writes, dma=key)

    def pe_mms(lst, reads, writes):
        def fn(e, lst=lst):
            ins = None
            for (o, l, r, st, sp) in lst:
                ins = e.matmul(o, lhsT=l, rhs=r, start=st, stop=sp)
            return ins
        return S.add('pe', fn, reads, writes)

    def pe_tr(lst, reads, writes):
        def fn(e, lst=lst):
            ins = None
            for (o, i, idn) in lst:
                ins = e.transpose(o, i, idn)
            return ins
        return S.add('pe', fn, reads, writes)

    def act(out_ap, in_ap, func, reads, writes, **kw):
        def fn(e, out_ap=out_ap, in_ap=in_ap, func=func, kw=kw):
            return e.activation(out=out_ap, in_=in_ap, func=func, **kw)
        return S.add('act', fn, reads, writes)

    def ts(eng, out_ap, in0, s1, s2, op0, op1, reads, writes):
        def fn(e, out_ap=out_ap, in0=in0, s1=s1, s2=s2, op0=op0, op1=op1):
            if op1 is None:
                return e.tensor_scalar(out=out_ap, in0=in0, scalar1=s1, scalar2=None, op0=op0)
            return e.tensor_scalar(out=out_ap, in0=in0, scalar1=s1, scalar2=s2, op0=op0, op1=op1)
        return S.add(eng, fn, reads, writes)

    def tt(eng, out_ap, in0, in1, op, reads, writes):
        def fn(e, out_ap=out_ap, in0=in0, in1=in1, op=op):
            return e.tensor_tensor(out=out_ap, in0=in0, in1=in1, op=op)
        return S.add(eng, fn, reads, writes)

    def stt(out_ap, in0, sc, in1, op0, op1, reads, writes):
        def fn(e, out_ap=out_ap, in0=in0, sc=sc, in1=in1, op0=op0, op1=op1):
            return e.scalar_tensor_tensor(out=out_ap, in0=in0, scalar=sc, in1=in1, op0=op0, op1=op1)
        return S.add('dve', fn, reads, writes)

    def cp(eng, out_ap, in_ap, reads, writes):
        if eng == 'act':
            return act(out_ap, in_ap, AF.Copy, reads, writes)
        def fn(e, out_ap=out_ap, in_ap=in_ap):
            return e.tensor_copy(out=out_ap, in_=in_ap)
        return S.add(eng, fn, reads, writes)

    def memset(eng, ap, val, writes):
        def fn(e, ap=ap, val=val):
            return e.memset(ap, val)
        return S.add(eng, fn, (), writes)

    def psb(b):
        return ps[b][:]

    def psbf(b):
        return ps[b][:].bitcast(BF16)

    dma(sp_q, cst_sb, cst[:, :], (), ['cst'], 'cst')
    cp('dve', ident_b, ident_f, ['cst'], ['ident_b'])
    cp('dve', ones_b, ones_f, ['cst'], ['ones_b'])
    gsrc = [norm_mix[0], norm_ffn[0], norm_mix[1], norm_ffn[1], norm_final]
    for i, g in enumerate(gsrc):
        gv_ = g.rearrange("(k p o) -> k p o", p=128, o=1)
        for k in range(8):
            dma(sp_q, gvec[:, i, k:k + 1], gv_[k], (), ['gvec'], 'gvec')
    ts('dve', gvec, gvec, 32.0, None, ALU.mult, None, ['gvec'], ['gvec'])

    WK = [[('W', 0, p) for p in range(12)], [('W', 1, p) for p in range(12)]]

    def hoff(h):
        return PBYTES + h * HB

    def load_w_A(l, h):
        base = hoff(h)
        win = view(base, BF16, [8, DIN])
        wout = view(base + 8 * DIN * 2, BF16, [8, D])
        src_in = w_in[l].rearrange("(k p) n -> p k n", p=128)
        src_out = w_out[l].rearrange("(k p) n -> p k n", p=128)
        for k in range(8):
            dma('pool', win[:, k, :], src_in[:, k, :], (), [WK[h][k]], ('W', h))
        for k in range(0, 8, 4):
            dma('pool', wout[:, k:k + 4, :], src_out[:, k:k + 4, :], (), [WK[h][8 + k // 4]], ('W', h))
        return win, wout

    def load_w_B(l, half, h):
        base = hoff(h)
        wup = view(base, BF16, [8, 2048])
        wdn = view(base + 8 * 2048 * 2, BF16, [16, D])
        src_up = w_up[l].rearrange("(k p) n -> p k n", p=128)
        src_dn = w_down[l].rearrange("(j p) n -> p j n", p=128)
        for k in range(8):
            dma('pool', wup[:, k, :], src_up[:, k, half * 2048:(half + 1) * 2048], (), [WK[h][k]], ('W', h))
        for j in range(0, 16, 4):
            dma('pool', wdn[:, j:j + 4, :], src_dn[:, half * 16 + j:half * 16 + j + 4, :], (), [WK[h][8 + j // 4]],
                ('W', h))
        return wup, wdn


    def units(lst):
        us = []
        cur = []
        for it in lst:
            if it[0] == 'pe' and any(x[0] == 'pe' for x in cur):
                us.append(cur)
                cur = []
            cur.append(it)
        if cur:
            us.append(cur)
        return us

    def merge(*streams):
        streams = [s_ for s_ in streams if s_]
        tot = [len(s_) for s_ in streams]
        pos = [0] * len(streams)
        outl = []
        while True:
            cand = [k for k in range(len(streams)) if pos[k] < tot[k]]
            if not cand:
                break
            k = min(cand, key=lambda k: (pos[k] + 0.5) / tot[k])
            outl.extend(streams[k][pos[k]])
            pos[k] += 1
        return outl

    def rms_stats(buf, gi, bank, inplace=False):
        act(hsq, hT[buf], AF.Square, [('hT', buf)], ['hsq'])
        lst = [(psb(bank)[:, 0:T], ones_b, hsq[:, k, :], k == 0, k == 7) for k in range(8)]
        pe_mms(lst, ['hsq', 'ones_b'], [('ps', bank)])
        act(rstd_b, psb(bank)[:, 0:T], AF.Ln, [('ps', bank)], ['rstd'], bias=float(D * EPS))
        act(rstd_b, rstd_b, AF.Exp, ['rstd'], ['rstd'], scale=-0.5)
        for k in range(8):
            if inplace:
                stt(hT[buf][:, k, :], hT[buf][:, k, :], gvec[:, gi, k:k + 1], rstd_b, ALU.mult, ALU.mult,
                    [('hT', buf), 'rstd', 'gvec'], [('hT', buf)])
            else:
                stt(xnT[buf][:, k, :], hT[buf][:, k, :], gvec[:, gi, k:k + 1], rstd_b, ALU.mult, ALU.mult,
                    [('hT', buf), 'rstd', 'gvec'], [('xnT', buf)])

    state = {}

    def phase_A(l, hw, hsrc, hdst):
        hb = 1 - hw
        AKEY = lambda n: ('A', l, n)
        win, wout = state['wA']
        Hc = Carver(hoff(hb), hoff(hb) + HB)
        dg = Hc.get(BF16, [2, 31, 128])
        biasb = Hc.get(BF16, [6, 640])
        kT = Hc.get(BF16, [3, 1024])
        vr = Hc.get(BF16, [8, 6, 65])
        pT = [Hc.get(BF16, [640]) for _ in range(2)]
        sig = [Hc.get(F32, [T]) for _ in range(2)]
        cy = Hc.get(F32, [2, T])
        cysq = Hc.get(F32, [2, T])
        cm = Hc.get(F32, [T])
        cmsq = Hc.get(F32, [T])
        cvar = Hc.get(F32, [T])
        crs = Hc.get(F32, [T])
        cd = Hc.get(F32, [2, T])
        cpar = Hc.get(F32, [3, 2])
        wT = Hc.get(F32, [2, 31])
        wtmp = Hc.get(F32, [256], parts=32)
        g_e = Hc.get(F32, [192])
        g_sp = Hc.get(F32, [192])
        g_ed = Hc.get(F32, [192])
        kdec = Hc.get(BF16, [192])
        Sst = Hc.get(F32, [4, 96], parts=48)
        Sbf = [Hc.get(BF16, [4, 96], parts=48) for _ in range(2)]
        dec = Hc.get(F32, [4, 2], parts=48)
        gsq = Hc.get(F32, [384])
        gms = Hc.get(F32, [4])
        gon = Hc.get(BF16, [384])
        wg = Hc.get(F32, [192], parts=32)
        rden = Hc.get(F32, [3])
        aob = Hc.get(BF16, [384])
        a_keys = [AKEY(n) for n in (['dg', 'bias'] + ['kT%d' % q for q in range(8)] + ['v%d' % q for q in range(8)] +
                                    ['pT0', 'pT1', 'sig0', 'sig1', 'cy', 'cysq', 'cm', 'cmsq', 'cvar', 'crs', 'cd',
                                     'cpar', 'wT', 'wtmp', 'g_e', 'g_sp', 'g_ed', 'kdec', 'S0', 'S1', 'S2', 'S3',
                                     'Sbf0', 'Sbf1', 'dec', 'gsq', 'gms', 'gon', 'wg', 'rden', 'aob', 'vones'])]
        S.fence(WK[hb], a_keys)
        S.fence(PB_KEYS, PA_KEYS)

        for i, src in enumerate((b_dw[l], conv_ln_g[l], conv_ln_b[l])):
            sv_ = src.rearrange("(b p o) -> b p o", p=128, o=1)
            for blk in range(2):
                dma(sp_q, cpar[:, i, blk:blk + 1], sv_[blk], (), [AKEY('cpar')], AKEY('cpar'))
        dma(sp_q, wtmp[0:31, :], w_dw[l], (), [AKEY('wtmp')], AKEY('wtmp'))
        for blk in range(2):
            pe_tr([(psb(6)[:, blk * 32:blk * 32 + 31], wtmp[0:31, blk * 128:(blk + 1) * 128], ident_f[0:31, 0:31])],
                  [AKEY('wtmp'), 'cst'], [('ps', 6)])
        cp('dve', wT, psb(6)[:, 0:64].rearrange("p (b j) -> p b j", b=2)[:, :, 0:31], [('ps', 6)], [AKEY('wT')])
        for blk in range(2):
            for j in range(31):
                ts('pool', dg[:, blk, j, :], ident_b, wT[:, blk, j:j + 1], None, ALU.mult, None,
                   [AKEY('wT'), 'ident_b'], [AKEY('dg')])
        memset('pool', hglu[0][:, :, 0:32], 0.0, [('hglu', 0)])
        dma(sp_q, wg[0:16, :], w_gla_gate[l], (), [AKEY('wg')], AKEY('wg'))
        dma(sp_q, wg[16:17, :], b_gla_gate[l].rearrange("(o n) -> o n", o=1), (), [AKEY('wg')], AKEY('wg'))
        dma(sp_q, gnb, gla_norm[l].partition_broadcast(128), (), ['gnb'], 'gnb')
        ts('dve', gnb, gnb, float(np.sqrt(96.0)), None, ALU.mult, None, ['gnb'], ['gnb'])
        for p_ in range(2):
            memset('pool', lrT[p_], 1.0, [('lrT', p_)])
        memset('pool', Sst, 0.0, [AKEY('S%d' % h) for h in range(4)])
        memset('pool', vr[:, :, :, 64:65], 1.0, [AKEY('vones')])
        dma(sp_q, ext[:, 0:256], rel_bias[l][:, 1:257], (), ['ext'], 'ext')
        dma(sp_q, btmp[0:6, 512:513], rel_bias[l][:, 256:257], (), ['btmp'], 'btmp',
            allow_slow_non_contiguous=True)
        cp('dve', btmp[0:6, 0:512], btmp[0:6, 512:513].to_broadcast([6, 512]), ['btmp'], ['btmp'])
        dma(sp_q, ext[:, 256:768], btmp[0:6, 0:512], ['btmp'], ['ext'], 'ext')
        for h in range(6):
            src = bass.AP(tensor=ext.tensor, offset=ext.offset + h * 768, ap=[[1, 128], [1, 640]])
            dma(sp_q, btmp, src, ['ext'], ['btmp'], 'btmp')
            pe_mms([(psb(4)[:, 0:512], J_f, btmp[:, 0:512], True, True),
                    (psb(5)[:, 0:128], J_f, btmp[:, 512:640], True, True)], ['btmp', 'cst'],
                   [('ps', 4), ('ps', 5, 'a')])
            cp('dve', biasb[:, h, 0:512], psb(4)[:, 0:512], [('ps', 4)], [AKEY('bias')])
            cp('dve', biasb[:, h, 512:640], psb(5)[:, 0:128], [('ps', 5, 'a')], [AKEY('bias')])
        memset('pool', biasb[64:128, :, 0:64], NEG, [AKEY('bias')])
        memset('pool', biasb[0:64, :, 576:640], NEG, [AKEY('bias')])

        def load_x(i):
            dma(sp_q, otile, x[i * T:(i + 1) * T, :].rearrange("(s p) d -> p s d", p=128), (), ['otile'], 'otile')

        def stage_P(i):
            par = i % 2
            S.record()
            if l == 0:
                for s in range(2):
                    for kk in range(0, 8, 4):
                        b = kk // 4
                        pe_tr([(psb(b)[:, j * 128:(j + 1) * 128], otile[:, s, (kk + j) * 128:(kk + j + 1) * 128], ident_f)
                               for j in range(4)], ['otile', 'cst'], [('ps', b)])
                        cp('act' if kk == 0 else 'dve', hT[par][:, kk:kk + 4, s * 128:(s + 1) * 128],
                           psb(b).rearrange("p (j t) -> p j t", j=4), [('ps', b)], [('hT', par)])
                if i + 1 < ntiles:
                    load_x(i + 1)
            else:
                dma(sp_q, hT[par].rearrange("p k t -> p (k t)"), hsrc[i], [('hs', id(hsrc), i)], [('hT', par)],
                    ('hT', par))
            rms_stats(par, 2 * l, 0)
            XN = ('xnT', par)
            xn = xnT[par]
            cnt = [0]

            def nb():
                cnt[0] += 1
                return cnt[0] % 2

            def fm(cols, m, pbase=0):
                b = nb()
                pe_mms([(psb(b)[pbase:pbase + m, 0:T], win[:, k, cols:cols + m], xn[:, k, :], k == 0, k == 7)
                        for k in range(8)], [XN] + WK[hw], [('ps', b)])
                return b
            b = fm(LR, 16)
            cp('dve', lrT[par][0:16, :], psb(b)[0:16, 0:T], [('ps', b)], [('lrT', par)])
            for h in range(4):
                b = fm(GQ + 48 * h, 48)
                act(gqT[par][h], psb(b)[0:48, 0:T], AF.Copy, [('ps', b)], [('gqT', par, h)], scale=float(48 ** -0.5))
            for blk in range(2):
                ba = fm(CA + 128 * blk, 128)
                bb = fm(CB + 128 * blk, 128)
                act(sig[blk], psb(bb)[:, 0:T], AF.Sigmoid, [('ps', bb)], [AKEY('sig%d' % blk)])
                tt('dve', hglu[par][:, blk, 32:32 + T], psb(ba)[:, 0:T], sig[blk], ALU.mult,
                   [('ps', ba), AKEY('sig%d' % blk)], [('hglu', par)])
            if i > 0:
                cp('pool', hglu[par][:, :, 0:32], hglu[1 - par][:, :, T:T + 32], [('hglu', 1 - par)], [('hglu', par)])
            slot0 = (2 * i) % 8
            for pr in range(3):
                b = fm(AQ + 128 * pr, 128)
                act(aqT[par][:, pr, :], psb(b)[:, 0:T], AF.Copy, [('ps', b)], [('aqT', par)], scale=0.125)
            for pr in range(3):
                b = fm(AK + 128 * pr, 128)
                cp('dve', kT[:, pr, slot0 * 128:slot0 * 128 + T], psb(b)[:, 0:T], [('ps', b)],
                   [AKEY('kT%d' % slot0), AKEY('kT%d' % (slot0 + 1))])
            for s in range(2):
                g = 2 * i + s
                slot = g % 8

                def tm(cols, n):
                    b = nb()
                    pe_mms([(psb(b)[:, 0:n], xn[:, k, s * 128:(s + 1) * 128], win[:, k, cols:cols + n], k == 0, k == 7)
                            for k in range(8)], [XN] + WK[hw], [('ps', b)])
                    return b
                b = tm(GK, 192)
                cp('dve', gla_k[par][:, s, :], psb(b)[:, 0:192], [('ps', b)], [('gla_k', par, s)])
                b = tm(GV, 384)
                cp('act', gla_v[par][:, s, :], psb(b)[:, 0:384], [('ps', b)], [('gla_v', par, s)])
                b = tm(GG, 384)
                act(gla_g[par][:, s, :], psb(b)[:, 0:384], AF.Silu, [('ps', b)], [('gla_g', par, s)])
                b = tm(AV, 384)
                cp('dve', vr[:, slot, :, 0:64], psb(b)[:, 0:384].rearrange("p (h d) -> p h d", h=6), [('ps', b)],
                   [AKEY('v%d' % slot)])
            return S.stop()

        def stage_G(i):
            par = i % 2
            S.record()
            for s in range(2):
                pe_mms([(psb(2)[:, 0:192], lrT[par][0:17, s * 128:(s + 1) * 128], wg[0:17, :], True, True)],
                       [('lrT', par), AKEY('wg')], [('ps', 2)])
                act(g_e, psb(2)[:, 0:192], AF.Exp, [('ps', 2)], [AKEY('g_e')], scale=-1.0)
                act(g_sp, g_e, AF.Ln, [AKEY('g_e')], [AKEY('g_sp')], bias=1.0)
                pe_mms([(psb(2)[:, 0:192], tri_f, g_sp, True, True)] +
                       [(psb(3)[0:48, 2 * h:2 * h + 2], g_sp[:, 48 * h:48 * h + 48], ind_f, True, True) for h in range(4)],
                       [AKEY('g_sp'), 'cst'], [('ps', 2), ('ps', 3)])
                act(g_ed, psb(2)[:, 0:192], AF.Exp, [('ps', 2)], [AKEY('g_ed')])
                act(dec, psb(3)[0:48, 0:8].rearrange("p (h c) -> p h c", h=4), AF.Exp, [('ps', 3)], [AKEY('dec')])
                tt('dve', kdec, gla_k[par][:, s, :], g_ed, ALU.mult, [('gla_k', par, s), AKEY('g_ed')], [AKEY('kdec')])
                for c in range(2):
                    sb = c
                    pe_mms([(psb(3)[0:48, 96 * h:96 * h + 96], kdec[c * 64:(c + 1) * 64, 48 * h:48 * h + 48],
                             gla_v[par][c * 64:(c + 1) * 64, s, 96 * h:96 * h + 96], True, True) for h in range(4)],
                           [AKEY('kdec'), ('gla_v', par, s)], [('ps', 3)])
                    for h in range(4):
                        stt(Sst[:, h, :], Sst[:, h, :], dec[:, h, c:c + 1], psb(3)[0:48, 96 * h:96 * h + 96],
                            ALU.mult, ALU.add, [AKEY('S%d' % h), AKEY('dec'), ('ps', 3)], [AKEY('S%d' % h)])
                    cp('act', Sbf[sb], Sst, [AKEY('S%d' % h) for h in range(4)], [AKEY('Sbf%d' % sb)])
                    pe_mms([(psb(2)[c * 64:(c + 1) * 64, 96 * h:96 * h + 96],
                             gqT[par][h][:, s * 128 + c * 64:s * 128 + c * 64 + 64], Sbf[sb][:, h, :], True, True)
                            for h in range(4)], [('gqT', par, h) for h in range(4)] + [AKEY('Sbf%d' % sb)],
                           [('ps', 2)])
                cp('act', go, psb(2)[:, 0:384], [('ps', 2)], ['go'])
                tt('pool', gsq, go, go, ALU.mult, ['go'], [AKEY('gsq')])

                def red(e, gms=gms, gsq=gsq):
                    return e.tensor_reduce(out=gms, in_=gsq.rearrange("p (h v) -> p h v", h=4), axis=AX.X, op=ALU.add)
                S.add('dve', red, [AKEY('gsq')], [AKEY('gms')])
                act(gms, gms, AF.Ln, [AKEY('gms')], [AKEY('gms')], bias=float(96 * EPS))
                act(gms, gms, AF.Exp, [AKEY('gms')], [AKEY('gms')], scale=-0.5)
                tt('dve', go, go, gnb, ALU.mult, ['go', 'gnb'], ['go'])
                tt('dve', go, go, gla_g[par][:, s, :], ALU.mult, ['go', ('gla_g', par, s)], ['go'])
                tt('dve', gon.rearrange("p (h v) -> p h v", h=4), go.rearrange("p (h v) -> p h v", h=4),
                   gms.unsqueeze(2).to_broadcast([128, 4, 96]), ALU.mult, ['go', AKEY('gms')], [AKEY('gon')])
                pe_tr([(psbf(3)[:, j * 128:(j + 1) * 128], gon[:, j * 128:(j + 1) * 128], ident_b) for j in range(3)],
                      [AKEY('gon'), 'ident_b'], [('ps', 3)])
                cp('dve', mixT[:, 0:3, s * 128:(s + 1) * 128], psbf(3)[:, 0:384].rearrange("p (j t) -> p j t", j=3),
                   [('ps', 3)], [('mixT', 'g', s)])
            for blk in range(2):
                pe_mms([(psb(7)[:, blk * T:(blk + 1) * T], dg[:, blk, j, :], hglu[par][:, blk, 2 + j:2 + j + T],
                         j == 0, j == 30) for j in range(31)], [AKEY('dg'), ('hglu', par)], [('ps', 7, blk)])
                act(cy[:, blk, :], psb(7)[:, blk * T:(blk + 1) * T], AF.Identity, [('ps', 7, blk), AKEY('cpar')],
                    [AKEY('cy')], bias=cpar[:, 0, blk:blk + 1])
                act(cysq[:, blk, :], psb(7)[:, blk * T:(blk + 1) * T], AF.Square, [('ps', 7, blk), AKEY('cpar')],
                    [AKEY('cysq')], bias=cpar[:, 0, blk:blk + 1])
            pe_mms([(psb(7)[:, 0:T], ones_f, cy[:, 0, :], True, False), (psb(7)[:, 0:T], ones_f, cy[:, 1, :], False, True),
                    (psb(7)[:, T:2 * T], ones_f, cysq[:, 0, :], True, False),
                    (psb(7)[:, T:2 * T], ones_f, cysq[:, 1, :], False, True)],
                   [AKEY('cy'), AKEY('cysq'), 'cst'], [('ps', 7, 0), ('ps', 7, 1)])
            ts('dve', cm, psb(7)[:, 0:T], 1.0 / 256.0, None, ALU.mult, None, [('ps', 7, 0)], [AKEY('cm')])
            tt('dve', cmsq, cm, cm, ALU.mult, [AKEY('cm')], [AKEY('cmsq')])
            stt(cvar, psb(7)[:, T:2 * T], 1.0 / 256.0, cmsq, ALU.mult, ALU.subtract, [('ps', 7, 1), AKEY('cmsq')],
                [AKEY('cvar')])
            act(crs, cvar, AF.Ln, [AKEY('cvar')], [AKEY('crs')], bias=float(EPS))
            act(crs, crs, AF.Exp, [AKEY('crs')], [AKEY('crs')], scale=-0.5)
            for blk in range(2):
                tt('pool', cd[:, blk, :], cy[:, blk, :], cm, ALU.subtract, [AKEY('cy'), AKEY('cm')], [AKEY('cd')])
                tt('pool', cd[:, blk, :], cd[:, blk, :], crs, ALU.mult, [AKEY('cd'), AKEY('crs')], [AKEY('cd')])
                act(mixT[:, 3 + blk, :], cd[:, blk, :], AF.Silu, [AKEY('cd'), AKEY('cpar')], [('mixT', 'c', blk)],
                    scale=cpar[:, 1, blk:blk + 1], bias=cpar[:, 2, blk:blk + 1])
            return S.stop()

        def stage_Att(i):
            par = i % 2
            S.record()
            for s in range(2):
                g = 2 * i + s
                nj = min(5, g + 1)
                for hb3 in range(2):
                    for hh in range(3):
                        h = hb3 * 3 + hh
                        pr, p0 = h // 2, 64 * (h % 2)
                        sb = h % 2
                        bA = 4 if sb == 0 else 6
                        kB = ('ps', 5, 'a') if sb == 0 else ('ps', 5, 'b')
                        cB = 0 if sb == 0 else 128
                        q_ap = aqT[par][p0:p0 + 64, pr, s * 128:(s + 1) * 128]
                        lst = [(psb(bA)[:, 0:512], ident_b, biasb[:, h, 0:512], True, False)]
                        for j in range(min(nj, 4)):
                            sl = (g - j) % 8
                            lst.append((psb(bA)[:, j * 128:(j + 1) * 128], kT[p0:p0 + 64, pr, sl * 128:(sl + 1) * 128],
                                        q_ap, False, j == min(nj, 4) - 1))
                        rd = ['ident_b', AKEY('bias'), ('aqT', par)] + [AKEY('kT%d' % ((g - j) % 8)) for j in range(nj)]
                        wr = [('ps', bA)]
                        if nj == 5:
                            sl = (g - 4) % 8
                            lst.append((psb(5)[:, cB:cB + 128], ident_b, biasb[:, h, 512:640], True, False))
                            lst.append((psb(5)[:, cB:cB + 128], kT[p0:p0 + 64, pr, sl * 128:(sl + 1) * 128], q_ap, False, True))
                            wr.append(kB)
                        pe_mms(lst, rd, wr)
                        na = min(nj, 4) * 128
                        act(pT[sb][:, 0:na], psb(bA)[:, 0:na], AF.Exp, [('ps', bA)], [AKEY('pT%d' % sb)])
                        if nj == 5:
                            act(pT[sb][:, 512:640], psb(5)[:, cB:cB + 128], AF.Exp, [kB], [AKEY('pT%d' % sb)])
                        pe_mms([(psb(5)[:, 256 + 65 * hh:256 + 65 * hh + 65], pT[sb][:, j * 128:(j + 1) * 128],
                                 vr[:, (g - j) % 8, h, :], j == 0, j == nj - 1) for j in range(nj)],
                               [AKEY('pT%d' % sb), AKEY('vones')] + [AKEY('v%d' % ((g - j) % 8)) for j in range(nj)],
                               [('ps', 5, 'c')])
                    o3 = psb(5)[:, 256:256 + 195].rearrange("p (h d) -> p h d", h=3)

                    def rcp(e, rden=rden, o3=o3):
                        return e.reciprocal(out=rden.unsqueeze(2), in_=o3[:, :, 64:65])
                    S.add('dve', rcp, [('ps', 5, 'c')], [AKEY('rden')])
                    tt('dve', aob[:, hb3 * 192:(hb3 + 1) * 192].rearrange("p (h d) -> p h d", h=3), o3[:, :, 0:64],
                       rden.unsqueeze(2).to_broadcast([128, 3, 64]), ALU.mult, [('ps', 5, 'c'), AKEY('rden')],
                       [AKEY('aob')])
                pe_tr([(psbf(5)[:, 512 + j * 128:512 + (j + 1) * 128], aob[:, j * 128:(j + 1) * 128], ident_b)
                       for j in range(3)], [AKEY('aob'), 'ident_b'], [('ps', 5, 'c')])
                cp('dve', mixT[:, 5:8, s * 128:(s + 1) * 128],
                   psbf(5)[:, 512:896].rearrange("p (j t) -> p j t", j=3), [('ps', 5, 'c')], [('mixT', 'a', s)])
            return S.stop()

        MIXK = [('mixT', 'g', 0), ('mixT', 'g', 1), ('mixT', 'c', 0), ('mixT', 'c', 1), ('mixT', 'a', 0), ('mixT', 'a', 1)]

        def stage_O(i):
            par = i % 2
            S.record()
            for m in range(8):
                b = 2 + (m % 2)
                pe_mms([(psb(b)[:, 0:T], wout[:, k, m * 128:(m + 1) * 128], mixT[:, k, :], k == 0, k == 7)
                        for k in range(8)], MIXK + WK[hw], [('ps', b)])
                tt('dve', hT[par][:, m, :], hT[par][:, m, :], psb(b)[:, 0:T], ALU.add, [('hT', par), ('ps', b)],
                   [('hT', par)])
            dma(sp_q, hdst[i], hT[par].rearrange("p k t -> p (k t)"), [('hT', par)], [('hs', id(hdst), i)],
                ('hTst', par))
            return S.stop()

        if l == 0:
            load_x(0)
        S.replay(stage_P(0))
        for i in range(ntiles):
            pn = units(stage_P(i + 1)) if i + 1 < ntiles else []
            gu = units(stage_G(i))
            au = units(stage_Att(i))
            S.replay(merge(pn, gu, au))
            S.replay(stage_O(i))
        return a_keys

    def phase_B(l, half, hw, hsrc, hdst, final=False):
        wup, wdn = state['wB']
        S.fence(PA_KEYS, PB_KEYS)

        def stage_X(i):
            par = i % 2
            dma(sp_q, hT[par].rearrange("p k t -> p (k t)"), hsrc[i], [('hs', id(hsrc), i)], [('hT', par)], ('hT', par))
            if half == 1:
                dma(sp_q, xnT[par].rearrange("p k t -> p (k t)"), xs[i], [('xs', i)], [('xnT', par)], ('xnT', par))
            else:
                rms_stats(par, 2 * l + 1, 0)
                dma(sp_q, xs[i], xnT[par].rearrange("p k t -> p (k t)"), [('xnT', par)], [('xs', i)], ('xnTst', par))

        def stage_U(i):
            par = i % 2
            S.record()
            for j in range(16):
                b = j % 4
                pe_mms([(psb(b)[:, 0:T], wup[:, k, j * 128:(j + 1) * 128], xnT[par][:, k, :], k == 0, k == 7)
                        for k in range(8)], [('xnT', par)] + WK[hw], [('ps', b)])
                r = relu_t[j % 2]
                act(r, psb(b)[:, 0:T], AF.Relu, [('ps', b)], [('relu', j % 2)])
                tt('pool' if j % 2 == 0 else 'dve', hidT[par][:, j, :], r, r, ALU.mult, [('relu', j % 2)],
                   [('hid', par, j)])
            return S.stop()

        def stage_D(i):
            par = i % 2
            for m in range(8):
                b = 4 + (m % 4)
                pe_mms([(psb(b)[:, 0:T], wdn[:, j, m * 128:(m + 1) * 128], hidT[par][:, j, :], j == 0, j == 15)
                        for j in range(16)], [('hid', par, j) for j in range(16)] + WK[hw], [('ps', b)])
                tt('dve', hT[par][:, m, :], hT[par][:, m, :], psb(b)[:, 0:T], ALU.add, [('hT', par), ('ps', b)],
                   [('hT', par)])
            if not final:
                dma(sp_q, hdst[i], hT[par].rearrange("p k t -> p (k t)"), [('hT', par)], [('hs', id(hdst), i)],
                    ('hTst', par))

        def stage_F(i):
            par = i % 2
            S.record()
            rms_stats(par, 4, 4, inplace=True)
            for s in range(2):
                for kk in range(0, 8, 4):
                    b = 5 + (kk // 4)
                    pe_tr([(psb(b)[:, j * 128:(j + 1) * 128], hT[par][:, kk + j, s * 128:(s + 1) * 128], ident_f)
                           for j in range(4)], [('hT', par), 'cst'], [('ps', b)])
                    cp('act' if kk == 0 else 'dve', otile[:, s, kk * 128:(kk + 4) * 128], psb(b), [('ps', b)],
                       ['otile'])
            dma(sp_q, out[i * T:(i + 1) * T, :].rearrange("(s p) d -> p s d", p=128), otile, ['otile'], ['out'],
                'otile_st')
            return S.stop()

        stage_X(0)
        S.replay(stage_U(0))
        for i in range(ntiles):
            if i + 1 < ntiles:
                stage_X(i + 1)
            stage_D(i)
            un = units(stage_U(i + 1)) if i + 1 < ntiles else []
            fu = units(stage_F(i)) if final else []
            S.replay(merge(un, fu))

    def dump_dbg(hsbuf):
        dbg = nc.dram_tensor("dbg", [NT, 128, 8 * T], F32, kind="ExternalOutput").ap()
        for i in range(ntiles):
            dma(sp_q, hT[0].rearrange("p k t -> p (k t)"), hsbuf[i], [('hs', id(hsbuf), i)], [('hT', 0)], ('hT', 0))
            dma(sp_q, dbg[i], hT[0].rearrange("p k t -> p (k t)"), [('hT', 0)], ['dbg'], 'dbg')

    cur = 0
    wA = load_w_A(0, 0)
    for l in range(DEPTH):
        hw = l % 2
        state['wA'] = wA
        if l == 0:
            akeys = phase_A(l, hw, None, hs[0])
            cur = 0
        else:
            akeys = phase_A(l, hw, hs[cur], hs[1 - cur])
            cur = 1 - cur
        if stop_after == (l, 'A'):
            dump_dbg(hs[cur])
            break
        S.fence(akeys, WK[1 - hw])
        wB1 = load_w_B(l, 0, 1 - hw)
        wB2 = load_w_B(l, 1, hw)
        state['wB'] = wB1
        phase_B(l, 0, 1 - hw, hs[cur], hs[1 - cur])
        cur = 1 - cur
        if stop_after == (l, 'B1'):
            dump_dbg(hs[cur])
            break
        if l + 1 < DEPTH:
            wA = load_w_A(l + 1, 1 - hw)
        state['wB'] = wB2
        fin = (l == DEPTH - 1 and stop_after is None)
        phase_B(l, 1, hw, hs[cur], hs[1 - cur], final=fin)
        cur = 1 - cur
        if stop_after == (l, 'B2'):
            dump_dbg(hs[cur])
            break

    S.emit(nc, es)
    es.close()
    return nc


_NC_CACHE = {}


def kernel(**inputs):
    key = 'full'
    if key not in _NC_CACHE:
        _NC_CACHE[key] = build()
    nc = _NC_CACHE[key]
    cstv = make_consts()
    names = ["norm_mix", "w_in", "w_gla_gate", "b_gla_gate", "gla_norm", "w_dw", "b_dw", "conv_ln_g", "conv_ln_b",
             "rel_bias", "w_out", "norm_ffn", "w_up", "w_down", "norm_final"]
    shared = {n: np.ascontiguousarray(np.asarray(inputs[n], dtype=np.float32)) for n in names}
    xfull = np.asarray(inputs["x"], dtype=np.float32)
    in_maps = []
    for c in range(8):
        m = dict(shared)
        m["x"] = np.ascontiguousarray(xfull[c])
        m["cst"] = cstv
        in_maps.append(m)
    res = run_bass_kernel_spmd(nc, in_maps, core_ids=list(range(8)))
    return np.stack([np.asarray(r["out"], dtype=np.float32) for r in res.results], axis=0)
```

```python
import numpy as np
from contextlib import ExitStack
import concourse.bass as bass
import concourse.mybir as mybir
from concourse.bass_utils import run_bass_kernel_spmd

F32 = mybir.dt.float32
BF16 = mybir.dt.bfloat16
AF = mybir.ActivationFunctionType
ALU = mybir.AluOpType
AX = mybir.AxisListType

D = 1024
SEQ = 4096
DEPTH = 2
T = 256
NT = SEQ // T
DIN = 2832
DFF = 4096
EPS = 1e-6
GQ, GK, GV, GG, LR, CA, CB, AQ, AK, AV = 0, 192, 384, 768, 1152, 1168, 1424, 1680, 2064, 2448
NEG = -30000.0

C_ID, C_TRI, C_ONES, C_IND, C_J = 0, 128, 256, 384, 386
NCST = 514


def make_consts():
    c = np.zeros((128, NCST), np.float32)
    c[:, C_ID:C_ID + 128] = np.eye(128, dtype=np.float32)
    s = np.arange(128)[:, None]
    t = np.arange(128)[None, :]
    c[:, C_TRI:C_TRI + 128] = np.where((s > t) & (s // 64 == t // 64), -1.0 / 16.0, 0.0)
    c[:, C_ONES:C_ONES + 128] = 1.0
    c[:, C_IND:C_IND + 2] = np.where(s // 64 == np.arange(2)[None, :], -1.0 / 16.0, 0.0)
    c[:, C_J:C_J + 128] = np.eye(128, dtype=np.float32)[::-1]
    return c


class Sched:
    def __init__(self):
        self.ops = []
        self.lw = {}
        self.rd = {}
        self.rec = None

    def record(self):
        assert self.rec is None
        self.rec = []

    def stop(self):
        r = self.rec
        self.rec = None
        return r

    def replay(self, items):
        assert self.rec is None
        for it in items:
            self.add(*it)

    def add(self, eng, fn, reads=(), writes=(), dma=None):
        if self.rec is not None:
            self.rec.append((eng, fn, tuple(reads), tuple(writes), dma))
            return -1
        raw = set()
        for k in reads:
            raw |= self.lw.get(k, set())
        oth = set()
        for k in writes:
            oth |= self.lw.get(k, set())
            oth |= set(self.rd.get(k, {}).values())
        mysig = ('d', dma) if dma is not None else ('e', eng)
        for k in reads:
            if isinstance(k, tuple) and k[0] == 'ps':
                oth |= {v for sg, v in self.rd.get(k, {}).items() if sg != mysig}
        i = len(self.ops)
        self.ops.append(dict(eng=eng, fn=fn, raw=raw, oth=oth - raw, dma=dma, need=False))
        sig = ('d', dma) if dma is not None else ('e', eng)
        for k in reads:
            self.rd.setdefault(k, {})[sig] = i
        for k in writes:
            self.lw[k] = {i}
            self.rd[k] = {}
        return i

    def fence(self, src, dst):
        acc = set()
        for k in src:
            acc |= self.lw.get(k, set()) | set(self.rd.get(k, {}).values())
        for k in dst:
            self.lw[k] = self.lw.get(k, set()) | acc

    def emit(self, nc, es):
        ops = self.ops
        CAP = 30000
        for o in ops:
            deps = set()
            for j in o['raw']:
                pj = ops[j]
                if pj['dma'] is not None or pj['eng'] != o['eng'] or o['eng'] != 'pe':
                    deps.add(j)
            for j in o['oth']:
                pj = ops[j]
                if pj['dma'] is not None or pj['eng'] != o['eng'] or o['dma'] is not None:
                    deps.add(j)
            o['deps'] = deps
            for j in deps:
                ops[j]['need'] = True
        cnt = {}
        dcnt = {}
        engsems = {}
        dsems = {}

        def get_esem(e, idx):
            key = (e, idx)
            if key not in engsems:
                engsems[key] = es.enter_context(nc.semaphore("s_%s_%d" % (e, idx)))
            return engsems[key]

        for o in ops:
            if o['dma'] is not None:
                k = o['dma']
                if k not in dsems:
                    dsems[k] = es.enter_context(nc.semaphore("d_%d" % len(dsems)))
                dcnt[k] = dcnt.get(k, 0) + 16
                o['sem'] = dsems[k]
                o['val'] = dcnt[k]
                o['need'] = True
            elif o['need']:
                e = o['eng']
                c = cnt.get(e, 0)
                o['sem'] = get_esem(e, c // CAP)
                o['val'] = c % CAP + 1
                cnt[e] = c + 1
        self.nsem = len(engsems) + len(dsems)
        per = {e: [] for e in ('pe', 'act', 'dve', 'pool', 'sp')}
        for o in ops:
            per[o['eng']].append(o)

        def run(eng_obj, lst):
            waited = {}
            for o in lst:
                for j in sorted(o['deps']):
                    pj = ops[j]
                    sid = id(pj['sem'])
                    if waited.get(sid, 0) >= pj['val']:
                        continue
                    waited[sid] = pj['val']
                    eng_obj.wait_ge(pj['sem'], pj['val'])
                inst = o['fn'](eng_obj)
                if o['dma'] is not None:
                    inst.then_inc(o['sem'], 16)
                elif o['need']:
                    inst.then_inc(o['sem'], 1)
            last = {}
            for o in lst:
                if o['dma'] is not None:
                    last[id(o['sem'])] = (o['sem'], o['val'])
            for sid, (sm, v) in last.items():
                if waited.get(sid, 0) < v:
                    eng_obj.wait_ge(sm, v)

        block = es.enter_context(nc.Block())

        @block.sync
        def _(e):
            run(e, per['sp'])

        @block.gpsimd
        def _(e):
            run(e, per['pool'])

        @block.scalar
        def _(e):
            run(e, per['act'])

        @block.vector
        def _(e):
            run(e, per['dve'])

        @block.tensor
        def _(e):
            run(e, per['pe'])


def build(ntiles=NT, stop_after=None):
    nc = bass.Bass("TRN2", target_bir_lowering=False)
    es = ExitStack()
    dr = {}

    def din(name, shape):
        dr[name] = nc.dram_tensor(name, list(shape), F32, kind="ExternalInput").ap()
        return dr[name]

    x = din("x", [SEQ, D])
    norm_mix = din("norm_mix", [DEPTH, D])
    w_in = din("w_in", [DEPTH, D, DIN])
    w_gla_gate = din("w_gla_gate", [DEPTH, 16, 192])
    b_gla_gate = din("b_gla_gate", [DEPTH, 192])
    gla_norm = din("gla_norm", [DEPTH, 384])
    w_dw = din("w_dw", [DEPTH, 31, 256])
    b_dw = din("b_dw", [DEPTH, 256])
    conv_ln_g = din("conv_ln_g", [DEPTH, 256])
    conv_ln_b = din("conv_ln_b", [DEPTH, 256])
    rel_bias = din("rel_bias", [DEPTH, 6, 257])
    w_out = din("w_out", [DEPTH, D, D])
    norm_ffn = din("norm_ffn", [DEPTH, D])
    w_up = din("w_up", [DEPTH, D, DFF])
    w_down = din("w_down", [DEPTH, DFF, D])
    norm_final = din("norm_final", [D])
    cst = din("cst", [128, NCST])
    out = nc.dram_tensor("out", [SEQ, D], F32, kind="ExternalOutput").ap()
    hs = [nc.dram_tensor("hs%d" % i, [NT, 128, 8 * T], F32, kind="Internal").ap() for i in range(2)]
    xs = nc.dram_tensor("xs", [NT, 128, 8 * T], BF16, kind="Internal").ap()
    ext = nc.dram_tensor("ext", [6, 768], F32, kind="Internal").ap()

    PBYTES = 78 * 1024
    HB = 64 * 1024
    TOT = PBYTES + 2 * HB
    arena_t = es.enter_context(nc.sbuf_tensor("arena", [128, TOT // 4], F32))
    ps = [es.enter_context(nc.psum_tensor("ps%d" % b, [128, 512], F32)) for b in range(8)]

    def view(off, dtype, shape, parts=128, p0=0):
        n = int(np.prod(shape))
        esz = 4 if dtype == F32 else 2
        nb = n * esz
        assert off % 4 == 0
        a = arena_t[p0:p0 + parts, off // 4:(off + nb + 3) // 4]
        if dtype != F32:
            a = a.bitcast(dtype)
        a = a[:, 0:n]
        if len(shape) == 2:
            a = a.rearrange("p (a b) -> p a b", a=shape[0])
        elif len(shape) == 3:
            a = a.rearrange("p (a b c) -> p a b c", a=shape[0], b=shape[1])
        return a

    class Carver:
        def __init__(self, base, limit):
            self.off = base
            self.limit = limit

        def get(self, dtype, shape, parts=128, p0=0):
            n = int(np.prod(shape))
            nb = n * (4 if dtype == F32 else 2)
            nb = (nb + 31) // 32 * 32
            o = self.off
            self.off += nb
            assert self.off <= self.limit, ("arena overflow", self.off, self.limit)
            return view(o, dtype, shape, parts, p0)

    P = Carver(0, PBYTES)
    cst_sb = P.get(F32, [NCST])
    ident_f = cst_sb[:, C_ID:C_ID + 128]
    tri_f = cst_sb[:, C_TRI:C_TRI + 128]
    ones_f = cst_sb[:, C_ONES:C_ONES + 128]
    ind_f = cst_sb[:, C_IND:C_IND + 2]
    J_f = cst_sb[:, C_J:C_J + 128]
    ident_b = P.get(BF16, [128])
    ones_b = P.get(BF16, [128])
    gvec = P.get(F32, [5, 8])
    hT = [P.get(F32, [8, T]) for _ in range(2)]
    xnT = [P.get(BF16, [8, T]) for _ in range(2)]
    hsq = P.get(BF16, [8, T])
    mixT = P.get(BF16, [8, T])
    rstd_b = P.get(F32, [T])
    otile = P.get(F32, [2, D])
    ab_base = P.off
    PA = Carver(ab_base, PBYTES)
    gla_k = [PA.get(F32, [2, 192]) for _ in range(2)]
    gla_v = [PA.get(BF16, [2, 384]) for _ in range(2)]
    gla_g = [PA.get(F32, [2, 384]) for _ in range(2)]
    gqT = [[PA.get(BF16, [T], parts=48) for _ in range(4)] for _ in range(2)]
    aqT = [PA.get(BF16, [3, T]) for _ in range(2)]
    lrT = [PA.get(F32, [T], parts=32) for _ in range(2)]
    hglu = [PA.get(BF16, [2, 32 + T]) for _ in range(2)]
    btmp = PA.get(F32, [640])
    gnb = PA.get(F32, [384])
    go = PA.get(F32, [384])
    PA_KEYS = ([('gla_k', p, s_) for p in range(2) for s_ in range(2)] + [('gla_v', p, s_) for p in range(2) for s_ in range(2)]
               + [('gla_g', p, s_) for p in range(2) for s_ in range(2)] + [('gqT', p, h) for p in range(2) for h in range(4)]
               + [('aqT', p) for p in range(2)] + [('lrT', p) for p in range(2)] + [('hglu', p) for p in range(2)]
               + ['btmp', 'gnb', 'go'])
    PB = Carver(ab_base, PBYTES)
    hidT = [PB.get(BF16, [16, T]) for _ in range(2)]
    relu_t = [PB.get(F32, [T]) for _ in range(2)]
    PB_KEYS = [('hid', p, j) for p in range(2) for j in range(16)] + [('relu', n) for n in range(2)]

    S = Sched()
    sp_q = 'sp'

    def dma(q, out_ap, in_ap, reads, writes, key, **kw):
        def fn(e, out_ap=out_ap, in_ap=in_ap, kw=kw):
            return e.dma_start(out=out_ap, in_=in_ap, **kw)
        return S.add(q, fn, reads, writes, dma=key)

    def pe_mms(lst, reads, writes):
        def fn(e, lst=lst):
            ins = None
            for (o, l, r, st, sp) in lst:
                ins = e.matmul(o, lhsT=l, rhs=r, start=st, stop=sp)
            return ins
        return S.add('pe', fn, reads, writes)

    def pe_tr(lst, reads, writes):
        def fn(e, lst=lst):
            ins = None
            for (o, i, idn) in lst:
                ins = e.transpose(o, i, idn)
            return ins
        return S.add('pe', fn, reads, writes)

    def act(out_ap, in_ap, func, reads, writes, **kw):
        def fn(e, out_ap=out_ap, in_ap=in_ap, func=func, kw=kw):
            return e.activation(out=out_ap, in_=in_ap, func=func, **kw)
        return S.add('act', fn, reads, writes)

    def ts(eng, out_ap, in0, s1, s2, op0, op1, reads, writes):
        def fn(e, out_ap=out_ap, in0=in0, s1=s1, s2=s2, op0=op0, op1=op1):
            if op1 is None:
                return e.tensor_scalar(out=out_ap, in0=in0, scalar1=s1, scalar2=None, op0=op0)
            return e.tensor_scalar(out=out_ap, in0=in0, scalar1=s1, scalar2=s2, op0=op0, op1=op1)
        return S.add(eng, fn, reads, writes)

    def tt(eng, out_ap, in0, in1, op, reads, writes):
        def fn(e, out_ap=out_ap, in0=in0, in1=in1, op=op):
            return e.tensor_tensor(out=out_ap, in0=in0, in1=in1, op=op)
        return S.add(eng, fn, reads, writes)

    def stt(out_ap, in0, sc, in1, op0, op1, reads, writes):
        def fn(e, out_ap=out_ap, in0=in0, sc=sc, in1=in1, op0=op0, op1=op1):
            return e.scalar_tensor_tensor(out=out_ap, in0=in0, scalar=sc, in1=in1, op0=op0, op1=op1)
        return S.add('dve', fn, reads, writes)

    def cp(eng, out_ap, in_ap, reads, writes):
        if eng == 'act':
            return act(out_ap, in_ap, AF.Copy, reads, writes)
        def fn(e, out_ap=out_ap, in_ap=in_ap):
            return e.tensor_copy(out=out_ap, in_=in_ap)
        return S.add(eng, fn, reads, writes)

    def memset(eng, ap, val, writes):
        def fn(e, ap=ap, val=val):
            return e.memset(ap, val)
        return S.add(eng, fn, (), writes)

    def psb(b):
        return ps[b][:]

    def psbf(b):
        return ps[b][:].bitcast(BF16)

    dma(sp_q, cst_sb, cst[:, :], (), ['cst'], 'cst')
    cp('dve', ident_b, ident_f, ['cst'], ['ident_b'])
    cp('dve', ones_b, ones_f, ['cst'], ['ones_b'])
    gsrc = [norm_mix[0], norm_ffn[0], norm_mix[1], norm_ffn[1], norm_final]
    for i, g in enumerate(gsrc):
        gv_ = g.rearrange("(k p o) -> k p o", p=128, o=1)
        for k in range(8):
            dma(sp_q, gvec[:, i, k:k + 1], gv_[k], (), ['gvec'], 'gvec')
    ts('dve', gvec, gvec, 32.0, None, ALU.mult, None, ['gvec'], ['gvec'])

    WK = [[('W', 0, p) for p in range(12)], [('W', 1, p) for p in range(12)]]

    def hoff(h):
        return PBYTES + h * HB

    def load_w_A(l, h):
        base = hoff(h)
        win = view(base, BF16, [8, DIN])
        wout = view(base + 8 * DIN * 2, BF16, [8, D])
        src_in = w_in[l].rearrange("(k p) n -> p k n", p=128)
        src_out = w_out[l].rearrange("(k p) n -> p k n", p=128)
        for k in range(8):
            dma('pool', win[:, k, :], src_in[:, k, :], (), [WK[h][k]], ('W', h))
        for k in range(0, 8, 4):
            dma('pool', wout[:, k:k + 4, :], src_out[:, k:k + 4, :], (), [WK[h][8 + k // 4]], ('W', h))
        return win, wout

    def load_w_B(l, half, h):
        base = hoff(h)
        wup = view(base, BF16, [8, 2048])
        wdn = view(base + 8 * 2048 * 2, BF16, [16, D])
        src_up = w_up[l].rearrange("(k p) n -> p k n", p=128)
        src_dn = w_down[l].rearrange("(j p) n -> p j n", p=128)
        for k in range(8):
            dma('pool', wup[:, k, :], src_up[:, k, half * 2048:(half + 1) * 2048], (), [WK[h][k]], ('W', h))
        for j in range(0, 16, 4):
            dma('pool', wdn[:, j:j + 4, :], src_dn[:, half * 16 + j:half * 16 + j + 4, :], (), [WK[h][8 + j // 4]],
                ('W', h))
        return wup, wdn


    def units(lst):
        us = []
        cur = []
        for it in lst:
            if it[0] == 'pe' and any(x[0] == 'pe' for x in cur):
                us.append(cur)
                cur = []
            cur.append(it)
        if cur:
            us.append(cur)
        return us

    def merge(*streams):
        keyed = []
        for si, st in enumerate(streams):
            us, lo, hi = st[0], st[1], st[2]
            early = st[3] if len(st) > 3 else False
            n = len(us)
            for j, u in enumerate(us):
                k = lo + (j + 0.5) / n * (hi - lo)
                if early and j == 0:
                    k = -1.0
                keyed.append((k, si, j, u))
        keyed.sort(key=lambda t: (t[0], t[1], t[2]))
        outl = []
        for _, _, _, u in keyed:
            outl.extend(u)
        return outl

    def rms_stats(buf, gi, bank, inplace=False):
        act(hsq, hT[buf], AF.Square, [('hT', buf)], ['hsq'])
        lst = [(psb(bank)[:, 0:T], ones_b, hsq[:, k, :], k == 0, k == 7) for k in range(8)]
        pe_mms(lst, ['hsq', 'ones_b'], [('ps', bank)])
        act(rstd_b, psb(bank)[:, 0:T], AF.Ln, [('ps', bank)], ['rstd'], bias=float(D * EPS))
        act(rstd_b, rstd_b, AF.Exp, ['rstd'], ['rstd'], scale=-0.5)
        for k in range(8):
            if inplace:
                stt(hT[buf][:, k, :], hT[buf][:, k, :], gvec[:, gi, k:k + 1], rstd_b, ALU.mult, ALU.mult,
                    [('hT', buf), 'rstd', 'gvec'], [('hT', buf)])
            else:
                stt(xnT[buf][:, k, :], hT[buf][:, k, :], gvec[:, gi, k:k + 1], rstd_b, ALU.mult, ALU.mult,
                    [('hT', buf), 'rstd', 'gvec'], [('xnT', buf)])

    state = {}

    def phase_A(l, hw, hsrc, hdst):
        hb = 1 - hw
        AKEY = lambda n: ('A', l, n)
        win, wout = state['wA']
        Hc = Carver(hoff(hb), hoff(hb) + HB)
        dg = Hc.get(BF16, [2, 31, 128])
        biasb = Hc.get(BF16, [6, 640])
        kT = Hc.get(BF16, [3, 1024])
        vr = Hc.get(BF16, [8, 6, 65])
        pT = [Hc.get(BF16, [640]) for _ in range(2)]
        sig = [Hc.get(F32, [T]) for _ in range(2)]
        cy = Hc.get(F32, [2, T])
        cysq = Hc.get(F32, [2, T])
        cm = Hc.get(F32, [T])
        cmsq = Hc.get(F32, [T])
        cvar = Hc.get(F32, [T])
        crs = Hc.get(F32, [T])
        cd = Hc.get(F32, [2, T])
        cpar = Hc.get(F32, [3, 2])
        wT = Hc.get(F32, [2, 31])
        wtmp = Hc.get(F32, [256], parts=32)
        g_e = Hc.get(F32, [192])
        g_sp = Hc.get(F32, [192])
        g_ed = Hc.get(F32, [192])
        kdec = Hc.get(BF16, [192])
        Sst = Hc.get(F32, [4, 96], parts=48)
        Sbf = [Hc.get(BF16, [4, 96], parts=48) for _ in range(2)]
        dec = Hc.get(F32, [4, 2], parts=48)
        gsq = Hc.get(F32, [384])
        gms = Hc.get(F32, [4])
        gon = Hc.get(BF16, [384])
        wg = Hc.get(F32, [192], parts=32)
        rden = Hc.get(F32, [3])
        aob = Hc.get(BF16, [384])
        gsg = Hc.get(F32, [384])
        ncp = Hc.get(F32, [2, 2])
        a_keys = [AKEY(n) for n in (['dg', 'bias'] + ['kT%d' % q for q in range(8)] + ['v%d' % q for q in range(8)] +
                                    ['pT0', 'pT1', 'sig0', 'sig1', 'cy', 'cysq', 'cm', 'cmsq', 'cvar', 'crs', 'cd',
                                     'cpar', 'wT', 'wtmp', 'g_e', 'g_sp', 'g_ed', 'kdec', 'S0', 'S1', 'S2', 'S3',
                                     'Sbf0', 'Sbf1', 'dec', 'gsq', 'gms', 'gon', 'wg', 'rden', 'aob', 'vones', 'gsg', 'ncp'])]
        S.fence(WK[hb], a_keys)
        S.fence(PB_KEYS, PA_KEYS)

        for i, src in enumerate((b_dw[l], conv_ln_g[l], conv_ln_b[l])):
            sv_ = src.rearrange("(b p o) -> b p o", p=128, o=1)
            for blk in range(2):
                dma(sp_q, cpar[:, i, blk:blk + 1], sv_[blk], (), [AKEY('cpar')], AKEY('cpar'))
        ts('dve', ncp, cpar[:, 1:3, :], -1.0, None, ALU.mult, None, [AKEY('cpar')], [AKEY('ncp')])
        dma(sp_q, wtmp[0:31, :], w_dw[l], (), [AKEY('wtmp')], AKEY('wtmp'))
        for blk in range(2):
            pe_tr([(psb(6)[:, blk * 32:blk * 32 + 31], wtmp[0:31, blk * 128:(blk + 1) * 128], ident_f[0:31, 0:31])],
                  [AKEY('wtmp'), 'cst'], [('ps', 6)])
        cp('dve', wT, psb(6)[:, 0:64].rearrange("p (b j) -> p b j", b=2)[:, :, 0:31], [('ps', 6)], [AKEY('wT')])
        for blk in range(2):
            for j in range(31):
                ts('pool', dg[:, blk, j, :], ident_b, wT[:, blk, j:j + 1], None, ALU.mult, None,
                   [AKEY('wT'), 'ident_b'], [AKEY('dg')])
        memset('pool', hglu[0][:, :, 0:32], 0.0, [('hglu', 0)])
        dma(sp_q, wg[0:16, :], w_gla_gate[l], (), [AKEY('wg')], AKEY('wg'))
        dma(sp_q, wg[16:17, :], b_gla_gate[l].rearrange("(o n) -> o n", o=1), (), [AKEY('wg')], AKEY('wg'))
        dma(sp_q, gnb, gla_norm[l].partition_broadcast(128), (), ['gnb'], 'gnb')
        ts('dve', gnb, gnb, float(np.sqrt(96.0)), None, ALU.mult, None, ['gnb'], ['gnb'])
        for p_ in range(2):
            memset('pool', lrT[p_], 1.0, [('lrT', p_)])
        memset('pool', Sst, 0.0, [AKEY('S%d' % h) for h in range(4)])
        memset('pool', vr[:, :, :, 64:65], 1.0, [AKEY('vones')])
        dma(sp_q, ext[:, 0:256], rel_bias[l][:, 1:257], (), ['ext'], 'ext')
        dma(sp_q, btmp[0:6, 512:513], rel_bias[l][:, 256:257], (), ['btmp'], 'btmp',
            allow_slow_non_contiguous=True)
        cp('dve', btmp[0:6, 0:512], btmp[0:6, 512:513].to_broadcast([6, 512]), ['btmp'], ['btmp'])
        dma(sp_q, ext[:, 256:768], btmp[0:6, 0:512], ['btmp'], ['ext'], 'ext')
        for h in range(6):
            src = bass.AP(tensor=ext.tensor, offset=ext.offset + h * 768, ap=[[1, 128], [1, 640]])
            dma(sp_q, btmp, src, ['ext'], ['btmp'], 'btmp')
            pe_mms([(psb(4)[:, 0:512], J_f, btmp[:, 0:512], True, True),
                    (psb(5)[:, 0:128], J_f, btmp[:, 512:640], True, True)], ['btmp', 'cst'],
                   [('ps', 4), ('ps', 5)])
            cp('dve', biasb[:, h, 0:512], psb(4)[:, 0:512], [('ps', 4)], [AKEY('bias')])
            cp('dve', biasb[:, h, 512:640], psb(5)[:, 0:128], [('ps', 5)], [AKEY('bias')])
        memset('pool', biasb[64:128, :, 0:64], NEG, [AKEY('bias')])
        memset('pool', biasb[0:64, :, 576:640], NEG, [AKEY('bias')])

        def load_x(i):
            dma(sp_q, otile, x[i * T:(i + 1) * T, :].rearrange("(s p) d -> p s d", p=128), (), ['otile'], 'otile')

        def stage_P(i):
            par = i % 2
            S.record()
            if l == 0:
                for s in range(2):
                    for kk in range(0, 8, 4):
                        b = kk // 4
                        pe_tr([(psb(b)[:, j * 128:(j + 1) * 128], otile[:, s, (kk + j) * 128:(kk + j + 1) * 128], ident_f)
                               for j in range(4)], ['otile', 'cst'], [('ps', b)])
                        cp('act' if kk == 0 else 'dve', hT[par][:, kk:kk + 4, s * 128:(s + 1) * 128],
                           psb(b).rearrange("p (j t) -> p j t", j=4), [('ps', b)], [('hT', par)])
                if i + 1 < ntiles:
                    load_x(i + 1)
            else:
                dma(sp_q, hT[par].rearrange("p k t -> p (k t)"), hsrc[i], [('hs', id(hsrc), i)], [('hT', par)],
                    ('hT', par))
            rms_stats(par, 2 * l, 0)
            XN = ('xnT', par)
            xn = xnT[par]
            cnt = [0]

            def nb():
                cnt[0] += 1
                return cnt[0] % 2

            def fm(cols, m, pbase=0):
                b = nb()
                pe_mms([(psb(b)[pbase:pbase + m, 0:T], win[:, k, cols:cols + m], xn[:, k, :], k == 0, k == 7)
                        for k in range(8)], [XN] + WK[hw], [('ps', b)])
                return b
            b = fm(LR, 16)
            cp('dve', lrT[par][0:16, :], psb(b)[0:16, 0:T], [('ps', b)], [('lrT', par)])
            for h in range(4):
                b = fm(GQ + 48 * h, 48)
                act(gqT[par][h], psb(b)[0:48, 0:T], AF.Copy, [('ps', b)], [('gqT', par, h)], scale=float(48 ** -0.5))
            for blk in range(2):
                ba = fm(CA + 128 * blk, 128)
                bb = fm(CB + 128 * blk, 128)
                act(sig[blk], psb(bb)[:, 0:T], AF.Exp, [('ps', bb)], [AKEY('sig%d' % blk)], scale=-1.0)
                act(sig[blk], sig[blk], AF.Ln, [AKEY('sig%d' % blk)], [AKEY('sig%d' % blk)], bias=1.0)
                act(sig[blk], sig[blk], AF.Exp, [AKEY('sig%d' % blk)], [AKEY('sig%d' % blk)], scale=-1.0)
                tt('dve', hglu[par][:, blk, 32:32 + T], psb(ba)[:, 0:T], sig[blk], ALU.mult,
                   [('ps', ba), AKEY('sig%d' % blk)], [('hglu', par)])
            if i > 0:
                cp('pool', hglu[par][:, :, 0:32], hglu[1 - par][:, :, T:T + 32], [('hglu', 1 - par)], [('hglu', par)])
            slot0 = (2 * i) % 8
            for pr in range(3):
                b = fm(AQ + 128 * pr, 128)
                act(aqT[par][:, pr, :], psb(b)[:, 0:T], AF.Copy, [('ps', b)], [('aqT', par)], scale=0.125)
            for pr in range(3):
                b = fm(AK + 128 * pr, 128)
                cp('dve', kT[:, pr, slot0 * 128:slot0 * 128 + T], psb(b)[:, 0:T], [('ps', b)],
                   [AKEY('kT%d' % slot0), AKEY('kT%d' % (slot0 + 1))])
            for s in range(2):
                g = 2 * i + s
                slot = g % 8

                def tm(cols, n):
                    b = nb()
                    pe_mms([(psb(b)[:, 0:n], xn[:, k, s * 128:(s + 1) * 128], win[:, k, cols:cols + n], k == 0, k == 7)
                            for k in range(8)], [XN] + WK[hw], [('ps', b)])
                    return b
                b = tm(GK, 192)
                cp('dve', gla_k[par][:, s, :], psb(b)[:, 0:192], [('ps', b)], [('gla_k', par, s)])
                b = tm(GV, 384)
                cp('act', gla_v[par][:, s, :], psb(b)[:, 0:384], [('ps', b)], [('gla_v', par, s)])
                b = tm(GG, 384)
                act(gsg, psb(b)[:, 0:384], AF.Exp, [('ps', b)], [AKEY('gsg')], scale=-1.0)
                act(gsg, gsg, AF.Ln, [AKEY('gsg')], [AKEY('gsg')], bias=1.0)
                act(gsg, gsg, AF.Exp, [AKEY('gsg')], [AKEY('gsg')], scale=-1.0)
                tt('dve', gla_g[par][:, s, :], psb(b)[:, 0:384], gsg, ALU.mult, [('ps', b), AKEY('gsg')],
                   [('gla_g', par, s)])
                b = tm(AV, 384)
                cp('dve', vr[:, slot, :, 0:64], psb(b)[:, 0:384].rearrange("p (h d) -> p h d", h=6), [('ps', b)],
                   [AKEY('v%d' % slot)])
            return S.stop()

        def stage_G(i):
            par = i % 2
            S.record()
            B2 = [('ps', 2)]
            for s in range(2):
                pe_mms([(psb(2)[:, 0:192], lrT[par][0:17, s * 128:(s + 1) * 128], wg[0:17, :], True, True)],
                       [('lrT', par), AKEY('wg')], B2)
                act(g_e, psb(2)[:, 0:192], AF.Exp, B2, [AKEY('g_e')], scale=-1.0)
                act(g_sp, g_e, AF.Ln, [AKEY('g_e')], [AKEY('g_sp')], bias=1.0)
                pe_mms([(psb(2)[:, 0:192], tri_f, g_sp, True, True)] +
                       [(psb(2)[0:48, 192 + 2 * h:192 + 2 * h + 2], g_sp[:, 48 * h:48 * h + 48], ind_f, True, True)
                        for h in range(4)], [AKEY('g_sp'), 'cst'], B2)
                act(g_ed, psb(2)[:, 0:192], AF.Exp, B2, [AKEY('g_ed')])
                act(dec, psb(2)[0:48, 192:200].rearrange("p (h c) -> p h c", h=4), AF.Exp, B2, [AKEY('dec')])
                tt('dve', kdec, gla_k[par][:, s, :], g_ed, ALU.mult, [('gla_k', par, s), AKEY('g_ed')], [AKEY('kdec')])
                for c in range(2):
                    sb = c
                    pe_mms([(psb(2)[0:48, 96 * h:96 * h + 96], kdec[c * 64:(c + 1) * 64, 48 * h:48 * h + 48],
                             gla_v[par][c * 64:(c + 1) * 64, s, 96 * h:96 * h + 96], True, True) for h in range(4)],
                           [AKEY('kdec'), ('gla_v', par, s)], B2)
                    for h in range(4):
                        stt(Sst[:, h, :], Sst[:, h, :], dec[:, h, c:c + 1], psb(2)[0:48, 96 * h:96 * h + 96],
                            ALU.mult, ALU.add, [AKEY('S%d' % h), AKEY('dec'), ('ps', 2)], [AKEY('S%d' % h)])
                    cp('act', Sbf[sb], Sst, [AKEY('S%d' % h) for h in range(4)], [AKEY('Sbf%d' % sb)])
                    pe_mms([(psb(2)[c * 64:(c + 1) * 64, 96 * h:96 * h + 96],
                             gqT[par][h][:, s * 128 + c * 64:s * 128 + c * 64 + 64], Sbf[sb][:, h, :], True, True)
                            for h in range(4)], [('gqT', par, h) for h in range(4)] + [AKEY('Sbf%d' % sb)], B2)
                    cp('act', go[c * 64:(c + 1) * 64, :], psb(2)[c * 64:(c + 1) * 64, 0:384], B2, ['go'])
                tt('pool', gsq, go, go, ALU.mult, ['go'], [AKEY('gsq')])

                def red(e, gms=gms, gsq=gsq):
                    return e.tensor_reduce(out=gms, in_=gsq.rearrange("p (h v) -> p h v", h=4), axis=AX.X, op=ALU.add)
                S.add('dve', red, [AKEY('gsq')], [AKEY('gms')])
                act(gms, gms, AF.Ln, [AKEY('gms')], [AKEY('gms')], bias=float(96 * EPS))
                act(gms, gms, AF.Exp, [AKEY('gms')], [AKEY('gms')], scale=-0.5)
                tt('dve', go, go, gnb, ALU.mult, ['go', 'gnb'], ['go'])
                tt('dve', go, go, gla_g[par][:, s, :], ALU.mult, ['go', ('gla_g', par, s)], ['go'])
                tt('dve', gon.rearrange("p (h v) -> p h v", h=4), go.rearrange("p (h v) -> p h v", h=4),
                   gms.unsqueeze(2).to_broadcast([128, 4, 96]), ALU.mult, ['go', AKEY('gms')], [AKEY('gon')])
                pe_tr([(psbf(2)[:, j * 128:(j + 1) * 128], gon[:, j * 128:(j + 1) * 128], ident_b) for j in range(3)],
                      [AKEY('gon'), 'ident_b'], B2)
                cp('dve', mixT[:, 0:3, s * 128:(s + 1) * 128], psbf(2)[:, 0:384].rearrange("p (j t) -> p j t", j=3),
                   B2, [('mixT', 'g', s)])
            B3 = [('ps', 2)]
            pe_mms([(psb(2)[:, blk * T:(blk + 1) * T], dg[:, blk, j, :], hglu[par][:, blk, 2 + j:2 + j + T],
                     j == 0, j == 30) for blk in range(2) for j in range(31)], [AKEY('dg'), ('hglu', par)], B3)
            for blk in range(2):
                act(cy[:, blk, :], psb(2)[:, blk * T:(blk + 1) * T], AF.Identity, B3 + [AKEY('cpar')],
                    [AKEY('cy')], bias=cpar[:, 0, blk:blk + 1])
                act(cysq[:, blk, :], psb(2)[:, blk * T:(blk + 1) * T], AF.Square, B3 + [AKEY('cpar')],
                    [AKEY('cysq')], bias=cpar[:, 0, blk:blk + 1])
            pe_mms([(psb(2)[:, 0:T], ones_f, cy[:, 0, :], True, False), (psb(2)[:, 0:T], ones_f, cy[:, 1, :], False, True),
                    (psb(2)[:, T:2 * T], ones_f, cysq[:, 0, :], True, False),
                    (psb(2)[:, T:2 * T], ones_f, cysq[:, 1, :], False, True)],
                   [AKEY('cy'), AKEY('cysq'), 'cst'], B3)
            ts('dve', cm, psb(2)[:, 0:T], 1.0 / 256.0, None, ALU.mult, None, B3, [AKEY('cm')])
            tt('dve', cmsq, cm, cm, ALU.mult, [AKEY('cm')], [AKEY('cmsq')])
            stt(cvar, psb(2)[:, T:2 * T], 1.0 / 256.0, cmsq, ALU.mult, ALU.subtract, B3 + [AKEY('cmsq')],
                [AKEY('cvar')])
            act(crs, cvar, AF.Ln, [AKEY('cvar')], [AKEY('crs')], bias=float(EPS))
            act(crs, crs, AF.Exp, [AKEY('crs')], [AKEY('crs')], scale=-0.5)
            for blk in range(2):
                tt('pool', cd[:, blk, :], cy[:, blk, :], cm, ALU.subtract, [AKEY('cy'), AKEY('cm')], [AKEY('cd')])
                tt('pool', cd[:, blk, :], cd[:, blk, :], crs, ALU.mult, [AKEY('cd'), AKEY('crs')], [AKEY('cd')])
                act(cysq[:, blk, :], cd[:, blk, :], AF.Exp, [AKEY('cd'), AKEY('ncp')], [AKEY('cysq')],
                    scale=ncp[:, 0, blk:blk + 1], bias=ncp[:, 1, blk:blk + 1])
                act(cysq[:, blk, :], cysq[:, blk, :], AF.Ln, [AKEY('cysq')], [AKEY('cysq')], bias=1.0)
                act(cysq[:, blk, :], cysq[:, blk, :], AF.Exp, [AKEY('cysq')], [AKEY('cysq')], scale=-1.0)
                ts('pool', cd[:, blk, :], cd[:, blk, :], cpar[:, 1, blk:blk + 1], cpar[:, 2, blk:blk + 1], ALU.mult,
                   ALU.add, [AKEY('cd'), AKEY('cpar')], [AKEY('cd')])
                tt('dve', mixT[:, 3 + blk, :], cd[:, blk, :], cysq[:, blk, :], ALU.mult, [AKEY('cd'), AKEY('cysq')],
                   [('mixT', 'c', blk)])
            return S.stop()

        def stage_Att(i):
            par = i % 2
            S.record()

            def scores(s, h):
                g = 2 * i + s
                nj = min(5, g + 1)
                pr, p0 = h // 2, 64 * (h % 2)
                sb = h % 2
                bA = 4 if sb == 0 else 6
                bB = bA + 1
                kB = ('ps', bB)
                cB = 0
                q_ap = aqT[par][p0:p0 + 64, pr, s * 128:(s + 1) * 128]
                lst = [(psb(bA)[:, 0:512], ident_b, biasb[:, h, 0:512], True, False)]
                for j in range(min(nj, 4)):
                    sl = (g - j) % 8
                    lst.append((psb(bA)[:, j * 128:(j + 1) * 128], kT[p0:p0 + 64, pr, sl * 128:(sl + 1) * 128],
                                q_ap, False, j == min(nj, 4) - 1))
                rd = ['ident_b', AKEY('bias'), ('aqT', par)] + [AKEY('kT%d' % ((g - j) % 8)) for j in range(nj)]
                wr = [('ps', bA)]
                if nj == 5:
                    sl = (g - 4) % 8
                    lst.append((psb(bB)[:, cB:cB + 128], ident_b, biasb[:, h, 512:640], True, False))
                    lst.append((psb(bB)[:, cB:cB + 128], kT[p0:p0 + 64, pr, sl * 128:(sl + 1) * 128], q_ap, False, True))
                    wr.append(kB)
                pe_mms(lst, rd, wr)
                na = min(nj, 4) * 128
                act(pT[sb][:, 0:na], psb(bA)[:, 0:na], AF.Exp, [('ps', bA)], [AKEY('pT%d' % sb)])
                if nj == 5:
                    act(pT[sb][:, 512:640], psb(bB)[:, cB:cB + 128], AF.Exp, [kB], [AKEY('pT%d' % sb)])

            def pv(s, h):
                g = 2 * i + s
                nj = min(5, g + 1)
                sb = h % 2
                hb3, hh = h // 3, h % 3
                pe_mms([(psb(3)[:, 65 * hh:65 * hh + 65], pT[sb][:, j * 128:(j + 1) * 128],
                         vr[:, (g - j) % 8, h, :], j == 0, j == nj - 1) for j in range(nj)],
                       [AKEY('pT%d' % sb), AKEY('vones')] + [AKEY('v%d' % ((g - j) % 8)) for j in range(nj)],
                       [('ps', 3)])
                if hh == 2:
                    o3 = psb(3)[:, 0:195].rearrange("p (h d) -> p h d", h=3)

                    def rcp(e, rden=rden, o3=o3):
                        return e.reciprocal(out=rden.unsqueeze(2), in_=o3[:, :, 64:65])
                    S.add('dve', rcp, [('ps', 3)], [AKEY('rden')])
                    tt('dve', aob[:, hb3 * 192:(hb3 + 1) * 192].rearrange("p (h d) -> p h d", h=3), o3[:, :, 0:64],
                       rden.unsqueeze(2).to_broadcast([128, 3, 64]), ALU.mult, [('ps', 3), AKEY('rden')],
                       [AKEY('aob')])
                if h == 5:
                    pe_tr([(psbf(3)[:, j * 128:(j + 1) * 128], aob[:, j * 128:(j + 1) * 128], ident_b)
                           for j in range(3)], [AKEY('aob'), 'ident_b'], [('ps', 3)])
                    cp('dve', mixT[:, 5:8, s * 128:(s + 1) * 128],
                       psbf(3)[:, 0:384].rearrange("p (j t) -> p j t", j=3), [('ps', 3)], [('mixT', 'a', s)])

            pairs = [(s, h) for s in range(2) for h in range(6)]
            scores(*pairs[0])
            for k in range(len(pairs)):
                if k + 1 < len(pairs):
                    scores(*pairs[k + 1])
                pv(*pairs[k])
            return S.stop()

        MIXK = [('mixT', 'g', 0), ('mixT', 'g', 1), ('mixT', 'c', 0), ('mixT', 'c', 1), ('mixT', 'a', 0), ('mixT', 'a', 1)]

        def stage_O(i):
            par = i % 2
            S.record()
            for m in range(8):
                b = 2 + (m % 2)
                pe_mms([(psb(b)[:, 0:T], wout[:, k, m * 128:(m + 1) * 128], mixT[:, k, :], k == 0, k == 7)
                        for k in range(8)], MIXK + WK[hw], [('ps', b)])
                tt('dve', hT[par][:, m, :], hT[par][:, m, :], psb(b)[:, 0:T], ALU.add, [('hT', par), ('ps', b)],
                   [('hT', par)])
            dma(sp_q, hdst[i], hT[par].rearrange("p k t -> p (k t)"), [('hT', par)], [('hs', id(hdst), i)],
                ('hTst', par))
            return S.stop()

        if l == 0:
            load_x(0)
        S.replay(stage_P(0))
        for i in range(ntiles):
            pn = units(stage_P(i + 1)) if i + 1 < ntiles else []
            gu = units(stage_G(i))
            au = units(stage_Att(i))
            S.replay(merge((pn, 0.3, 1.0, True), (gu, 0.0, 0.85), (au, 0.0, 0.85)))
            S.replay(stage_O(i))
        return a_keys

    def phase_B(l, half, hw, hsrc, hdst, final=False):
        wup, wdn = state['wB']
        S.fence(PA_KEYS, PB_KEYS)

        def stage_X(i):
            par = i % 2
            dma(sp_q, hT[par].rearrange("p k t -> p (k t)"), hsrc[i], [('hs', id(hsrc), i)], [('hT', par)], ('hT', par))
            if half == 1:
                dma(sp_q, xnT[par].rearrange("p k t -> p (k t)"), xs[i], [('xs', i)], [('xnT', par)], ('xnT', par))
            else:
                rms_stats(par, 2 * l + 1, 0)
                dma(sp_q, xs[i], xnT[par].rearrange("p k t -> p (k t)"), [('xnT', par)], [('xs', i)], ('xnTst', par))

        def stage_U(i):
            par = i % 2
            S.record()
            for j in range(16):
                b = j % 4
                pe_mms([(psb(b)[:, 0:T], wup[:, k, j * 128:(j + 1) * 128], xnT[par][:, k, :], k == 0, k == 7)
                        for k in range(8)], [('xnT', par)] + WK[hw], [('ps', b)])
                r = relu_t[j % 2]
                act(r, psb(b)[:, 0:T], AF.Relu, [('ps', b)], [('relu', j % 2)])
                tt('pool' if j % 2 == 0 else 'dve', hidT[par][:, j, :], r, r, ALU.mult, [('relu', j % 2)],
                   [('hid', par, j)])
            return S.stop()

        def stage_D(i):
            par = i % 2
            for m in range(8):
                b = 4 + (m % 4)
                pe_mms([(psb(b)[:, 0:T], wdn[:, j, m * 128:(m + 1) * 128], hidT[par][:, j, :], j == 0, j == 15)
                        for j in range(16)], [('hid', par, j) for j in range(16)] + WK[hw], [('ps', b)])
                tt('dve', hT[par][:, m, :], hT[par][:, m, :], psb(b)[:, 0:T], ALU.add, [('hT', par), ('ps', b)],
                   [('hT', par)])
            if not final:
                dma(sp_q, hdst[i], hT[par].rearrange("p k t -> p (k t)"), [('hT', par)], [('hs', id(hdst), i)],
                    ('hTst', par))

        def stage_F(i):
            par = i % 2
            S.record()
            rms_stats(par, 4, 4, inplace=True)
            for s in range(2):
                for kk in range(0, 8, 4):
                    b = 5 + (kk // 4)
                    pe_tr([(psb(b)[:, j * 128:(j + 1) * 128], hT[par][:, kk + j, s * 128:(s + 1) * 128], ident_f)
                           for j in range(4)], [('hT', par), 'cst'], [('ps', b)])
                    cp('act' if kk == 0 else 'dve', otile[:, s, kk * 128:(kk + 4) * 128], psb(b), [('ps', b)],
                       ['otile'])
            dma(sp_q, out[i * T:(i + 1) * T, :].rearrange("(s p) d -> p s d", p=128), otile, ['otile'], ['out'],
                'otile_st')
            return S.stop()

        stage_X(0)
        S.replay(stage_U(0))
        for i in range(ntiles):
            if i + 1 < ntiles:
                stage_X(i + 1)
            stage_D(i)
            un = units(stage_U(i + 1)) if i + 1 < ntiles else []
            fu = units(stage_F(i)) if final else []
            S.replay(merge((un, 0.0, 1.0), (fu, 0.0, 1.0)))

    def dump_dbg(hsbuf):
        dbg = nc.dram_tensor("dbg", [NT, 128, 8 * T], F32, kind="ExternalOutput").ap()
        for i in range(ntiles):
            dma(sp_q, hT[0].rearrange("p k t -> p (k t)"), hsbuf[i], [('hs', id(hsbuf), i)], [('hT', 0)], ('hT', 0))
            dma(sp_q, dbg[i], hT[0].rearrange("p k t -> p (k t)"), [('hT', 0)], ['dbg'], 'dbg')

    cur = 0
    wA = load_w_A(0, 0)
    for l in range(DEPTH):
        hw = l % 2
        state['wA'] = wA
        if l == 0:
            akeys = phase_A(l, hw, None, hs[0])
            cur = 0
        else:
            akeys = phase_A(l, hw, hs[cur], hs[1 - cur])
            cur = 1 - cur
        if stop_after == (l, 'A'):
            dump_dbg(hs[cur])
            break
        S.fence(akeys, WK[1 - hw])
        wB1 = load_w_B(l, 0, 1 - hw)
        wB2 = load_w_B(l, 1, hw)
        state['wB'] = wB1
        phase_B(l, 0, 1 - hw, hs[cur], hs[1 - cur])
        cur = 1 - cur
        if stop_after == (l, 'B1'):
            dump_dbg(hs[cur])
            break
        if l + 1 < DEPTH:
            wA = load_w_A(l + 1, 1 - hw)
        state['wB'] = wB2
        fin = (l == DEPTH - 1 and stop_after is None)
        phase_B(l, 1, hw, hs[cur], hs[1 - cur], final=fin)
        cur = 1 - cur
        if stop_after == (l, 'B2'):
            dump_dbg(hs[cur])
            break

    S.emit(nc, es)
    es.close()
    return nc


_NC_CACHE = {}


def kernel(**inputs):
    key = 'full'
    if key not in _NC_CACHE:
        _NC_CACHE[key] = build()
    nc = _NC_CACHE[key]
    cstv = make_consts()
    names = ["norm_mix", "w_in", "w_gla_gate", "b_gla_gate", "gla_norm", "w_dw", "b_dw", "conv_ln_g", "conv_ln_b",
             "rel_bias", "w_out", "norm_ffn", "w_up", "w_down", "norm_final"]
    shared = {n: np.ascontiguousarray(np.asarray(inputs[n], dtype=np.float32)) for n in names}
    xfull = np.asarray(inputs["x"], dtype=np.float32)
    in_maps = []
    for c in range(8):
        m = dict(shared)
        m["x"] = np.ascontiguousarray(xfull[c])
        m["cst"] = cstv
        in_maps.append(m)
    res = run_bass_kernel_spmd(nc, in_maps, core_ids=list(range(8)))
    return np.stack([np.asarray(r["out"], dtype=np.float32) for r in res.results], axis=0)
```

```python
import numpy as np
from contextlib import ExitStack
import concourse.bass as bass
import concourse.mybir as mybir
from concourse.bass_utils import run_bass_kernel_spmd

F32 = mybir.dt.float32
BF16 = mybir.dt.bfloat16
AF = mybir.ActivationFunctionType
ALU = mybir.AluOpType
AX = mybir.AxisListType

D = 1024
SEQ = 4096
DEPTH = 2
T = 256
NT = SEQ // T
DIN = 2832
DFF = 4096
EPS = 1e-6
GQ, GK, GV, GG, LR, CA, CB, AQ, AK, AV = 0, 192, 384, 768, 1152, 1168, 1424, 1680, 2064, 2448
NEG = -30000.0

C_ID, C_TRI, C_ONES, C_IND, C_J = 0, 128, 256, 384, 386
NCST = 514


def make_consts():
    c = np.zeros((128, NCST), np.float32)
    c[:, C_ID:C_ID + 128] = np.eye(128, dtype=np.float32)
    s = np.arange(128)[:, None]
    t = np.arange(128)[None, :]
    c[:, C_TRI:C_TRI + 128] = np.where((s > t) & (s // 64 == t // 64), -1.0 / 16.0, 0.0)
    c[:, C_ONES:C_ONES + 128] = 1.0
    c[:, C_IND:C_IND + 2] = np.where(s // 64 == np.arange(2)[None, :], -1.0 / 16.0, 0.0)
    c[:, C_J:C_J + 128] = np.eye(128, dtype=np.float32)[::-1]
    return c


class Sched:
    def __init__(self):
        self.ops = []
        self.lw = {}
        self.rd = {}
        self.rec = None

    def record(self):
        assert self.rec is None
        self.rec = []

    def stop(self):
        r = self.rec
        self.rec = None
        return r

    def replay(self, items):
        assert self.rec is None
        for it in items:
            self.add(*it)

    def add(self, eng, fn, reads=(), writes=(), dma=None):
        if self.rec is not None:
            self.rec.append((eng, fn, tuple(reads), tuple(writes), dma))
            return -1
        raw = set()
        for k in reads:
            raw |= self.lw.get(k, set())
        oth = set()
        for k in writes:
            oth |= self.lw.get(k, set())
            oth |= set(self.rd.get(k, {}).values())
        mysig = ('d', dma) if dma is not None else ('e', eng)
        for k in reads:
            if isinstance(k, tuple) and k[0] == 'ps':
                oth |= {v for sg, v in self.rd.get(k, {}).items() if sg != mysig}
        i = len(self.ops)
        self.ops.append(dict(eng=eng, fn=fn, raw=raw, oth=oth - raw, dma=dma, need=False))
        sig = ('d', dma) if dma is not None else ('e', eng)
        for k in reads:
            self.rd.setdefault(k, {})[sig] = i
        for k in writes:
            self.lw[k] = {i}
            self.rd[k] = {}
        return i

    def fence(self, src, dst):
        acc = set()
        for k in src:
            acc |= self.lw.get(k, set()) | set(self.rd.get(k, {}).values())
        for k in dst:
            self.lw[k] = self.lw.get(k, set()) | acc

    def emit(self, nc, es):
        ops = self.ops
        CAP = 30000
        for o in ops:
            deps = set()
            for j in o['raw']:
                pj = ops[j]
                if pj['dma'] is not None or pj['eng'] != o['eng'] or o['eng'] != 'pe':
                    deps.add(j)
            for j in o['oth']:
                pj = ops[j]
                if pj['dma'] is not None or pj['eng'] != o['eng'] or o['dma'] is not None:
                    deps.add(j)
            o['deps'] = deps
            for j in deps:
                ops[j]['need'] = True
        cnt = {}
        dcnt = {}
        engsems = {}
        dsems = {}

        def get_esem(e, idx):
            key = (e, idx)
            if key not in engsems:
                engsems[key] = es.enter_context(nc.semaphore("s_%s_%d" % (e, idx)))
            return engsems[key]

        for o in ops:
            if o['dma'] is not None:
                k = o['dma']
                if k not in dsems:
                    dsems[k] = es.enter_context(nc.semaphore("d_%d" % len(dsems)))
                dcnt[k] = dcnt.get(k, 0) + 16
                o['sem'] = dsems[k]
                o['val'] = dcnt[k]
                o['need'] = True
            elif o['need']:
                e = o['eng']
                c = cnt.get(e, 0)
                o['sem'] = get_esem(e, c // CAP)
                o['val'] = c % CAP + 1
                cnt[e] = c + 1
        self.nsem = len(engsems) + len(dsems)
        per = {e: [] for e in ('pe', 'act', 'dve', 'pool', 'sp')}
        for o in ops:
            per[o['eng']].append(o)

        def run(eng_obj, lst):
            waited = {}
            for o in lst:
                for j in sorted(o['deps']):
                    pj = ops[j]
                    sid = id(pj['sem'])
                    if waited.get(sid, 0) >= pj['val']:
                        continue
                    waited[sid] = pj['val']
                    eng_obj.wait_ge(pj['sem'], pj['val'])
                inst = o['fn'](eng_obj)
                if o['dma'] is not None:
                    inst.then_inc(o['sem'], 16)
                elif o['need']:
                    inst.then_inc(o['sem'], 1)
            last = {}
            for o in lst:
                if o['dma'] is not None:
                    last[id(o['sem'])] = (o['sem'], o['val'])
            for sid, (sm, v) in last.items():
                if waited.get(sid, 0) < v:
                    eng_obj.wait_ge(sm, v)

        block = es.enter_context(nc.Block())

        @block.sync
        def _(e):
            run(e, per['sp'])

        @block.gpsimd
        def _(e):
            run(e, per['pool'])

        @block.scalar
        def _(e):
            run(e, per['act'])

        @block.vector
        def _(e):
            run(e, per['dve'])

        @block.tensor
        def _(e):
            run(e, per['pe'])


def build(ntiles=NT, stop_after=None):
    nc = bass.Bass("TRN2", target_bir_lowering=False)
    es = ExitStack()
    dr = {}

    def din(name, shape):
        dr[name] = nc.dram_tensor(name, list(shape), F32, kind="ExternalInput").ap()
        return dr[name]

    x = din("x", [SEQ, D])
    norm_mix = din("norm_mix", [DEPTH, D])
    w_in = din("w_in", [DEPTH, D, DIN])
    w_gla_gate = din("w_gla_gate", [DEPTH, 16, 192])
    b_gla_gate = din("b_gla_gate", [DEPTH, 192])
    gla_norm = din("gla_norm", [DEPTH, 384])
    w_dw = din("w_dw", [DEPTH, 31, 256])
    b_dw = din("b_dw", [DEPTH, 256])
    conv_ln_g = din("conv_ln_g", [DEPTH, 256])
    conv_ln_b = din("conv_ln_b", [DEPTH, 256])
    rel_bias = din("rel_bias", [DEPTH, 6, 257])
    w_out = din("w_out", [DEPTH, D, D])
    norm_ffn = din("norm_ffn", [DEPTH, D])
    w_up = din("w_up", [DEPTH, D, DFF])
    w_down = din("w_down", [DEPTH, DFF, D])
    norm_final = din("norm_final", [D])
    cst = din("cst", [128, NCST])
    out = nc.dram_tensor("out", [SEQ, D], F32, kind="ExternalOutput").ap()
    hs = [nc.dram_tensor("hs%d" % i, [NT, 128, 8 * T], F32, kind="Internal").ap() for i in range(2)]
    xs = nc.dram_tensor("xs", [NT, 128, 8 * T], BF16, kind="Internal").ap()
    ext = nc.dram_tensor("ext", [6, 768], F32, kind="Internal").ap()

    PBYTES = 78 * 1024
    HB = 64 * 1024
    TOT = PBYTES + 2 * HB
    arena_t = es.enter_context(nc.sbuf_tensor("arena", [128, TOT // 4], F32))
    ps = [es.enter_context(nc.psum_tensor("ps%d" % b, [128, 512], F32)) for b in range(8)]

    def view(off, dtype, shape, parts=128, p0=0):
        n = int(np.prod(shape))
        esz = 4 if dtype == F32 else 2
        nb = n * esz
        assert off % 4 == 0
        a = arena_t[p0:p0 + parts, off // 4:(off + nb + 3) // 4]
        if dtype != F32:
            a = a.bitcast(dtype)
        a = a[:, 0:n]
        if len(shape) == 2:
            a = a.rearrange("p (a b) -> p a b", a=shape[0])
        elif len(shape) == 3:
            a = a.rearrange("p (a b c) -> p a b c", a=shape[0], b=shape[1])
        return a

    class Carver:
        def __init__(self, base, limit):
            self.off = base
            self.limit = limit

        def get(self, dtype, shape, parts=128, p0=0):
            n = int(np.prod(shape))
            nb = n * (4 if dtype == F32 else 2)
            nb = (nb + 31) // 32 * 32
            o = self.off
            self.off += nb
            assert self.off <= self.limit, ("arena overflow", self.off, self.limit)
            return view(o, dtype, shape, parts, p0)

    P = Carver(0, PBYTES)
    cst_sb = P.get(F32, [NCST])
    ident_f = cst_sb[:, C_ID:C_ID + 128]
    tri_f = cst_sb[:, C_TRI:C_TRI + 128]
    ones_f = cst_sb[:, C_ONES:C_ONES + 128]
    ind_f = cst_sb[:, C_IND:C_IND + 2]
    J_f = cst_sb[:, C_J:C_J + 128]
    ident_b = P.get(BF16, [128])
    ones_b = P.get(BF16, [128])
    gvec = P.get(F32, [5, 8])
    hT = [P.get(F32, [8, T]) for _ in range(2)]
    xnT = [P.get(BF16, [8, T]) for _ in range(2)]
    hsq = P.get(BF16, [8, T])
    mixT = P.get(BF16, [8, T])
    rstd_b = P.get(F32, [T])
    otile = P.get(F32, [2, D])
    ab_base = P.off
    PA = Carver(ab_base, PBYTES)
    gla_k = [PA.get(F32, [2, 192]) for _ in range(2)]
    gla_v = [PA.get(BF16, [2, 384]) for _ in range(2)]
    gla_g = [PA.get(F32, [2, 384]) for _ in range(2)]
    gqT = [[PA.get(BF16, [T], parts=48) for _ in range(4)] for _ in range(2)]
    aqT = [PA.get(BF16, [3, T]) for _ in range(2)]
    lrT = [PA.get(F32, [T], parts=32) for _ in range(2)]
    hglu = [PA.get(BF16, [2, 32 + T]) for _ in range(2)]
    btmp = PA.get(F32, [640])
    gnb = PA.get(F32, [384])
    go = PA.get(F32, [384])
    PA_KEYS = ([('gla_k', p, s_) for p in range(2) for s_ in range(2)] + [('gla_v', p, s_) for p in range(2) for s_ in range(2)]
               + [('gla_g', p, s_) for p in range(2) for s_ in range(2)] + [('gqT', p, h) for p in range(2) for h in range(4)]
               + [('aqT', p) for p in range(2)] + [('lrT', p) for p in range(2)] + [('hglu', p) for p in range(2)]
               + ['btmp', 'gnb', 'go'])
    PB = Carver(ab_base, PBYTES)
    hidT = [PB.get(BF16, [16, T]) for _ in range(2)]
    relu_t = [PB.get(F32, [T]) for _ in range(2)]
    PB_KEYS = [('hid', p, j) for p in range(2) for j in range(16)] + [('relu', n) for n in range(2)]

    S = Sched()
    sp_q = 'sp'

    def dma(q, out_ap, in_ap, reads, writes, key, **kw):
        def fn(e, out_ap=out_ap, in_ap=in_ap, kw=kw):
            return e.dma_start(out=out_ap, in_=in_ap, **kw)
        return S.add(q, fn, reads, writes, dma=key)

    def pe_mms(lst, reads, writes):
        def fn(e, lst=lst):
            ins = None
            for (o, l, r, st, sp) in lst:
                ins = e.matmul(o, lhsT=l, rhs=r, start=st, stop=sp)
            return ins
        return S.add('pe', fn, reads, writes)

    def pe_tr(lst, reads, writes):
        def fn(e, lst=lst):
            ins = None
            for (o, i, idn) in lst:
                ins = e.transpose(o, i, idn)
            return ins
        return S.add('pe', fn, reads, writes)

    def act(out_ap, in_ap, func, reads, writes, **kw):
        def fn(e, out_ap=out_ap, in_ap=in_ap, func=func, kw=kw):
            return e.activation(out=out_ap, in_=in_ap, func=func, **kw)
        return S.add('act', fn, reads, writes)

    def ts(eng, out_ap, in0, s1, s2, op0, op1, reads, writes):
        def fn(e, out_ap=out_ap, in0=in0, s1=s1, s2=s2, op0=op0, op1=op1):
            if op1 is None:
                return e.tensor_scalar(out=out_ap, in0=in0, scalar1=s1, scalar2=None, op0=op0)
            return e.tensor_scalar(out=out_ap, in0=in0, scalar1=s1, scalar2=s2, op0=op0, op1=op1)
        return S.add(eng, fn, reads, writes)

    def tt(eng, out_ap, in0, in1, op, reads, writes):
        def fn(e, out_ap=out_ap, in0=in0, in1=in1, op=op):
            return e.tensor_tensor(out=out_ap, in0=in0, in1=in1, op=op)
        return S.add(eng, fn, reads, writes)

    def stt(out_ap, in0, sc, in1, op0, op1, reads, writes):
        def fn(e, out_ap=out_ap, in0=in0, sc=sc, in1=in1, op0=op0, op1=op1):
            return e.scalar_tensor_tensor(out=out_ap, in0=in0, scalar=sc, in1=in1, op0=op0, op1=op1)
        return S.add('dve', fn, reads, writes)

    def cp(eng, out_ap, in_ap, reads, writes):
        if eng == 'act':
            return act(out_ap, in_ap, AF.Copy, reads, writes)
        def fn(e, out_ap=out_ap, in_ap=in_ap):
            return e.tensor_copy(out=out_ap, in_=in_ap)
        return S.add(eng, fn, reads, writes)

    def memset(eng, ap, val, writes):
        def fn(e, ap=ap, val=val):
            return e.memset(ap, val)
        return S.add(eng, fn, (), writes)

    def psb(b):
        return ps[b][:]

    def psbf(b):
        return ps[b][:].bitcast(BF16)

    dma(sp_q, cst_sb, cst[:, :], (), ['cst'], 'cst')
    cp('dve', ident_b, ident_f, ['cst'], ['ident_b'])
    cp('dve', ones_b, ones_f, ['cst'], ['ones_b'])
    gsrc = [norm_mix[0], norm_ffn[0], norm_mix[1], norm_ffn[1], norm_final]
    for i, g in enumerate(gsrc):
        gv_ = g.rearrange("(k p o) -> k p o", p=128, o=1)
        for k in range(8):
            dma(sp_q, gvec[:, i, k:k + 1], gv_[k], (), ['gvec'], 'gvec')
    ts('dve', gvec, gvec, 32.0, None, ALU.mult, None, ['gvec'], ['gvec'])

    WK = [[('W', 0, p) for p in range(12)], [('W', 1, p) for p in range(12)]]

    def hoff(h):
        return PBYTES + h * HB

    def load_w_A(l, h):
        base = hoff(h)
        win = view(base, BF16, [8, DIN])
        wout = view(base + 8 * DIN * 2, BF16, [8, D])
        src_in = w_in[l].rearrange("(k p) n -> p k n", p=128)
        src_out = w_out[l].rearrange("(k p) n -> p k n", p=128)
        for k in range(8):
            dma('pool', win[:, k, :], src_in[:, k, :], (), [WK[h][k]], ('W', h))
        for k in range(0, 8, 4):
            dma('pool', wout[:, k:k + 4, :], src_out[:, k:k + 4, :], (), [WK[h][8 + k // 4]], ('W', h))
        return win, wout

    def load_w_B(l, half, h):
        base = hoff(h)
        wup = view(base, BF16, [8, 2048])
        wdn = view(base + 8 * 2048 * 2, BF16, [16, D])
        src_up = w_up[l].rearrange("(k p) n -> p k n", p=128)
        src_dn = w_down[l].rearrange("(j p) n -> p j n", p=128)
        for k in range(8):
            dma('pool', wup[:, k, :], src_up[:, k, half * 2048:(half + 1) * 2048], (), [WK[h][k]], ('W', h))
        for j in range(0, 16, 4):
            dma('pool', wdn[:, j:j + 4, :], src_dn[:, half * 16 + j:half * 16 + j + 4, :], (), [WK[h][8 + j // 4]],
                ('W', h))
        return wup, wdn


    def units(lst):
        us = []
        cur = []
        for it in lst:
            if it[0] == 'pe' and any(x[0] == 'pe' for x in cur):
                us.append(cur)
                cur = []
            cur.append(it)
        if cur:
            us.append(cur)
        return us

    def merge(*streams):
        keyed = []
        for si, st in enumerate(streams):
            us, lo, hi = st[0], st[1], st[2]
            early = st[3] if len(st) > 3 else 0
            n = len(us)
            for j, u in enumerate(us):
                k = lo + (j + 0.5) / n * (hi - lo)
                if j < early:
                    k = -1.0
                keyed.append((k, si, j, u))
        keyed.sort(key=lambda t: (t[0], t[1], t[2]))
        outl = []
        for _, _, _, u in keyed:
            outl.extend(u)
        return outl

    def rms_stats(buf, gi, bank, inplace=False):
        act(hsq, hT[buf], AF.Square, [('hT', buf)], ['hsq'])
        lst = [(psb(bank)[:, 0:T], ones_b, hsq[:, k, :], k == 0, k == 7) for k in range(8)]
        pe_mms(lst, ['hsq', 'ones_b'], [('ps', bank)])
        act(rstd_b, psb(bank)[:, 0:T], AF.Ln, [('ps', bank)], ['rstd'], bias=float(D * EPS))
        act(rstd_b, rstd_b, AF.Exp, ['rstd'], ['rstd'], scale=-0.5)
        for k in range(8):
            if inplace:
                stt(hT[buf][:, k, :], hT[buf][:, k, :], gvec[:, gi, k:k + 1], rstd_b, ALU.mult, ALU.mult,
                    [('hT', buf), 'rstd', 'gvec'], [('hT', buf)])
            else:
                stt(xnT[buf][:, k, :], hT[buf][:, k, :], gvec[:, gi, k:k + 1], rstd_b, ALU.mult, ALU.mult,
                    [('hT', buf), 'rstd', 'gvec'], [('xnT', buf)])

    state = {}

    def phase_A(l, hw, hsrc, hdst):
        hb = 1 - hw
        AKEY = lambda n: ('A', l, n)
        win, wout = state['wA']
        Hc = Carver(hoff(hb), hoff(hb) + HB)
        dg = Hc.get(BF16, [2, 31, 128])
        biasb = Hc.get(BF16, [6, 640])
        kT = Hc.get(BF16, [3, 1024])
        vr = Hc.get(BF16, [8, 6, 65])
        pT = [Hc.get(BF16, [640]) for _ in range(2)]
        sig = [Hc.get(F32, [T]) for _ in range(2)]
        cy = Hc.get(F32, [2, T])
        cysq = Hc.get(F32, [2, T])
        cm = Hc.get(F32, [T])
        cmsq = Hc.get(F32, [T])
        cvar = Hc.get(F32, [T])
        crs = Hc.get(F32, [T])
        cd = Hc.get(F32, [2, T])
        cpar = Hc.get(F32, [3, 2])
        wT = Hc.get(F32, [2, 31])
        wtmp = Hc.get(F32, [256], parts=32)
        g_e = Hc.get(F32, [192])
        g_sp = Hc.get(F32, [192])
        g_ed = Hc.get(F32, [192])
        kdec = Hc.get(BF16, [192])
        Sst = Hc.get(F32, [4, 96], parts=48)
        Sbf = [Hc.get(BF16, [4, 96], parts=48) for _ in range(2)]
        dec = Hc.get(F32, [4, 2], parts=48)
        gsq = Hc.get(F32, [384])
        gms = Hc.get(F32, [4])
        gon = Hc.get(BF16, [384])
        wg = Hc.get(F32, [192], parts=32)
        rden = Hc.get(F32, [3])
        aob = Hc.get(BF16, [384])
        gsg = Hc.get(F32, [384])
        ncp = Hc.get(F32, [2, 2])
        a_keys = [AKEY(n) for n in (['dg', 'bias'] + ['kT%d' % q for q in range(8)] + ['v%d' % q for q in range(8)] +
                                    ['pT0', 'pT1', 'sig0', 'sig1', 'cy', 'cysq', 'cm', 'cmsq', 'cvar', 'crs', 'cd',
                                     'cpar', 'wT', 'wtmp', 'g_e', 'g_sp', 'g_ed', 'kdec', 'S0', 'S1', 'S2', 'S3',
                                     'Sbf0', 'Sbf1', 'dec', 'gsq', 'gms', 'gon', 'wg', 'rden', 'aob', 'vones', 'gsg', 'ncp'])]
        S.fence(WK[hb], a_keys)
        S.fence(PB_KEYS, PA_KEYS)

        for i, src in enumerate((b_dw[l], conv_ln_g[l], conv_ln_b[l])):
            sv_ = src.rearrange("(b p o) -> b p o", p=128, o=1)
            for blk in range(2):
                dma(sp_q, cpar[:, i, blk:blk + 1], sv_[blk], (), [AKEY('cpar')], AKEY('cpar'))
        ts('dve', ncp, cpar[:, 1:3, :], -1.0, None, ALU.mult, None, [AKEY('cpar')], [AKEY('ncp')])
        dma(sp_q, wtmp[0:31, :], w_dw[l], (), [AKEY('wtmp')], AKEY('wtmp'))
        for blk in range(2):
            pe_tr([(psb(6)[:, blk * 32:blk * 32 + 31], wtmp[0:31, blk * 128:(blk + 1) * 128], ident_f[0:31, 0:31])],
                  [AKEY('wtmp'), 'cst'], [('ps', 6)])
        cp('dve', wT, psb(6)[:, 0:64].rearrange("p (b j) -> p b j", b=2)[:, :, 0:31], [('ps', 6)], [AKEY('wT')])
        for blk in range(2):
            for j in range(31):
                ts('pool', dg[:, blk, j, :], ident_b, wT[:, blk, j:j + 1], None, ALU.mult, None,
                   [AKEY('wT'), 'ident_b'], [AKEY('dg')])
        memset('pool', hglu[0][:, :, 0:32], 0.0, [('hglu', 0)])
        dma(sp_q, wg[0:16, :], w_gla_gate[l], (), [AKEY('wg')], AKEY('wg'))
        dma(sp_q, wg[16:17, :], b_gla_gate[l].rearrange("(o n) -> o n", o=1), (), [AKEY('wg')], AKEY('wg'))
        dma(sp_q, gnb, gla_norm[l].partition_broadcast(128), (), ['gnb'], 'gnb')
        ts('dve', gnb, gnb, float(np.sqrt(96.0)), None, ALU.mult, None, ['gnb'], ['gnb'])
        for p_ in range(2):
            memset('pool', lrT[p_], 1.0, [('lrT', p_)])
        memset('pool', Sst, 0.0, [AKEY('S%d' % h) for h in range(4)])
        memset('pool', vr[:, :, :, 64:65], 1.0, [AKEY('vones')])
        dma(sp_q, ext[:, 0:256], rel_bias[l][:, 1:257], (), ['ext'], 'ext')
        dma(sp_q, btmp[0:6, 512:513], rel_bias[l][:, 256:257], (), ['btmp'], 'btmp',
            allow_slow_non_contiguous=True)
        cp('dve', btmp[0:6, 0:512], btmp[0:6, 512:513].to_broadcast([6, 512]), ['btmp'], ['btmp'])
        dma(sp_q, ext[:, 256:768], btmp[0:6, 0:512], ['btmp'], ['ext'], 'ext')
        for h in range(6):
            src = bass.AP(tensor=ext.tensor, offset=ext.offset + h * 768, ap=[[1, 128], [1, 640]])
            dma(sp_q, btmp, src, ['ext'], ['btmp'], 'btmp')
            pe_mms([(psb(4)[:, 0:512], J_f, btmp[:, 0:512], True, True),
                    (psb(5)[:, 0:128], J_f, btmp[:, 512:640], True, True)], ['btmp', 'cst'],
                   [('ps', 4), ('ps', 5)])
            cp('dve', biasb[:, h, 0:512], psb(4)[:, 0:512], [('ps', 4)], [AKEY('bias')])
            cp('dve', biasb[:, h, 512:640], psb(5)[:, 0:128], [('ps', 5)], [AKEY('bias')])
        memset('pool', biasb[64:128, :, 0:64], NEG, [AKEY('bias')])
        memset('pool', biasb[0:64, :, 576:640], NEG, [AKEY('bias')])

        def load_x(i):
            dma(sp_q, otile, x[i * T:(i + 1) * T, :].rearrange("(s p) d -> p s d", p=128), (), ['otile'], 'otile')

        def stage_P(i):
            par = i % 2
            S.record()
            if l == 0:
                for s in range(2):
                    for kk in range(0, 8, 4):
                        b = kk // 4
                        pe_tr([(psb(b)[:, j * 128:(j + 1) * 128], otile[:, s, (kk + j) * 128:(kk + j + 1) * 128], ident_f)
                               for j in range(4)], ['otile', 'cst'], [('ps', b)])
                        cp('act' if kk == 0 else 'dve', hT[par][:, kk:kk + 4, s * 128:(s + 1) * 128],
                           psb(b).rearrange("p (j t) -> p j t", j=4), [('ps', b)], [('hT', par)])
                if i + 1 < ntiles:
                    load_x(i + 1)
            else:
                dma(sp_q, hT[par].rearrange("p k t -> p (k t)"), hsrc[i], [('hs', id(hsrc), i)], [('hT', par)],
                    ('hT', par))
            rms_stats(par, 2 * l, 0)
            XN = ('xnT', par)
            xn = xnT[par]
            cnt = [0]

            def nb():
                cnt[0] += 1
                return cnt[0] % 2

            def fm(cols, m, pbase=0):
                b = nb()
                pe_mms([(psb(b)[pbase:pbase + m, 0:T], win[:, k, cols:cols + m], xn[:, k, :], k == 0, k == 7)
                        for k in range(8)], [XN] + WK[hw], [('ps', b)])
                return b
            b = fm(LR, 128)
            cp('dve', lrT[par][0:16, :], psb(b)[0:16, 0:T], [('ps', b)], [('lrT', par)])
            for h in range(4):
                b = fm(GQ + 48 * h, 48)
                ts('dve', gqT[par][h], psb(b)[0:48, 0:T], float(48 ** -0.5), None, ALU.mult, None, [('ps', b)],
                   [('gqT', par, h)])
            for blk in range(2):
                bb = fm(CB + 128 * blk, 128)
                act(sig[blk], psb(bb)[:, 0:T], AF.Exp, [('ps', bb)], [AKEY('sig%d' % blk)], scale=-1.0)
                act(sig[blk], sig[blk], AF.Ln, [AKEY('sig%d' % blk)], [AKEY('sig%d' % blk)], bias=1.0)
                act(sig[blk], sig[blk], AF.Exp, [AKEY('sig%d' % blk)], [AKEY('sig%d' % blk)], scale=-1.0)
            slot0 = (2 * i) % 8
            for pr in range(3):
                b = fm(AQ + 128 * pr, 128)
                ts('dve', aqT[par][:, pr, :], psb(b)[:, 0:T], 0.125, None, ALU.mult, None, [('ps', b)], [('aqT', par)])
            for pr in range(3):
                b = fm(AK + 128 * pr, 128)
                cp('dve', kT[:, pr, slot0 * 128:slot0 * 128 + T], psb(b)[:, 0:T], [('ps', b)],
                   [AKEY('kT%d' % slot0), AKEY('kT%d' % (slot0 + 1))])
            for blk in range(2):
                ba = fm(CA + 128 * blk, 128)
                tt('dve', hglu[par][:, blk, 32:32 + T], psb(ba)[:, 0:T], sig[blk], ALU.mult,
                   [('ps', ba), AKEY('sig%d' % blk)], [('hglu', par)])
            if i > 0:
                cp('pool', hglu[par][:, :, 0:32], hglu[1 - par][:, :, T:T + 32], [('hglu', 1 - par)], [('hglu', par)])
            for s in range(2):
                g = 2 * i + s
                slot = g % 8

                def tm(cols, n):
                    b = nb()
                    pe_mms([(psb(b)[:, 0:n], xn[:, k, s * 128:(s + 1) * 128], win[:, k, cols:cols + n], k == 0, k == 7)
                            for k in range(8)], [XN] + WK[hw], [('ps', b)])
                    return b
                b = tm(GG, 384)
                cp('dve', gla_g[par][:, s, :], psb(b)[:, 0:384], [('ps', b)], [('gla_g', par, s)])
                act(gsg, gla_g[par][:, s, :], AF.Exp, [('gla_g', par, s)], [AKEY('gsg')], scale=-1.0)
                act(gsg, gsg, AF.Ln, [AKEY('gsg')], [AKEY('gsg')], bias=1.0)
                act(gsg, gsg, AF.Exp, [AKEY('gsg')], [AKEY('gsg')], scale=-1.0)
                tt('pool', gla_g[par][:, s, :], gla_g[par][:, s, :], gsg, ALU.mult, [('gla_g', par, s), AKEY('gsg')],
                   [('gla_g', par, s)])
                b = tm(GK, 192)
                cp('dve', gla_k[par][:, s, :], psb(b)[:, 0:192], [('ps', b)], [('gla_k', par, s)])
                b = tm(GV, 384)
                cp('act', gla_v[par][:, s, :], psb(b)[:, 0:384], [('ps', b)], [('gla_v', par, s)])
                b = tm(AV, 384)
                cp('dve', vr[:, slot, :, 0:64], psb(b)[:, 0:384].rearrange("p (h d) -> p h d", h=6), [('ps', b)],
                   [AKEY('v%d' % slot)])
            return S.stop()

        def stage_G(i):
            par = i % 2
            S.record()
            B2 = [('ps', 2)]
            for s in range(2):
                pe_mms([(psb(2)[:, 0:192], lrT[par][0:17, s * 128:(s + 1) * 128], wg[0:17, :], True, True)],
                       [('lrT', par), AKEY('wg')], B2)
                act(g_e, psb(2)[:, 0:192], AF.Exp, B2, [AKEY('g_e')], scale=-1.0)
                act(g_sp, g_e, AF.Ln, [AKEY('g_e')], [AKEY('g_sp')], bias=1.0)
                pe_mms([(psb(2)[:, 0:192], tri_f, g_sp, True, True)] +
                       [(psb(2)[0:48, 192 + 2 * h:192 + 2 * h + 2], g_sp[:, 48 * h:48 * h + 48], ind_f, True, True)
                        for h in range(4)], [AKEY('g_sp'), 'cst'], B2)
                act(g_ed, psb(2)[:, 0:192], AF.Exp, B2, [AKEY('g_ed')])
                act(dec, psb(2)[0:48, 192:200].rearrange("p (h c) -> p h c", h=4), AF.Exp, B2, [AKEY('dec')])
                tt('dve', kdec, gla_k[par][:, s, :], g_ed, ALU.mult, [('gla_k', par, s), AKEY('g_ed')], [AKEY('kdec')])
                for c in range(2):
                    sb = c
                    pe_mms([(psb(2)[0:48, 96 * h:96 * h + 96], kdec[c * 64:(c + 1) * 64, 48 * h:48 * h + 48],
                             gla_v[par][c * 64:(c + 1) * 64, s, 96 * h:96 * h + 96], True, True) for h in range(4)],
                           [AKEY('kdec'), ('gla_v', par, s)], B2)
                    for h in range(4):
                        stt(Sst[:, h, :], Sst[:, h, :], dec[:, h, c:c + 1], psb(2)[0:48, 96 * h:96 * h + 96],
                            ALU.mult, ALU.add, [AKEY('S%d' % h), AKEY('dec'), ('ps', 2)], [AKEY('S%d' % h)])
                    cp('act', Sbf[sb], Sst, [AKEY('S%d' % h) for h in range(4)], [AKEY('Sbf%d' % sb)])
                    pe_mms([(psb(2)[c * 64:(c + 1) * 64, 96 * h:96 * h + 96],
                             gqT[par][h][:, s * 128 + c * 64:s * 128 + c * 64 + 64], Sbf[sb][:, h, :], True, True)
                            for h in range(4)], [('gqT', par, h) for h in range(4)] + [AKEY('Sbf%d' % sb)], B2)
                    cp('act', go[c * 64:(c + 1) * 64, :], psb(2)[c * 64:(c + 1) * 64, 0:384], B2, ['go'])
                tt('pool', gsq, go, go, ALU.mult, ['go'], [AKEY('gsq')])

                def red(e, gms=gms, gsq=gsq):
                    return e.tensor_reduce(out=gms, in_=gsq.rearrange("p (h v) -> p h v", h=4), axis=AX.X, op=ALU.add)
                S.add('dve', red, [AKEY('gsq')], [AKEY('gms')])
                act(gms, gms, AF.Ln, [AKEY('gms')], [AKEY('gms')], bias=float(96 * EPS))
                act(gms, gms, AF.Exp, [AKEY('gms')], [AKEY('gms')], scale=-0.5)
                tt('pool', go, go, gnb, ALU.mult, ['go', 'gnb'], ['go'])
                tt('pool', go, go, gla_g[par][:, s, :], ALU.mult, ['go', ('gla_g', par, s)], ['go'])
                tt('dve', gon.rearrange("p (h v) -> p h v", h=4), go.rearrange("p (h v) -> p h v", h=4),
                   gms.unsqueeze(2).to_broadcast([128, 4, 96]), ALU.mult, ['go', AKEY('gms')], [AKEY('gon')])
                pe_tr([(psbf(2)[:, j * 128:(j + 1) * 128], gon[:, j * 128:(j + 1) * 128], ident_b) for j in range(3)],
                      [AKEY('gon'), 'ident_b'], B2)
                cp('dve', mixT[:, 0:3, s * 128:(s + 1) * 128], psbf(2)[:, 0:384].rearrange("p (j t) -> p j t", j=3),
                   B2, [('mixT', 'g', s)])
            B3 = [('ps', 2)]
            pe_mms([(psb(2)[:, blk * T:(blk + 1) * T], dg[:, blk, j, :], hglu[par][:, blk, 2 + j:2 + j + T],
                     j == 0, j == 30) for blk in range(2) for j in range(31)], [AKEY('dg'), ('hglu', par)], B3)
            for blk in range(2):
                act(cy[:, blk, :], psb(2)[:, blk * T:(blk + 1) * T], AF.Identity, B3 + [AKEY('cpar')],
                    [AKEY('cy')], bias=cpar[:, 0, blk:blk + 1])
                act(cysq[:, blk, :], psb(2)[:, blk * T:(blk + 1) * T], AF.Square, B3 + [AKEY('cpar')],
                    [AKEY('cysq')], bias=cpar[:, 0, blk:blk + 1])
            pe_mms([(psb(2)[:, 0:T], ones_f, cy[:, 0, :], True, False), (psb(2)[:, 0:T], ones_f, cy[:, 1, :], False, True),
                    (psb(2)[:, T:2 * T], ones_f, cysq[:, 0, :], True, False),
                    (psb(2)[:, T:2 * T], ones_f, cysq[:, 1, :], False, True)],
                   [AKEY('cy'), AKEY('cysq'), 'cst'], B3)
            ts('dve', cm, psb(2)[:, 0:T], 1.0 / 256.0, None, ALU.mult, None, B3, [AKEY('cm')])
            tt('dve', cmsq, cm, cm, ALU.mult, [AKEY('cm')], [AKEY('cmsq')])
            stt(cvar, psb(2)[:, T:2 * T], 1.0 / 256.0, cmsq, ALU.mult, ALU.subtract, B3 + [AKEY('cmsq')],
                [AKEY('cvar')])
            act(crs, cvar, AF.Ln, [AKEY('cvar')], [AKEY('crs')], bias=float(EPS))
            act(crs, crs, AF.Exp, [AKEY('crs')], [AKEY('crs')], scale=-0.5)
            for blk in range(2):
                tt('pool', cd[:, blk, :], cy[:, blk, :], cm, ALU.subtract, [AKEY('cy'), AKEY('cm')], [AKEY('cd')])
                tt('pool', cd[:, blk, :], cd[:, blk, :], crs, ALU.mult, [AKEY('cd'), AKEY('crs')], [AKEY('cd')])
                act(cysq[:, blk, :], cd[:, blk, :], AF.Exp, [AKEY('cd'), AKEY('ncp')], [AKEY('cysq')],
                    scale=ncp[:, 0, blk:blk + 1], bias=ncp[:, 1, blk:blk + 1])
                act(cysq[:, blk, :], cysq[:, blk, :], AF.Ln, [AKEY('cysq')], [AKEY('cysq')], bias=1.0)
                act(cysq[:, blk, :], cysq[:, blk, :], AF.Exp, [AKEY('cysq')], [AKEY('cysq')], scale=-1.0)
                ts('pool', cd[:, blk, :], cd[:, blk, :], cpar[:, 1, blk:blk + 1], cpar[:, 2, blk:blk + 1], ALU.mult,
                   ALU.add, [AKEY('cd'), AKEY('cpar')], [AKEY('cd')])
                tt('dve', mixT[:, 3 + blk, :], cd[:, blk, :], cysq[:, blk, :], ALU.mult, [AKEY('cd'), AKEY('cysq')],
                   [('mixT', 'c', blk)])
            return S.stop()

        def stage_Att(i):
            par = i % 2
            S.record()

            def scores(s, h):
                g = 2 * i + s
                nj = min(5, g + 1)
                pr, p0 = h // 2, 64 * (h % 2)
                sb = h % 2
                bA = 4 if sb == 0 else 6
                bB = bA + 1
                kB = ('ps', bB)
                cB = 0
                q_ap = aqT[par][p0:p0 + 64, pr, s * 128:(s + 1) * 128]
                lst = [(psb(bA)[:, 0:512], ident_b, biasb[:, h, 0:512], True, False)]
                for j in range(min(nj, 4)):
                    sl = (g - j) % 8
                    lst.append((psb(bA)[:, j * 128:(j + 1) * 128], kT[p0:p0 + 64, pr, sl * 128:(sl + 1) * 128],
                                q_ap, False, j == min(nj, 4) - 1))
                rd = ['ident_b', AKEY('bias'), ('aqT', par)] + [AKEY('kT%d' % ((g - j) % 8)) for j in range(nj)]
                wr = [('ps', bA)]
                if nj == 5:
                    sl = (g - 4) % 8
                    lst.append((psb(bB)[:, cB:cB + 128], ident_b, biasb[:, h, 512:640], True, False))
                    lst.append((psb(bB)[:, cB:cB + 128], kT[p0:p0 + 64, pr, sl * 128:(sl + 1) * 128], q_ap, False, True))
                    wr.append(kB)
                pe_mms(lst, rd, wr)
                na = min(nj, 4) * 128
                act(pT[sb][:, 0:na], psb(bA)[:, 0:na], AF.Exp, [('ps', bA)], [AKEY('pT%d' % sb)])
                if nj == 5:
                    act(pT[sb][:, 512:640], psb(bB)[:, cB:cB + 128], AF.Exp, [kB], [AKEY('pT%d' % sb)])

            def pv(s, h):
                g = 2 * i + s
                nj = min(5, g + 1)
                sb = h % 2
                hb3, hh = h // 3, h % 3
                pe_mms([(psb(3)[:, 65 * hh:65 * hh + 65], pT[sb][:, j * 128:(j + 1) * 128],
                         vr[:, (g - j) % 8, h, :], j == 0, j == nj - 1) for j in range(nj)],
                       [AKEY('pT%d' % sb), AKEY('vones')] + [AKEY('v%d' % ((g - j) % 8)) for j in range(nj)],
                       [('ps', 3)])
                if hh == 2:
                    o3 = psb(3)[:, 0:195].rearrange("p (h d) -> p h d", h=3)

                    def rcp(e, rden=rden, o3=o3):
                        return e.reciprocal(out=rden.unsqueeze(2), in_=o3[:, :, 64:65])
                    S.add('dve', rcp, [('ps', 3)], [AKEY('rden')])
                    tt('dve', aob[:, hb3 * 192:(hb3 + 1) * 192].rearrange("p (h d) -> p h d", h=3), o3[:, :, 0:64],
                       rden.unsqueeze(2).to_broadcast([128, 3, 64]), ALU.mult, [('ps', 3), AKEY('rden')],
                       [AKEY('aob')])
                if h == 5:
                    pe_tr([(psbf(3)[:, j * 128:(j + 1) * 128], aob[:, j * 128:(j + 1) * 128], ident_b)
                           for j in range(3)], [AKEY('aob'), 'ident_b'], [('ps', 3)])
                    cp('dve', mixT[:, 5:8, s * 128:(s + 1) * 128],
                       psbf(3)[:, 0:384].rearrange("p (j t) -> p j t", j=3), [('ps', 3)], [('mixT', 'a', s)])

            pairs = [(s, h) for s in range(2) for h in range(6)]
            scores(*pairs[0])
            for k in range(len(pairs)):
                if k + 1 < len(pairs):
                    scores(*pairs[k + 1])
                pv(*pairs[k])
            return S.stop()

        MIXK = [('mixT', 'g', 0), ('mixT', 'g', 1), ('mixT', 'c', 0), ('mixT', 'c', 1), ('mixT', 'a', 0), ('mixT', 'a', 1)]

        def stage_O(i):
            par = i % 2
            S.record()
            for m in range(8):
                b = 2 + (m % 2)
                pe_mms([(psb(b)[:, 0:T], wout[:, k, m * 128:(m + 1) * 128], mixT[:, k, :], k == 0, k == 7)
                        for k in range(8)], MIXK + WK[hw], [('ps', b)])
                tt('dve', hT[par][:, m, :], hT[par][:, m, :], psb(b)[:, 0:T], ALU.add, [('hT', par), ('ps', b)],
                   [('hT', par)])
            dma(sp_q, hdst[i], hT[par].rearrange("p k t -> p (k t)"), [('hT', par)], [('hs', id(hdst), i)],
                ('hTst', par))
            return S.stop()

        if l == 0:
            load_x(0)
        S.replay(stage_P(0))
        for i in range(ntiles):
            pn = units(stage_P(i + 1)) if i + 1 < ntiles else []
            gu = units(stage_G(i))
            au = units(stage_Att(i))
            S.replay(merge((pn, 0.3, 1.0, 5 if l == 0 else 1), (gu, 0.0, 0.85), (au, 0.0, 0.85)))
            S.replay(stage_O(i))
        return a_keys

    def phase_B(l, half, hw, hsrc, hdst, final=False):
        wup, wdn = state['wB']
        S.fence(PA_KEYS, PB_KEYS)

        def stage_X(i):
            par = i % 2
            dma(sp_q, hT[par].rearrange("p k t -> p (k t)"), hsrc[i], [('hs', id(hsrc), i)], [('hT', par)], ('hT', par))
            if half == 1:
                dma(sp_q, xnT[par].rearrange("p k t -> p (k t)"), xs[i], [('xs', i)], [('xnT', par)], ('xnT', par))
            else:
                rms_stats(par, 2 * l + 1, 0)
                dma(sp_q, xs[i], xnT[par].rearrange("p k t -> p (k t)"), [('xnT', par)], [('xs', i)], ('xnTst', par))

        def stage_U(i):
            par = i % 2
            S.record()
            for j in range(16):
                b = j % 4
                pe_mms([(psb(b)[:, 0:T], wup[:, k, j * 128:(j + 1) * 128], xnT[par][:, k, :], k == 0, k == 7)
                        for k in range(8)], [('xnT', par)] + WK[hw], [('ps', b)])
                r = relu_t[j % 2]
                act(r, psb(b)[:, 0:T], AF.Relu, [('ps', b)], [('relu', j % 2)])
                tt('pool' if j % 2 == 0 else 'dve', hidT[par][:, j, :], r, r, ALU.mult, [('relu', j % 2)],
                   [('hid', par, j)])
            return S.stop()

        def stage_D(i):
            par = i % 2
            for m in range(8):
                b = 4 + (m % 4)
                pe_mms([(psb(b)[:, 0:T], wdn[:, j, m * 128:(m + 1) * 128], hidT[par][:, j, :], j == 0, j == 15)
                        for j in range(16)], [('hid', par, j) for j in range(16)] + WK[hw], [('ps', b)])
                tt('dve', hT[par][:, m, :], hT[par][:, m, :], psb(b)[:, 0:T], ALU.add, [('hT', par), ('ps', b)],
                   [('hT', par)])
            if not final:
                dma(sp_q, hdst[i], hT[par].rearrange("p k t -> p (k t)"), [('hT', par)], [('hs', id(hdst), i)],
                    ('hTst', par))

        def stage_F(i):
            par = i % 2
            S.record()
            rms_stats(par, 4, 4, inplace=True)
            for s in range(2):
                for kk in range(0, 8, 4):
                    b = 5 + (kk // 4)
                    pe_tr([(psb(b)[:, j * 128:(j + 1) * 128], hT[par][:, kk + j, s * 128:(s + 1) * 128], ident_f)
                           for j in range(4)], [('hT', par), 'cst'], [('ps', b)])
                    cp('act' if kk == 0 else 'dve', otile[:, s, kk * 128:(kk + 4) * 128], psb(b), [('ps', b)],
                       ['otile'])
            dma(sp_q, out[i * T:(i + 1) * T, :].rearrange("(s p) d -> p s d", p=128), otile, ['otile'], ['out'],
                'otile_st')
            return S.stop()

        stage_X(0)
        S.replay(stage_U(0))
        for i in range(ntiles):
            if i + 1 < ntiles:
                stage_X(i + 1)
            stage_D(i)
            un = units(stage_U(i + 1)) if i + 1 < ntiles else []
            fu = units(stage_F(i)) if final else []
            S.replay(merge((un, 0.0, 1.0), (fu, 0.0, 1.0)))

    def dump_dbg(hsbuf):
        dbg = nc.dram_tensor("dbg", [NT, 128, 8 * T], F32, kind="ExternalOutput").ap()
        for i in range(ntiles):
            dma(sp_q, hT[0].rearrange("p k t -> p (k t)"), hsbuf[i], [('hs', id(hsbuf), i)], [('hT', 0)], ('hT', 0))
            dma(sp_q, dbg[i], hT[0].rearrange("p k t -> p (k t)"), [('hT', 0)], ['dbg'], 'dbg')

    cur = 0
    wA = load_w_A(0, 0)
    for l in range(DEPTH):
        hw = l % 2
        state['wA'] = wA
        if l == 0:
            akeys = phase_A(l, hw, None, hs[0])
            cur = 0
        else:
            akeys = phase_A(l, hw, hs[cur], hs[1 - cur])
            cur = 1 - cur
        if stop_after == (l, 'A'):
            dump_dbg(hs[cur])
            break
        S.fence(akeys, WK[1 - hw])
        wB1 = load_w_B(l, 0, 1 - hw)
        wB2 = load_w_B(l, 1, hw)
        state['wB'] = wB1
        phase_B(l, 0, 1 - hw, hs[cur], hs[1 - cur])
        cur = 1 - cur
        if stop_after == (l, 'B1'):
            dump_dbg(hs[cur])
            break
        if l + 1 < DEPTH:
            wA = load_w_A(l + 1, 1 - hw)
        state['wB'] = wB2
        fin = (l == DEPTH - 1 and stop_after is None)
        phase_B(l, 1, hw, hs[cur], hs[1 - cur], final=fin)
        cur = 1 - cur
        if stop_after == (l, 'B2'):
            dump_dbg(hs[cur])
            break

    S.emit(nc, es)
    es.close()
    return nc


_NC_CACHE = {}


def kernel(**inputs):
    key = 'full'
    if key not in _NC_CACHE:
        _NC_CACHE[key] = build()
    nc = _NC_CACHE[key]
    cstv = make_consts()
    names = ["norm_mix", "w_in", "w_gla_gate", "b_gla_gate", "gla_norm", "w_dw", "b_dw", "conv_ln_g", "conv_ln_b",
             "rel_bias", "w_out", "norm_ffn", "w_up", "w_down", "norm_final"]
    shared = {n: np.ascontiguousarray(np.asarray(inputs[n], dtype=np.float32)) for n in names}
    xfull = np.asarray(inputs["x"], dtype=np.float32)
    in_maps = []
    for c in range(8):
        m = dict(shared)
        m["x"] = np.ascontiguousarray(xfull[c])
        m["cst"] = cstv
        in_maps.append(m)
    res = run_bass_kernel_spmd(nc, in_maps, core_ids=list(range(8)))
    return np.stack([np.asarray(r["out"], dtype=np.float32) for r in res.results], axis=0)
```

```python
import numpy as np
from contextlib import ExitStack
import concourse.bass as bass
import concourse.mybir as mybir
from concourse.bass_utils import run_bass_kernel_spmd

F32 = mybir.dt.float32
BF16 = mybir.dt.bfloat16
AF = mybir.ActivationFunctionType
ALU = mybir.AluOpType
AX = mybir.AxisListType

D = 1024
SEQ = 4096
DEPTH = 2
T = 256
NT = SEQ // T
DIN = 2832
DFF = 4096
EPS = 1e-6
GQ, GK, GV, GG, LR, CA, CB, AQ, AK, AV = 0, 192, 384, 768, 1152, 1168, 1424, 1680, 2064, 2448
NEG = -30000.0

C_ID, C_TRI, C_ONES, C_IND, C_J = 0, 128, 256, 384, 386
NCST = 514


def make_consts():
    c = np.zeros((128, NCST), np.float32)
    c[:, C_ID:C_ID + 128] = np.eye(128, dtype=np.float32)
    s = np.arange(128)[:, None]
    t = np.arange(128)[None, :]
    c[:, C_TRI:C_TRI + 128] = np.where((s > t) & (s // 64 == t // 64), -1.0 / 16.0, 0.0)
    c[:, C_ONES:C_ONES + 128] = 1.0
    c[:, C_IND:C_IND + 2] = np.where(s // 64 == np.arange(2)[None, :], -1.0 / 16.0, 0.0)
    c[:, C_J:C_J + 128] = np.eye(128, dtype=np.float32)[::-1]
    return c


class Sched:
    def __init__(self):
        self.ops = []
        self.lw = {}
        self.rd = {}
        self.rec = None

    def record(self):
        assert self.rec is None
        self.rec = []

    def stop(self):
        r = self.rec
        self.rec = None
        return r

    def replay(self, items):
        assert self.rec is None
        for it in items:
            self.add(*it)

    def add(self, eng, fn, reads=(), writes=(), dma=None):
        if self.rec is not None:
            self.rec.append((eng, fn, tuple(reads), tuple(writes), dma))
            return -1
        raw = set()
        for k in reads:
            raw |= self.lw.get(k, set())
        oth = set()
        for k in writes:
            oth |= self.lw.get(k, set())
            oth |= set(self.rd.get(k, {}).values())
        mysig = ('d', dma) if dma is not None else ('e', eng)
        for k in reads:
            if isinstance(k, tuple) and k[0] == 'ps':
                oth |= {v for sg, v in self.rd.get(k, {}).items() if sg != mysig}
        i = len(self.ops)
        self.ops.append(dict(eng=eng, fn=fn, raw=raw, oth=oth - raw, dma=dma, need=False))
        sig = ('d', dma) if dma is not None else ('e', eng)
        for k in reads:
            self.rd.setdefault(k, {})[sig] = i
        for k in writes:
            self.lw[k] = {i}
            self.rd[k] = {}
        return i

    def fence(self, src, dst):
        acc = set()
        for k in src:
            acc |= self.lw.get(k, set()) | set(self.rd.get(k, {}).values())
        for k in dst:
            self.lw[k] = self.lw.get(k, set()) | acc

    def emit(self, nc, es):
        ops = self.ops
        CAP = 30000
        for o in ops:
            deps = set()
            for j in o['raw']:
                pj = ops[j]
                if pj['dma'] is not None or pj['eng'] != o['eng'] or o['eng'] != 'pe':
                    deps.add(j)
            for j in o['oth']:
                pj = ops[j]
                if pj['dma'] is not None or pj['eng'] != o['eng'] or o['dma'] is not None:
                    deps.add(j)
            o['deps'] = deps
            for j in deps:
                ops[j]['need'] = True
        cnt = {}
        dcnt = {}
        engsems = {}
        dsems = {}

        def get_esem(e, idx):
            key = (e, idx)
            if key not in engsems:
                engsems[key] = es.enter_context(nc.semaphore("s_%s_%d" % (e, idx)))
            return engsems[key]

        for o in ops:
            if o['dma'] is not None:
                k = o['dma']
                if k not in dsems:
                    dsems[k] = es.enter_context(nc.semaphore("d_%d" % len(dsems)))
                dcnt[k] = dcnt.get(k, 0) + 16
                o['sem'] = dsems[k]
                o['val'] = dcnt[k]
                o['need'] = True
            elif o['need']:
                e = o['eng']
                c = cnt.get(e, 0)
                o['sem'] = get_esem(e, c // CAP)
                o['val'] = c % CAP + 1
                cnt[e] = c + 1
        self.nsem = len(engsems) + len(dsems)
        per = {e: [] for e in ('pe', 'act', 'dve', 'pool', 'sp')}
        for o in ops:
            per[o['eng']].append(o)

        def run(eng_obj, lst):
            waited = {}
            for o in lst:
                for j in sorted(o['deps']):
                    pj = ops[j]
                    sid = id(pj['sem'])
                    if waited.get(sid, 0) >= pj['val']:
                        continue
                    waited[sid] = pj['val']
                    eng_obj.wait_ge(pj['sem'], pj['val'])
                inst = o['fn'](eng_obj)
                if o['dma'] is not None:
                    inst.then_inc(o['sem'], 16)
                elif o['need']:
                    inst.then_inc(o['sem'], 1)
            last = {}
            for o in lst:
                if o['dma'] is not None:
                    last[id(o['sem'])] = (o['sem'], o['val'])
            for sid, (sm, v) in last.items():
                if waited.get(sid, 0) < v:
                    eng_obj.wait_ge(sm, v)

        block = es.enter_context(nc.Block())

        @block.sync
        def _(e):
            run(e, per['sp'])

        @block.gpsimd
        def _(e):
            run(e, per['pool'])

        @block.scalar
        def _(e):
            run(e, per['act'])

        @block.vector
        def _(e):
            run(e, per['dve'])

        @block.tensor
        def _(e):
            run(e, per['pe'])


def build(ntiles=NT, stop_after=None):
    nc = bass.Bass("TRN2", target_bir_lowering=False)
    es = ExitStack()
    dr = {}

    def din(name, shape):
        dr[name] = nc.dram_tensor(name, list(shape), F32, kind="ExternalInput").ap()
        return dr[name]

    x = din("x", [SEQ, D])
    norm_mix = din("norm_mix", [DEPTH, D])
    w_in = din("w_in", [DEPTH, D, DIN])
    w_gla_gate = din("w_gla_gate", [DEPTH, 16, 192])
    b_gla_gate = din("b_gla_gate", [DEPTH, 192])
    gla_norm = din("gla_norm", [DEPTH, 384])
    w_dw = din("w_dw", [DEPTH, 31, 256])
    b_dw = din("b_dw", [DEPTH, 256])
    conv_ln_g = din("conv_ln_g", [DEPTH, 256])
    conv_ln_b = din("conv_ln_b", [DEPTH, 256])
    rel_bias = din("rel_bias", [DEPTH, 6, 257])
    w_out = din("w_out", [DEPTH, D, D])
    norm_ffn = din("norm_ffn", [DEPTH, D])
    w_up = din("w_up", [DEPTH, D, DFF])
    w_down = din("w_down", [DEPTH, DFF, D])
    norm_final = din("norm_final", [D])
    cst = din("cst", [128, NCST])
    out = nc.dram_tensor("out", [SEQ, D], F32, kind="ExternalOutput").ap()
    hs = [nc.dram_tensor("hs%d" % i, [NT, 128, 8 * T], F32, kind="Internal").ap() for i in range(2)]
    xs = nc.dram_tensor("xs", [NT, 128, 8 * T], BF16, kind="Internal").ap()
    ext = nc.dram_tensor("ext", [6, 768], F32, kind="Internal").ap()

    PBYTES = 78 * 1024
    HB = 64 * 1024
    TOT = PBYTES + 2 * HB
    arena_t = es.enter_context(nc.sbuf_tensor("arena", [128, TOT // 4], F32))
    ps = [es.enter_context(nc.psum_tensor("ps%d" % b, [128, 512], F32)) for b in range(8)]

    def view(off, dtype, shape, parts=128, p0=0):
        n = int(np.prod(shape))
        esz = 4 if dtype == F32 else 2
        nb = n * esz
        assert off % 4 == 0
        a = arena_t[p0:p0 + parts, off // 4:(off + nb + 3) // 4]
        if dtype != F32:
            a = a.bitcast(dtype)
        a = a[:, 0:n]
        if len(shape) == 2:
            a = a.rearrange("p (a b) -> p a b", a=shape[0])
        elif len(shape) == 3:
            a = a.rearrange("p (a b c) -> p a b c", a=shape[0], b=shape[1])
        return a

    class Carver:
        def __init__(self, base, limit):
            self.off = base
            self.limit = limit

        def get(self, dtype, shape, parts=128, p0=0):
            n = int(np.prod(shape))
            nb = n * (4 if dtype == F32 else 2)
            nb = (nb + 31) // 32 * 32
            o = self.off
            self.off += nb
            assert self.off <= self.limit, ("arena overflow", self.off, self.limit)
            return view(o, dtype, shape, parts, p0)

    P = Carver(0, PBYTES)
    cst_sb = P.get(F32, [NCST])
    ident_f = cst_sb[:, C_ID:C_ID + 128]
    tri_f = cst_sb[:, C_TRI:C_TRI + 128]
    ones_f = cst_sb[:, C_ONES:C_ONES + 128]
    ind_f = cst_sb[:, C_IND:C_IND + 2]
    J_f = cst_sb[:, C_J:C_J + 128]
    ident_b = P.get(BF16, [128])
    ones_b = P.get(BF16, [128])
    gvec = P.get(F32, [5, 8])
    hT = [P.get(F32, [8, T]) for _ in range(2)]
    xnT = [P.get(BF16, [8, T]) for _ in range(2)]
    hsq = P.get(BF16, [8, T])
    mixT = P.get(BF16, [8, T])
    rstd_b = P.get(F32, [T])
    otile = P.get(F32, [2, D])
    ab_base = P.off
    PA = Carver(ab_base, PBYTES)
    gla_k = [PA.get(F32, [2, 192]) for _ in range(2)]
    gla_v = [PA.get(BF16, [2, 384]) for _ in range(2)]
    gla_g = [PA.get(F32, [2, 384]) for _ in range(2)]
    gqT = [[PA.get(BF16, [T], parts=48) for _ in range(4)] for _ in range(2)]
    aqT = [PA.get(BF16, [3, T]) for _ in range(2)]
    lrT = [PA.get(F32, [T], parts=32) for _ in range(2)]
    hglu = [PA.get(BF16, [2, 32 + T]) for _ in range(2)]
    btmp = PA.get(F32, [640])
    gnb = PA.get(F32, [384])
    go = PA.get(F32, [384])
    PA_KEYS = ([('gla_k', p, s_) for p in range(2) for s_ in range(2)] + [('gla_v', p, s_) for p in range(2) for s_ in range(2)]
               + [('gla_g', p, s_) for p in range(2) for s_ in range(2)] + [('gqT', p, h) for p in range(2) for h in range(4)]
               + [('aqT', p) for p in range(2)] + [('lrT', p) for p in range(2)] + [('hglu', p) for p in range(2)]
               + ['btmp', 'gnb', 'go'])
    PB = Carver(ab_base, PBYTES)
    hidT = [PB.get(BF16, [16, T]) for _ in range(2)]
    relu_t = [PB.get(F32, [T]) for _ in range(2)]
    PB_KEYS = [('hid', p, j) for p in range(2) for j in range(16)] + [('relu', n) for n in range(2)]

    S = Sched()
    sp_q = 'sp'

    def dma(q, out_ap, in_ap, reads, writes, key, **kw):
        def fn(e, out_ap=out_ap, in_ap=in_ap, kw=kw):
            return e.dma_start(out=out_ap, in_=in_ap, **kw)
        return S.add(q, fn, reads, writes, dma=key)

    def pe_mms(lst, reads, writes):
        def fn(e, lst=lst):
            ins = None
            for (o, l, r, st, sp) in lst:
                ins = e.matmul(o, lhsT=l, rhs=r, start=st, stop=sp)
            return ins
        return S.add('pe', fn, reads, writes)

    def pe_tr(lst, reads, writes):
        def fn(e, lst=lst):
            ins = None
            for (o, i, idn) in lst:
                ins = e.transpose(o, i, idn)
            return ins
        return S.add('pe', fn, reads, writes)

    def act(out_ap, in_ap, func, reads, writes, **kw):
        def fn(e, out_ap=out_ap, in_ap=in_ap, func=func, kw=kw):
            return e.activation(out=out_ap, in_=in_ap, func=func, **kw)
        return S.add('act', fn, reads, writes)

    def ts(eng, out_ap, in0, s1, s2, op0, op1, reads, writes):
        def fn(e, out_ap=out_ap, in0=in0, s1=s1, s2=s2, op0=op0, op1=op1):
            if op1 is None:
                return e.tensor_scalar(out=out_ap, in0=in0, scalar1=s1, scalar2=None, op0=op0)
            return e.tensor_scalar(out=out_ap, in0=in0, scalar1=s1, scalar2=s2, op0=op0, op1=op1)
        return S.add(eng, fn, reads, writes)

    def tt(eng, out_ap, in0, in1, op, reads, writes):
        def fn(e, out_ap=out_ap, in0=in0, in1=in1, op=op):
            return e.tensor_tensor(out=out_ap, in0=in0, in1=in1, op=op)
        return S.add(eng, fn, reads, writes)

    def stt(out_ap, in0, sc, in1, op0, op1, reads, writes):
        def fn(e, out_ap=out_ap, in0=in0, sc=sc, in1=in1, op0=op0, op1=op1):
            return e.scalar_tensor_tensor(out=out_ap, in0=in0, scalar=sc, in1=in1, op0=op0, op1=op1)
        return S.add('dve', fn, reads, writes)

    def cp(eng, out_ap, in_ap, reads, writes):
        if eng == 'act':
            return act(out_ap, in_ap, AF.Copy, reads, writes)
        def fn(e, out_ap=out_ap, in_ap=in_ap):
            return e.tensor_copy(out=out_ap, in_=in_ap)
        return S.add(eng, fn, reads, writes)

    def memset(eng, ap, val, writes):
        def fn(e, ap=ap, val=val):
            return e.memset(ap, val)
        return S.add(eng, fn, (), writes)

    def psb(b):
        return ps[b][:]

    def psbf(b):
        return ps[b][:].bitcast(BF16)

    dma(sp_q, cst_sb, cst[:, :], (), ['cst'], 'cst')
    cp('dve', ident_b, ident_f, ['cst'], ['ident_b'])
    cp('dve', ones_b, ones_f, ['cst'], ['ones_b'])
    gsrc = [norm_mix[0], norm_ffn[0], norm_mix[1], norm_ffn[1], norm_final]
    for i, g in enumerate(gsrc):
        gv_ = g.rearrange("(k p o) -> k p o", p=128, o=1)
        for k in range(8):
            dma(sp_q, gvec[:, i, k:k + 1], gv_[k], (), ['gvec'], 'gvec')
    ts('dve', gvec, gvec, 32.0, None, ALU.mult, None, ['gvec'], ['gvec'])

    WK = [[('W', 0, p) for p in range(12)], [('W', 1, p) for p in range(12)]]

    def hoff(h):
        return PBYTES + h * HB

    WA_PIECES = [(LR, CB + 256), (GQ, GK), (AQ, AV), (GK, LR), (AV, DIN)]

    def wa_keys(h, c0, c1):
        ks = []
        for pi, (a, b_) in enumerate(WA_PIECES):
            if c0 < b_ and c1 > a:
                ks.append(WK[h][pi])
        return ks

    def load_w_A(l, h):
        base = hoff(h)
        win = view(base, BF16, [8, DIN])
        wout = view(base + 8 * DIN * 2, BF16, [8, D])
        src_in = w_in[l].rearrange("(k p) n -> p k n", p=128)
        src_out = w_out[l].rearrange("(k p) n -> p k n", p=128)
        for pi, (a, b_) in enumerate(WA_PIECES):
            for k0 in range(0, 8, 4):
                dma('pool', win[:, k0:k0 + 4, a:b_], src_in[:, k0:k0 + 4, a:b_], (), [WK[h][pi]], ('W', h))
        for k in range(0, 8, 4):
            dma('pool', wout[:, k:k + 4, :], src_out[:, k:k + 4, :], (), [WK[h][8 + k // 4]], ('W', h))
        return win, wout

    def load_w_B(l, half, h):
        base = hoff(h)
        wup = view(base, BF16, [8, 2048])
        wdn = view(base + 8 * 2048 * 2, BF16, [16, D])
        src_up = w_up[l].rearrange("(k p) n -> p k n", p=128)
        src_dn = w_down[l].rearrange("(j p) n -> p j n", p=128)
        for k in range(8):
            dma('pool', wup[:, k, :], src_up[:, k, half * 2048:(half + 1) * 2048], (), [WK[h][k]], ('W', h))
        for j in range(0, 16, 4):
            dma('pool', wdn[:, j:j + 4, :], src_dn[:, half * 16 + j:half * 16 + j + 4, :], (), [WK[h][8 + j // 4]],
                ('W', h))
        return wup, wdn


    state = {}

    def units(lst):
        return [[it] for it in lst]

    def merge(*streams):
        keyed = []
        for si, st in enumerate(streams):
            us, lo, hi = st[0], st[1], st[2]
            early = st[3] if len(st) > 3 else 0
            n = len(us)
            for j, u in enumerate(us):
                k = lo + (j + 0.5) / n * (hi - lo)
                if j < early:
                    k = -1.0
                keyed.append((k, si, j, u))
        keyed.sort(key=lambda t: (t[0], t[1], t[2]))
        outl = []
        for _, _, _, u in keyed:
            outl.extend(u)
        return outl

    def rms_stats(buf, gi, bank, inplace=False):
        act(hsq, hT[buf], AF.Square, [('hT', buf)], ['hsq'])
        lst = [(psb(bank)[:, 0:T], ones_b, hsq[:, k, :], k == 0, k == 7) for k in range(8)]
        pe_mms(lst, ['hsq', 'ones_b'], [('ps', bank)])
        if S.rec is not None:
            state['early_mark'] = len(S.rec)
        act(rstd_b, psb(bank)[:, 0:T], AF.Ln, [('ps', bank)], ['rstd'], bias=float(D * EPS))
        act(rstd_b, rstd_b, AF.Exp, ['rstd'], ['rstd'], scale=-0.5)
        for k in range(8):
            if inplace:
                stt(hT[buf][:, k, :], hT[buf][:, k, :], gvec[:, gi, k:k + 1], rstd_b, ALU.mult, ALU.mult,
                    [('hT', buf), 'rstd', 'gvec'], [('hT', buf)])
            else:
                stt(xnT[buf][:, k, :], hT[buf][:, k, :], gvec[:, gi, k:k + 1], rstd_b, ALU.mult, ALU.mult,
                    [('hT', buf), 'rstd', 'gvec'], [('xnT', buf)])


    def phase_A(l, hw, hsrc, hdst):
        hb = 1 - hw
        AKEY = lambda n: ('A', l, n)
        win, wout = state['wA']
        Hc = Carver(hoff(hb), hoff(hb) + HB)
        dg = Hc.get(BF16, [2, 31, 128])
        biasb = Hc.get(BF16, [6, 640])
        kT = Hc.get(BF16, [3, 1024])
        vr = Hc.get(BF16, [8, 6, 65])
        pT = [Hc.get(BF16, [640]) for _ in range(2)]
        sig = [Hc.get(F32, [T]) for _ in range(2)]
        cy = Hc.get(F32, [2, T])
        cysq = Hc.get(F32, [2, T])
        cm = Hc.get(F32, [T])
        cmsq = Hc.get(F32, [T])
        cvar = Hc.get(F32, [T])
        crs = Hc.get(F32, [T])
        cd = Hc.get(F32, [2, T])
        cpar = Hc.get(F32, [3, 2])
        wT = Hc.get(F32, [2, 31])
        wtmp = Hc.get(F32, [256], parts=32)
        g_e = Hc.get(F32, [192])
        g_sp = Hc.get(F32, [192])
        g_ed = Hc.get(F32, [192])
        kdec = Hc.get(BF16, [192])
        Sst = Hc.get(F32, [4, 96], parts=48)
        Sbf = [Hc.get(BF16, [4, 96], parts=48) for _ in range(2)]
        dec = Hc.get(F32, [4, 2], parts=48)
        gsq = Hc.get(F32, [384])
        gms = Hc.get(F32, [4])
        gon = Hc.get(BF16, [384])
        wg = Hc.get(F32, [192], parts=32)
        rden = Hc.get(F32, [3])
        aob = Hc.get(BF16, [384])
        gsg = Hc.get(F32, [384])
        ncp = Hc.get(F32, [2, 2])
        a_keys = [AKEY(n) for n in (['dg', 'bias'] + ['kT%d' % q for q in range(8)] + ['v%d' % q for q in range(8)] +
                                    ['pT0', 'pT1', 'sig0', 'sig1', 'cy', 'cysq', 'cm', 'cmsq', 'cvar', 'crs', 'cd',
                                     'cpar', 'wT', 'wtmp', 'g_e', 'g_sp', 'g_ed', 'kdec', 'S0', 'S1', 'S2', 'S3',
                                     'Sbf0', 'Sbf1', 'dec', 'gsq', 'gms', 'gon', 'wg', 'rden', 'aob', 'vones', 'gsg', 'ncp'])]
        S.fence(WK[hb], a_keys)
        S.fence(PB_KEYS, PA_KEYS)

        for i, src in enumerate((b_dw[l], conv_ln_g[l], conv_ln_b[l])):
            sv_ = src.rearrange("(b p o) -> b p o", p=128, o=1)
            for blk in range(2):
                dma(sp_q, cpar[:, i, blk:blk + 1], sv_[blk], (), [AKEY('cpar')], AKEY('cpar'))
        ts('dve', ncp, cpar[:, 1:3, :], -1.0, None, ALU.mult, None, [AKEY('cpar')], [AKEY('ncp')])
        dma(sp_q, wtmp[0:31, :], w_dw[l], (), [AKEY('wtmp')], AKEY('wtmp'))
        for blk in range(2):
            pe_tr([(psb(6)[:, blk * 32:blk * 32 + 31], wtmp[0:31, blk * 128:(blk + 1) * 128], ident_f[0:31, 0:31])],
                  [AKEY('wtmp'), 'cst'], [('ps', 6)])
        cp('dve', wT, psb(6)[:, 0:64].rearrange("p (b j) -> p b j", b=2)[:, :, 0:31], [('ps', 6)], [AKEY('wT')])
        for blk in range(2):
            for j in range(31):
                ts('pool', dg[:, blk, j, :], ident_b, wT[:, blk, j:j + 1], None, ALU.mult, None,
                   [AKEY('wT'), 'ident_b'], [AKEY('dg')])
        memset('pool', hglu[0][:, :, 0:32], 0.0, [('hglu', 0)])
        dma(sp_q, wg[0:16, :], w_gla_gate[l], (), [AKEY('wg')], AKEY('wg'))
        dma(sp_q, wg[16:17, :], b_gla_gate[l].rearrange("(o n) -> o n", o=1), (), [AKEY('wg')], AKEY('wg'))
        dma(sp_q, gnb, gla_norm[l].partition_broadcast(128), (), ['gnb'], 'gnb')
        ts('dve', gnb, gnb, float(np.sqrt(96.0)), None, ALU.mult, None, ['gnb'], ['gnb'])
        for p_ in range(2):
            memset('pool', lrT[p_], 1.0, [('lrT', p_)])
        memset('pool', Sst, 0.0, [AKEY('S%d' % h) for h in range(4)])
        memset('pool', vr[:, :, :, 64:65], 1.0, [AKEY('vones')])
        dma(sp_q, ext[:, 0:256], rel_bias[l][:, 1:257], (), ['ext'], 'ext')
        dma(sp_q, btmp[0:6, 512:513], rel_bias[l][:, 256:257], (), ['btmp'], 'btmp',
            allow_slow_non_contiguous=True)
        cp('dve', btmp[0:6, 0:512], btmp[0:6, 512:513].to_broadcast([6, 512]), ['btmp'], ['btmp'])
        dma(sp_q, ext[:, 256:768], btmp[0:6, 0:512], ['btmp'], ['ext'], 'ext')
        for h in range(6):
            src = bass.AP(tensor=ext.tensor, offset=ext.offset + h * 768, ap=[[1, 128], [1, 640]])
            dma(sp_q, btmp, src, ['ext'], ['btmp'], 'btmp')
            pe_mms([(psb(4)[:, 0:512], J_f, btmp[:, 0:512], True, True),
                    (psb(5)[:, 0:128], J_f, btmp[:, 512:640], True, True)], ['btmp', 'cst'],
                   [('ps', 4), ('ps', 5)])
            cp('dve', biasb[:, h, 0:512], psb(4)[:, 0:512], [('ps', 4)], [AKEY('bias')])
            cp('dve', biasb[:, h, 512:640], psb(5)[:, 0:128], [('ps', 5)], [AKEY('bias')])
        memset('pool', biasb[64:128, :, 0:64], NEG, [AKEY('bias')])
        memset('pool', biasb[0:64, :, 576:640], NEG, [AKEY('bias')])

        def load_x(i):
            dma(sp_q, otile, x[i * T:(i + 1) * T, :].rearrange("(s p) d -> p s d", p=128), (), ['otile'], 'otile')

        def stage_P(i):
            par = i % 2
            S.record()
            if l == 0:
                for s in range(2):
                    for kk in range(0, 8, 4):
                        b = kk // 4
                        pe_tr([(psb(b)[:, j * 128:(j + 1) * 128], otile[:, s, (kk + j) * 128:(kk + j + 1) * 128], ident_f)
                               for j in range(4)], ['otile', 'cst'], [('ps', b)])
                        cp('act' if kk == 0 else 'dve', hT[par][:, kk:kk + 4, s * 128:(s + 1) * 128],
                           psb(b).rearrange("p (j t) -> p j t", j=4), [('ps', b)], [('hT', par)])
                if i + 1 < ntiles:
                    load_x(i + 1)
            else:
                dma(sp_q, hT[par].rearrange("p k t -> p (k t)"), hsrc[i], [('hs', id(hsrc), i)], [('hT', par)],
                    ('hT', par))
            rms_stats(par, 2 * l, 0)
            XN = ('xnT', par)
            xn = xnT[par]
            cnt = [0]

            def nb():
                cnt[0] += 1
                return cnt[0] % 2

            def fm(cols, m, pbase=0):
                b = nb()
                pe_mms([(psb(b)[pbase:pbase + m, 0:T], win[:, k, cols:cols + m], xn[:, k, :], k == 0, k == 7)
                        for k in range(8)], [XN] + wa_keys(hw, cols, cols + m), [('ps', b)])
                return b
            b = fm(LR, 128)
            cp('dve', lrT[par][0:16, :], psb(b)[0:16, 0:T], [('ps', b)], [('lrT', par)])
            for h in range(4):
                b = fm(GQ + 48 * h, 48)
                ts('dve', gqT[par][h], psb(b)[0:48, 0:T], float(48 ** -0.5), None, ALU.mult, None, [('ps', b)],
                   [('gqT', par, h)])
            for blk in range(2):
                bb = fm(CB + 128 * blk, 128)
                act(sig[blk], psb(bb)[:, 0:T], AF.Exp, [('ps', bb)], [AKEY('sig%d' % blk)], scale=-1.0)
                act(sig[blk], sig[blk], AF.Ln, [AKEY('sig%d' % blk)], [AKEY('sig%d' % blk)], bias=1.0)
                act(sig[blk], sig[blk], AF.Exp, [AKEY('sig%d' % blk)], [AKEY('sig%d' % blk)], scale=-1.0)
            slot0 = (2 * i) % 8
            for pr in range(3):
                b = fm(AQ + 128 * pr, 128)
                ts('dve', aqT[par][:, pr, :], psb(b)[:, 0:T], 0.125, None, ALU.mult, None, [('ps', b)], [('aqT', par)])
            for pr in range(3):
                b = fm(AK + 128 * pr, 128)
                cp('dve', kT[:, pr, slot0 * 128:slot0 * 128 + T], psb(b)[:, 0:T], [('ps', b)],
                   [AKEY('kT%d' % slot0), AKEY('kT%d' % (slot0 + 1))])
            for blk in range(2):
                ba = fm(CA + 128 * blk, 128)
                tt('dve', hglu[par][:, blk, 32:32 + T], psb(ba)[:, 0:T], sig[blk], ALU.mult,
                   [('ps', ba), AKEY('sig%d' % blk)], [('hglu', par)])
            if i > 0:
                cp('pool', hglu[par][:, :, 0:32], hglu[1 - par][:, :, T:T + 32], [('hglu', 1 - par)], [('hglu', par)])
            for s in range(2):
                g = 2 * i + s
                slot = g % 8

                def tm(cols, n):
                    b = nb()
                    pe_mms([(psb(b)[:, 0:n], xn[:, k, s * 128:(s + 1) * 128], win[:, k, cols:cols + n], k == 0, k == 7)
                            for k in range(8)], [XN] + wa_keys(hw, cols, cols + n), [('ps', b)])
                    return b
                b = tm(GG, 384)
                cp('dve', gla_g[par][:, s, :], psb(b)[:, 0:384], [('ps', b)], [('gla_g', par, s)])
                act(gsg, gla_g[par][:, s, :], AF.Exp, [('gla_g', par, s)], [AKEY('gsg')], scale=-1.0)
                act(gsg, gsg, AF.Ln, [AKEY('gsg')], [AKEY('gsg')], bias=1.0)
                act(gsg, gsg, AF.Exp, [AKEY('gsg')], [AKEY('gsg')], scale=-1.0)
                tt('pool', gla_g[par][:, s, :], gla_g[par][:, s, :], gsg, ALU.mult, [('gla_g', par, s), AKEY('gsg')],
                   [('gla_g', par, s)])
                b = tm(GK, 192)
                cp('dve', gla_k[par][:, s, :], psb(b)[:, 0:192], [('ps', b)], [('gla_k', par, s)])
                b = tm(GV, 384)
                cp('act', gla_v[par][:, s, :], psb(b)[:, 0:384], [('ps', b)], [('gla_v', par, s)])
                b = tm(AV, 384)
                cp('dve', vr[:, slot, :, 0:64], psb(b)[:, 0:384].rearrange("p (h d) -> p h d", h=6), [('ps', b)],
                   [AKEY('v%d' % slot)])
            return S.stop()

        def stage_G(i):
            par = i % 2
            S.record()
            B2 = [('ps', 2)]
            for s in range(2):
                pe_mms([(psb(2)[:, 0:192], lrT[par][0:17, s * 128:(s + 1) * 128], wg[0:17, :], True, True)],
                       [('lrT', par), AKEY('wg')], B2)
                act(g_e, psb(2)[:, 0:192], AF.Exp, B2, [AKEY('g_e')], scale=-1.0)
                act(g_sp, g_e, AF.Ln, [AKEY('g_e')], [AKEY('g_sp')], bias=1.0)
                pe_mms([(psb(2)[:, 0:192], tri_f, g_sp, True, True)] +
                       [(psb(2)[0:48, 192 + 2 * h:192 + 2 * h + 2], g_sp[:, 48 * h:48 * h + 48], ind_f, True, True)
                        for h in range(4)], [AKEY('g_sp'), 'cst'], B2)
                act(g_ed, psb(2)[:, 0:192], AF.Exp, B2, [AKEY('g_ed')])
                act(dec, psb(2)[0:48, 192:200].rearrange("p (h c) -> p h c", h=4), AF.Exp, B2, [AKEY('dec')])
                tt('dve', kdec, gla_k[par][:, s, :], g_ed, ALU.mult, [('gla_k', par, s), AKEY('g_ed')], [AKEY('kdec')])
                for c in range(2):
                    sb = c
                    pe_mms([(psb(2)[0:48, 96 * h:96 * h + 96], kdec[c * 64:(c + 1) * 64, 48 * h:48 * h + 48],
                             gla_v[par][c * 64:(c + 1) * 64, s, 96 * h:96 * h + 96], True, True) for h in range(4)],
                           [AKEY('kdec'), ('gla_v', par, s)], B2)
                    for h in range(4):
                        stt(Sst[:, h, :], Sst[:, h, :], dec[:, h, c:c + 1], psb(2)[0:48, 96 * h:96 * h + 96],
                            ALU.mult, ALU.add, [AKEY('S%d' % h), AKEY('dec'), ('ps', 2)], [AKEY('S%d' % h)])
                    cp('act', Sbf[sb], Sst, [AKEY('S%d' % h) for h in range(4)], [AKEY('Sbf%d' % sb)])
                    pe_mms([(psb(2)[c * 64:(c + 1) * 64, 96 * h:96 * h + 96],
                             gqT[par][h][:, s * 128 + c * 64:s * 128 + c * 64 + 64], Sbf[sb][:, h, :], True, True)
                            for h in range(4)], [('gqT', par, h) for h in range(4)] + [AKEY('Sbf%d' % sb)], B2)
                    cp('act', go[c * 64:(c + 1) * 64, :], psb(2)[c * 64:(c + 1) * 64, 0:384], B2, ['go'])
                tt('pool', gsq, go, go, ALU.mult, ['go'], [AKEY('gsq')])

                def red(e, gms=gms, gsq=gsq):
                    return e.tensor_reduce(out=gms, in_=gsq.rearrange("p (h v) -> p h v", h=4), axis=AX.X, op=ALU.add)
                S.add('dve', red, [AKEY('gsq')], [AKEY('gms')])
                act(gms, gms, AF.Ln, [AKEY('gms')], [AKEY('gms')], bias=float(96 * EPS))
                act(gms, gms, AF.Exp, [AKEY('gms')], [AKEY('gms')], scale=-0.5)
                tt('pool', go, go, gnb, ALU.mult, ['go', 'gnb'], ['go'])
                tt('pool', go, go, gla_g[par][:, s, :], ALU.mult, ['go', ('gla_g', par, s)], ['go'])
                tt('dve', gon.rearrange("p (h v) -> p h v", h=4), go.rearrange("p (h v) -> p h v", h=4),
                   gms.unsqueeze(2).to_broadcast([128, 4, 96]), ALU.mult, ['go', AKEY('gms')], [AKEY('gon')])
                pe_tr([(psbf(2)[:, j * 128:(j + 1) * 128], gon[:, j * 128:(j + 1) * 128], ident_b) for j in range(3)],
                      [AKEY('gon'), 'ident_b'], B2)
                cp('dve', mixT[:, 0:3, s * 128:(s + 1) * 128], psbf(2)[:, 0:384].rearrange("p (j t) -> p j t", j=3),
                   B2, [('mixT', 'g', s)])
            B3 = [('ps', 2)]
            pe_mms([(psb(2)[:, blk * T:(blk + 1) * T], dg[:, blk, j, :], hglu[par][:, blk, 2 + j:2 + j + T],
                     j == 0, j == 30) for blk in range(2) for j in range(31)], [AKEY('dg'), ('hglu', par)], B3)
            for blk in range(2):
                act(cy[:, blk, :], psb(2)[:, blk * T:(blk + 1) * T], AF.Identity, B3 + [AKEY('cpar')],
                    [AKEY('cy')], bias=cpar[:, 0, blk:blk + 1])
                act(cysq[:, blk, :], psb(2)[:, blk * T:(blk + 1) * T], AF.Square, B3 + [AKEY('cpar')],
                    [AKEY('cysq')], bias=cpar[:, 0, blk:blk + 1])
            pe_mms([(psb(2)[:, 0:T], ones_f, cy[:, 0, :], True, False), (psb(2)[:, 0:T], ones_f, cy[:, 1, :], False, True),
                    (psb(2)[:, T:2 * T], ones_f, cysq[:, 0, :], True, False),
                    (psb(2)[:, T:2 * T], ones_f, cysq[:, 1, :], False, True)],
                   [AKEY('cy'), AKEY('cysq'), 'cst'], B3)
            ts('dve', cm, psb(2)[:, 0:T], 1.0 / 256.0, None, ALU.mult, None, B3, [AKEY('cm')])
            tt('dve', cmsq, cm, cm, ALU.mult, [AKEY('cm')], [AKEY('cmsq')])
            stt(cvar, psb(2)[:, T:2 * T], 1.0 / 256.0, cmsq, ALU.mult, ALU.subtract, B3 + [AKEY('cmsq')],
                [AKEY('cvar')])
            act(crs, cvar, AF.Ln, [AKEY('cvar')], [AKEY('crs')], bias=float(EPS))
            act(crs, crs, AF.Exp, [AKEY('crs')], [AKEY('crs')], scale=-0.5)
            for blk in range(2):
                tt('pool', cd[:, blk, :], cy[:, blk, :], cm, ALU.subtract, [AKEY('cy'), AKEY('cm')], [AKEY('cd')])
                tt('pool', cd[:, blk, :], cd[:, blk, :], crs, ALU.mult, [AKEY('cd'), AKEY('crs')], [AKEY('cd')])
                act(cysq[:, blk, :], cd[:, blk, :], AF.Exp, [AKEY('cd'), AKEY('ncp')], [AKEY('cysq')],
                    scale=ncp[:, 0, blk:blk + 1], bias=ncp[:, 1, blk:blk + 1])
                act(cysq[:, blk, :], cysq[:, blk, :], AF.Ln, [AKEY('cysq')], [AKEY('cysq')], bias=1.0)
                act(cysq[:, blk, :], cysq[:, blk, :], AF.Exp, [AKEY('cysq')], [AKEY('cysq')], scale=-1.0)
                ts('pool', cd[:, blk, :], cd[:, blk, :], cpar[:, 1, blk:blk + 1], cpar[:, 2, blk:blk + 1], ALU.mult,
                   ALU.add, [AKEY('cd'), AKEY('cpar')], [AKEY('cd')])
                tt('dve', mixT[:, 3 + blk, :], cd[:, blk, :], cysq[:, blk, :], ALU.mult, [AKEY('cd'), AKEY('cysq')],
                   [('mixT', 'c', blk)])
            return S.stop()

        def stage_Att(i):
            par = i % 2
            S.record()

            def scores(s, h):
                g = 2 * i + s
                nj = min(5, g + 1)
                pr, p0 = h // 2, 64 * (h % 2)
                sb = h % 2
                bA = 4 if sb == 0 else 6
                bB = bA + 1
                kB = ('ps', bB)
                cB = 0
                q_ap = aqT[par][p0:p0 + 64, pr, s * 128:(s + 1) * 128]
                lst = [(psb(bA)[:, 0:512], ident_b, biasb[:, h, 0:512], True, False)]
                for j in range(min(nj, 4)):
                    sl = (g - j) % 8
                    lst.append((psb(bA)[:, j * 128:(j + 1) * 128], kT[p0:p0 + 64, pr, sl * 128:(sl + 1) * 128],
                                q_ap, False, j == min(nj, 4) - 1))
                rd = ['ident_b', AKEY('bias'), ('aqT', par)] + [AKEY('kT%d' % ((g - j) % 8)) for j in range(nj)]
                wr = [('ps', bA)]
                if nj == 5:
                    sl = (g - 4) % 8
                    lst.append((psb(bB)[:, cB:cB + 128], ident_b, biasb[:, h, 512:640], True, False))
                    lst.append((psb(bB)[:, cB:cB + 128], kT[p0:p0 + 64, pr, sl * 128:(sl + 1) * 128], q_ap, False, True))
                    wr.append(kB)
                pe_mms(lst, rd, wr)
                na = min(nj, 4) * 128
                act(pT[sb][:, 0:na], psb(bA)[:, 0:na], AF.Exp, [('ps', bA)], [AKEY('pT%d' % sb)])
                if nj == 5:
                    act(pT[sb][:, 512:640], psb(bB)[:, cB:cB + 128], AF.Exp, [kB], [AKEY('pT%d' % sb)])

            def pv(s, h):
                g = 2 * i + s
                nj = min(5, g + 1)
                sb = h % 2
                hb3, hh = h // 3, h % 3
                pe_mms([(psb(3)[:, 65 * hh:65 * hh + 65], pT[sb][:, j * 128:(j + 1) * 128],
                         vr[:, (g - j) % 8, h, :], j == 0, j == nj - 1) for j in range(nj)],
                       [AKEY('pT%d' % sb), AKEY('vones')] + [AKEY('v%d' % ((g - j) % 8)) for j in range(nj)],
                       [('ps', 3)])
                if hh == 2:
                    o3 = psb(3)[:, 0:195].rearrange("p (h d) -> p h d", h=3)

                    def rcp(e, rden=rden, o3=o3):
                        return e.reciprocal(out=rden.unsqueeze(2), in_=o3[:, :, 64:65])
                    S.add('dve', rcp, [('ps', 3)], [AKEY('rden')])
                    tt('dve', aob[:, hb3 * 192:(hb3 + 1) * 192].rearrange("p (h d) -> p h d", h=3), o3[:, :, 0:64],
                       rden.unsqueeze(2).to_broadcast([128, 3, 64]), ALU.mult, [('ps', 3), AKEY('rden')],
                       [AKEY('aob')])
                if h == 5:
                    pe_tr([(psbf(3)[:, j * 128:(j + 1) * 128], aob[:, j * 128:(j + 1) * 128], ident_b)
                           for j in range(3)], [AKEY('aob'), 'ident_b'], [('ps', 3)])
                    cp('dve', mixT[:, 5:8, s * 128:(s + 1) * 128],
                       psbf(3)[:, 0:384].rearrange("p (j t) -> p j t", j=3), [('ps', 3)], [('mixT', 'a', s)])

            pairs = [(s, h) for s in range(2) for h in range(6)]
            scores(*pairs[0])
            for k in range(len(pairs)):
                if k + 1 < len(pairs):
                    scores(*pairs[k + 1])
                pv(*pairs[k])
            return S.stop()

        MIXK = [('mixT', 'g', 0), ('mixT', 'g', 1), ('mixT', 'c', 0), ('mixT', 'c', 1), ('mixT', 'a', 0), ('mixT', 'a', 1)]

        def stage_O(i):
            par = i % 2
            S.record()
            for m in range(8):
                b = 2 + (m % 2)
                pe_mms([(psb(b)[:, 0:T], wout[:, k, m * 128:(m + 1) * 128], mixT[:, k, :], k == 0, k == 7)
                        for k in range(8)], MIXK + WK[hw][8:10], [('ps', b)])
                tt('dve', hT[par][:, m, :], hT[par][:, m, :], psb(b)[:, 0:T], ALU.add, [('hT', par), ('ps', b)],
                   [('hT', par)])
            dma(sp_q, hdst[i], hT[par].rearrange("p k t -> p (k t)"), [('hT', par)], [('hs', id(hdst), i)],
                ('hTst', par))
            return S.stop()

        if l == 0:
            load_x(0)
        S.replay(stage_P(0))
        for i in range(ntiles):
            pn = units(stage_P(i + 1)) if i + 1 < ntiles else []
            n_early = state.get('early_mark', 0) if pn else 0
            gu = units(stage_G(i))
            au = units(stage_Att(i))
            S.replay(merge((pn, 0.25, 1.0, n_early), (gu, 0.0, 0.9), (au, 0.0, 0.9)))
            S.replay(stage_O(i))
        return a_keys

    def phase_B(l, half, hw, hsrc, hdst, final=False):
        wup, wdn = state['wB']
        S.fence(PA_KEYS, PB_KEYS)

        def stage_X(i):
            par = i % 2
            dma(sp_q, hT[par].rearrange("p k t -> p (k t)"), hsrc[i], [('hs', id(hsrc), i)], [('hT', par)], ('hT', par))
            if half == 1:
                dma(sp_q, xnT[par].rearrange("p k t -> p (k t)"), xs[i], [('xs', i)], [('xnT', par)], ('xnT', par))
            else:
                rms_stats(par, 2 * l + 1, 0)
                dma(sp_q, xs[i], xnT[par].rearrange("p k t -> p (k t)"), [('xnT', par)], [('xs', i)], ('xnTst', par))

        def stage_U(i):
            par = i % 2
            S.record()
            for j in range(16):
                b = j % 4
                pe_mms([(psb(b)[:, 0:T], wup[:, k, j * 128:(j + 1) * 128], xnT[par][:, k, :], k == 0, k == 7)
                        for k in range(8)], [('xnT', par)] + WK[hw], [('ps', b)])
                r = relu_t[j % 2]
                act(r, psb(b)[:, 0:T], AF.Relu, [('ps', b)], [('relu', j % 2)])
                tt('pool' if j % 2 == 0 else 'dve', hidT[par][:, j, :], r, r, ALU.mult, [('relu', j % 2)],
                   [('hid', par, j)])
            return S.stop()

        def stage_D(i):
            par = i % 2
            for m in range(8):
                b = 4 + (m % 4)
                pe_mms([(psb(b)[:, 0:T], wdn[:, j, m * 128:(m + 1) * 128], hidT[par][:, j, :], j == 0, j == 15)
                        for j in range(16)], [('hid', par, j) for j in range(16)] + WK[hw], [('ps', b)])
                tt('dve', hT[par][:, m, :], hT[par][:, m, :], psb(b)[:, 0:T], ALU.add, [('hT', par), ('ps', b)],
                   [('hT', par)])
            if not final:
                dma(sp_q, hdst[i], hT[par].rearrange("p k t -> p (k t)"), [('hT', par)], [('hs', id(hdst), i)],
                    ('hTst', par))

        def stage_F(i):
            par = i % 2
            S.record()
            rms_stats(par, 4, 4, inplace=True)
            for s in range(2):
                for kk in range(0, 8, 4):
                    b = 5 + (kk // 4)
                    pe_tr([(psb(b)[:, j * 128:(j + 1) * 128], hT[par][:, kk + j, s * 128:(s + 1) * 128], ident_f)
                           for j in range(4)], [('hT', par), 'cst'], [('ps', b)])
                    cp('act' if kk == 0 else 'dve', otile[:, s, kk * 128:(kk + 4) * 128], psb(b), [('ps', b)],
                       ['otile'])
            dma(sp_q, out[i * T:(i + 1) * T, :].rearrange("(s p) d -> p s d", p=128), otile, ['otile'], ['out'],
                'otile_st')
            return S.stop()

        stage_X(0)
        S.replay(stage_U(0))
        for i in range(ntiles):
            if i + 1 < ntiles:
                stage_X(i + 1)
            stage_D(i)
            un = units(stage_U(i + 1)) if i + 1 < ntiles else []
            fu = units(stage_F(i)) if final else []
            S.replay(merge((un, 0.0, 1.0), (fu, 0.0, 1.0)))

    def dump_dbg(hsbuf):
        dbg = nc.dram_tensor("dbg", [NT, 128, 8 * T], F32, kind="ExternalOutput").ap()
        for i in range(ntiles):
            dma(sp_q, hT[0].rearrange("p k t -> p (k t)"), hsbuf[i], [('hs', id(hsbuf), i)], [('hT', 0)], ('hT', 0))
            dma(sp_q, dbg[i], hT[0].rearrange("p k t -> p (k t)"), [('hT', 0)], ['dbg'], 'dbg')

    cur = 0
    wA = load_w_A(0, 0)
    for l in range(DEPTH):
        hw = l % 2
        state['wA'] = wA
        if l == 0:
            akeys = phase_A(l, hw, None, hs[0])
            cur = 0
        else:
            akeys = phase_A(l, hw, hs[cur], hs[1 - cur])
            cur = 1 - cur
        if stop_after == (l, 'A'):
            dump_dbg(hs[cur])
            break
        S.fence(akeys, WK[1 - hw])
        wB1 = load_w_B(l, 0, 1 - hw)
        wB2 = load_w_B(l, 1, hw)
        state['wB'] = wB1
        phase_B(l, 0, 1 - hw, hs[cur], hs[1 - cur])
        cur = 1 - cur
        if stop_after == (l, 'B1'):
            dump_dbg(hs[cur])
            break
        if l + 1 < DEPTH:
            wA = load_w_A(l + 1, 1 - hw)
        state['wB'] = wB2
        fin = (l == DEPTH - 1 and stop_after is None)
        phase_B(l, 1, hw, hs[cur], hs[1 - cur], final=fin)
        cur = 1 - cur
        if stop_after == (l, 'B2'):
            dump_dbg(hs[cur])
            break

    S.emit(nc, es)
    es.close()
    return nc


_NC_CACHE = {}


def kernel(**inputs):
    key = 'full'
    if key not in _NC_CACHE:
        _NC_CACHE[key] = build()
    nc = _NC_CACHE[key]
    cstv = make_consts()
    names = ["norm_mix", "w_in", "w_gla_gate", "b_gla_gate", "gla_norm", "w_dw", "b_dw", "conv_ln_g", "conv_ln_b",
             "rel_bias", "w_out", "norm_ffn", "w_up", "w_down", "norm_final"]
    shared = {n: np.ascontiguousarray(np.asarray(inputs[n], dtype=np.float32)) for n in names}
    xfull = np.asarray(inputs["x"], dtype=np.float32)
    in_maps = []
    for c in range(8):
        m = dict(shared)
        m["x"] = np.ascontiguousarray(xfull[c])
        m["cst"] = cstv
        in_maps.append(m)
    res = run_bass_kernel_spmd(nc, in_maps, core_ids=list(range(8)))
    return np.stack([np.asarray(r["out"], dtype=np.float32) for r in res.results], axis=0)
```

```python
import numpy as np
from contextlib import ExitStack
import concourse.bass as bass
import concourse.mybir as mybir
from concourse.bass_utils import run_bass_kernel_spmd

F32 = mybir.dt.float32
BF16 = mybir.dt.bfloat16
AF = mybir.ActivationFunctionType
ALU = mybir.AluOpType
AX = mybir.AxisListType

D = 1024
SEQ = 4096
DEPTH = 2
T = 256
NT = SEQ // T
DIN = 2832
DFF = 4096
EPS = 1e-6
GQ, GK, GV, GG, LR, CA, CB, AQ, AK, AV = 0, 192, 384, 768, 1152, 1168, 1424, 1680, 2064, 2448
NEG = -30000.0

C_ID, C_TRI, C_ONES, C_IND, C_J = 0, 128, 256, 384, 386
NCST = 514


def make_consts():
    c = np.zeros((128, NCST), np.float32)
    c[:, C_ID:C_ID + 128] = np.eye(128, dtype=np.float32)
    s = np.arange(128)[:, None]
    t = np.arange(128)[None, :]
    c[:, C_TRI:C_TRI + 128] = np.where((s > t) & (s // 64 == t // 64), -1.0 / 16.0, 0.0)
    c[:, C_ONES:C_ONES + 128] = 1.0
    c[:, C_IND:C_IND + 2] = np.where(s // 64 == np.arange(2)[None, :], -1.0 / 16.0, 0.0)
    c[:, C_J:C_J + 128] = np.eye(128, dtype=np.float32)[::-1]
    return c


class Sched:
    def __init__(self):
        self.ops = []
        self.lw = {}
        self.rd = {}
        self.rec = None

    def record(self):
        assert self.rec is None
        self.rec = []

    def stop(self):
        r = self.rec
        self.rec = None
        return r

    def replay(self, items):
        assert self.rec is None
        for it in items:
            self.add(*it)

    def add(self, eng, fn, reads=(), writes=(), dma=None):
        if self.rec is not None:
            self.rec.append((eng, fn, tuple(reads), tuple(writes), dma))
            return -1
        raw = set()
        for k in reads:
            raw |= self.lw.get(k, set())
        oth = set()
        for k in writes:
            oth |= self.lw.get(k, set())
            oth |= set(self.rd.get(k, {}).values())
        mysig = ('d', dma) if dma is not None else ('e', eng)
        for k in reads:
            if isinstance(k, tuple) and k[0] == 'ps':
                oth |= {v for sg, v in self.rd.get(k, {}).items() if sg != mysig}
        i = len(self.ops)
        self.ops.append(dict(eng=eng, fn=fn, raw=raw, oth=oth - raw, dma=dma, need=False))
        sig = ('d', dma) if dma is not None else ('e', eng)
        for k in reads:
            self.rd.setdefault(k, {})[sig] = i
        for k in writes:
            self.lw[k] = {i}
            self.rd[k] = {}
        return i

    def fence(self, src, dst):
        acc = set()
        for k in src:
            acc |= self.lw.get(k, set()) | set(self.rd.get(k, {}).values())
        for k in dst:
            self.lw[k] = self.lw.get(k, set()) | acc

    def emit(self, nc, es):
        ops = self.ops
        CAP = 30000
        for o in ops:
            deps = set()
            for j in o['raw']:
                pj = ops[j]
                if pj['dma'] is not None or pj['eng'] != o['eng'] or o['eng'] != 'pe':
                    deps.add(j)
            for j in o['oth']:
                pj = ops[j]
                if pj['dma'] is not None or pj['eng'] != o['eng'] or o['dma'] is not None:
                    deps.add(j)
            o['deps'] = deps
            for j in deps:
                ops[j]['need'] = True
        cnt = {}
        dcnt = {}
        engsems = {}
        dsems = {}

        def get_esem(e, idx):
            key = (e, idx)
            if key not in engsems:
                engsems[key] = es.enter_context(nc.semaphore("s_%s_%d" % (e, idx)))
            return engsems[key]

        for o in ops:
            if o['dma'] is not None:
                k = o['dma']
                if k not in dsems:
                    dsems[k] = es.enter_context(nc.semaphore("d_%d" % len(dsems)))
                dcnt[k] = dcnt.get(k, 0) + 16
                o['sem'] = dsems[k]
                o['val'] = dcnt[k]
                o['need'] = True
            elif o['need']:
                e = o['eng']
                c = cnt.get(e, 0)
                o['sem'] = get_esem(e, c // CAP)
                o['val'] = c % CAP + 1
                cnt[e] = c + 1
        self.nsem = len(engsems) + len(dsems)
        per = {e: [] for e in ('pe', 'act', 'dve', 'pool', 'sp')}
        for o in ops:
            per[o['eng']].append(o)

        def run(eng_obj, lst):
            waited = {}
            for o in lst:
                for j in sorted(o['deps']):
                    pj = ops[j]
                    sid = id(pj['sem'])
                    if waited.get(sid, 0) >= pj['val']:
                        continue
                    waited[sid] = pj['val']
                    eng_obj.wait_ge(pj['sem'], pj['val'])
                inst = o['fn'](eng_obj)
                if o['dma'] is not None:
                    inst.then_inc(o['sem'], 16)
                elif o['need']:
                    inst.then_inc(o['sem'], 1)
            last = {}
            for o in lst:
                if o['dma'] is not None:
                    last[id(o['sem'])] = (o['sem'], o['val'])
            for sid, (sm, v) in last.items():
                if waited.get(sid, 0) < v:
                    eng_obj.wait_ge(sm, v)

        block = es.enter_context(nc.Block())

        @block.sync
        def _(e):
            run(e, per['sp'])

        @block.gpsimd
        def _(e):
            run(e, per['pool'])

        @block.scalar
        def _(e):
            run(e, per['act'])

        @block.vector
        def _(e):
            run(e, per['dve'])

        @block.tensor
        def _(e):
            run(e, per['pe'])


def build(ntiles=NT, stop_after=None):
    nc = bass.Bass("TRN2", target_bir_lowering=False)
    es = ExitStack()
    dr = {}

    def din(name, shape):
        dr[name] = nc.dram_tensor(name, list(shape), F32, kind="ExternalInput").ap()
        return dr[name]

    x = din("x", [SEQ, D])
    norm_mix = din("norm_mix", [DEPTH, D])
    w_in = din("w_in", [DEPTH, D, DIN])
    w_gla_gate = din("w_gla_gate", [DEPTH, 16, 192])
    b_gla_gate = din("b_gla_gate", [DEPTH, 192])
    gla_norm = din("gla_norm", [DEPTH, 384])
    w_dw = din("w_dw", [DEPTH, 31, 256])
    b_dw = din("b_dw", [DEPTH, 256])
    conv_ln_g = din("conv_ln_g", [DEPTH, 256])
    conv_ln_b = din("conv_ln_b", [DEPTH, 256])
    rel_bias = din("rel_bias", [DEPTH, 6, 257])
    w_out = din("w_out", [DEPTH, D, D])
    norm_ffn = din("norm_ffn", [DEPTH, D])
    w_up = din("w_up", [DEPTH, D, DFF])
    w_down = din("w_down", [DEPTH, DFF, D])
    norm_final = din("norm_final", [D])
    cst = din("cst", [128, NCST])
    out = nc.dram_tensor("out", [SEQ, D], F32, kind="ExternalOutput").ap()
    hs = [nc.dram_tensor("hs%d" % i, [NT, 128, 8 * T], F32, kind="Internal").ap() for i in range(2)]
    xs = nc.dram_tensor("xs", [NT, 128, 8 * T], BF16, kind="Internal").ap()
    ext = nc.dram_tensor("ext", [6, 768], F32, kind="Internal").ap()

    PBYTES = 78 * 1024
    HB = 64 * 1024
    TOT = PBYTES + 2 * HB
    arena_t = es.enter_context(nc.sbuf_tensor("arena", [128, TOT // 4], F32))
    ps = [es.enter_context(nc.psum_tensor("ps%d" % b, [128, 512], F32)) for b in range(8)]

    def view(off, dtype, shape, parts=128, p0=0):
        n = int(np.prod(shape))
        esz = 4 if dtype == F32 else 2
        nb = n * esz
        assert off % 4 == 0
        a = arena_t[p0:p0 + parts, off // 4:(off + nb + 3) // 4]
        if dtype != F32:
            a = a.bitcast(dtype)
        a = a[:, 0:n]
        if len(shape) == 2:
            a = a.rearrange("p (a b) -> p a b", a=shape[0])
        elif len(shape) == 3:
            a = a.rearrange("p (a b c) -> p a b c", a=shape[0], b=shape[1])
        return a

    class Carver:
        def __init__(self, base, limit):
            self.off = base
            self.limit = limit

        def get(self, dtype, shape, parts=128, p0=0):
            n = int(np.prod(shape))
            nb = n * (4 if dtype == F32 else 2)
            nb = (nb + 31) // 32 * 32
            o = self.off
            self.off += nb
            assert self.off <= self.limit, ("arena overflow", self.off, self.limit)
            return view(o, dtype, shape, parts, p0)

    P = Carver(0, PBYTES)
    cst_sb = P.get(F32, [NCST])
    ident_f = cst_sb[:, C_ID:C_ID + 128]
    tri_f = cst_sb[:, C_TRI:C_TRI + 128]
    ones_f = cst_sb[:, C_ONES:C_ONES + 128]
    ind_f = cst_sb[:, C_IND:C_IND + 2]
    J_f = cst_sb[:, C_J:C_J + 128]
    ident_b = P.get(BF16, [128])
    ones_b = P.get(BF16, [128])
    gvec = P.get(F32, [5, 8])
    hT = [P.get(F32, [8, T]) for _ in range(2)]
    xnT = [P.get(BF16, [8, T]) for _ in range(2)]
    hsq = P.get(BF16, [8, T])
    mixT = P.get(BF16, [8, T])
    rstd_b = P.get(F32, [T])
    otile = P.get(F32, [2, D])
    ab_base = P.off
    PA = Carver(ab_base, PBYTES)
    gla_k = [PA.get(F32, [2, 192]) for _ in range(2)]
    gla_v = [PA.get(BF16, [2, 384]) for _ in range(2)]
    gla_g = [PA.get(F32, [2, 384]) for _ in range(2)]
    gqT = [[PA.get(BF16, [T], parts=48) for _ in range(4)] for _ in range(2)]
    aqT = [PA.get(BF16, [3, T]) for _ in range(2)]
    lrT = [PA.get(F32, [T], parts=32) for _ in range(2)]
    hglu = [PA.get(BF16, [2, 32 + T]) for _ in range(2)]
    btmp = PA.get(F32, [640])
    gnb = PA.get(F32, [384])
    go = PA.get(F32, [384])
    PA_KEYS = ([('gla_k', p, s_) for p in range(2) for s_ in range(2)] + [('gla_v', p, s_) for p in range(2) for s_ in range(2)]
               + [('gla_g', p, s_) for p in range(2) for s_ in range(2)] + [('gqT', p, h) for p in range(2) for h in range(4)]
               + [('aqT', p) for p in range(2)] + [('lrT', p) for p in range(2)] + [('hglu', p) for p in range(2)]
               + ['btmp', 'gnb', 'go'])
    PB = Carver(ab_base, PBYTES)
    hidT = [PB.get(BF16, [16, T]) for _ in range(2)]
    relu_t = [PB.get(F32, [T]) for _ in range(2)]
    PB_KEYS = [('hid', p, j) for p in range(2) for j in range(16)] + [('relu', n) for n in range(2)]

    S = Sched()
    sp_q = 'sp'

    def dma(q, out_ap, in_ap, reads, writes, key, **kw):
        def fn(e, out_ap=out_ap, in_ap=in_ap, kw=kw):
            return e.dma_start(out=out_ap, in_=in_ap, **kw)
        return S.add(q, fn, reads, writes, dma=key)

    def pe_mms(lst, reads, writes):
        def fn(e, lst=lst):
            ins = None
            for (o, l, r, st, sp) in lst:
                ins = e.matmul(o, lhsT=l, rhs=r, start=st, stop=sp)
            return ins
        return S.add('pe', fn, reads, writes)

    def pe_tr(lst, reads, writes):
        def fn(e, lst=lst):
            ins = None
            for (o, i, idn) in lst:
                ins = e.transpose(o, i, idn)
            return ins
        return S.add('pe', fn, reads, writes)

    def act(out_ap, in_ap, func, reads, writes, **kw):
        def fn(e, out_ap=out_ap, in_ap=in_ap, func=func, kw=kw):
            return e.activation(out=out_ap, in_=in_ap, func=func, **kw)
        return S.add('act', fn, reads, writes)

    def ts(eng, out_ap, in0, s1, s2, op0, op1, reads, writes):
        def fn(e, out_ap=out_ap, in0=in0, s1=s1, s2=s2, op0=op0, op1=op1):
            if op1 is None:
                return e.tensor_scalar(out=out_ap, in0=in0, scalar1=s1, scalar2=None, op0=op0)
            return e.tensor_scalar(out=out_ap, in0=in0, scalar1=s1, scalar2=s2, op0=op0, op1=op1)
        return S.add(eng, fn, reads, writes)

    def tt(eng, out_ap, in0, in1, op, reads, writes):
        def fn(e, out_ap=out_ap, in0=in0, in1=in1, op=op):
            return e.tensor_tensor(out=out_ap, in0=in0, in1=in1, op=op)
        return S.add(eng, fn, reads, writes)

    def stt(out_ap, in0, sc, in1, op0, op1, reads, writes):
        def fn(e, out_ap=out_ap, in0=in0, sc=sc, in1=in1, op0=op0, op1=op1):
            return e.scalar_tensor_tensor(out=out_ap, in0=in0, scalar=sc, in1=in1, op0=op0, op1=op1)
        return S.add('dve', fn, reads, writes)

    def cp(eng, out_ap, in_ap, reads, writes):
        if eng == 'act':
            return act(out_ap, in_ap, AF.Copy, reads, writes)
        def fn(e, out_ap=out_ap, in_ap=in_ap):
            return e.tensor_copy(out=out_ap, in_=in_ap)
        return S.add(eng, fn, reads, writes)

    def memset(eng, ap, val, writes):
        def fn(e, ap=ap, val=val):
            return e.memset(ap, val)
        return S.add(eng, fn, (), writes)

    def psb(b):
        return ps[b][:]

    def psbf(b):
        return ps[b][:].bitcast(BF16)

    dma(sp_q, cst_sb, cst[:, :], (), ['cst'], 'cst')
    cp('dve', ident_b, ident_f, ['cst'], ['ident_b'])
    cp('dve', ones_b, ones_f, ['cst'], ['ones_b'])
    gsrc = [norm_mix[0], norm_ffn[0], norm_mix[1], norm_ffn[1], norm_final]
    for i, g in enumerate(gsrc):
        gv_ = g.rearrange("(k p o) -> k p o", p=128, o=1)
        for k in range(8):
            dma(sp_q, gvec[:, i, k:k + 1], gv_[k], (), ['gvec'], 'gvec')
    ts('dve', gvec, gvec, 32.0, None, ALU.mult, None, ['gvec'], ['gvec'])

    WK = [[('W', 0, p) for p in range(16)], [('W', 1, p) for p in range(16)]]

    def hoff(h):
        return PBYTES + h * HB

    WA_PIECES = [(LR, CB + 256), (GQ, GK), (AQ, AV), (GK, LR), (AV, DIN)]

    def wa_keys(h, c0, c1):
        ks = []
        for pi, (a, b_) in enumerate(WA_PIECES):
            if c0 < b_ and c1 > a:
                ks.append(WK[h][pi])
                ks.append(WK[h][pi + 5])
        return ks

    def load_w_A(l, h):
        base = hoff(h)
        win = view(base, BF16, [8, DIN])
        wout = view(base + 8 * DIN * 2, BF16, [8, D])
        src_in = w_in[l].rearrange("(k p) n -> p k n", p=128)
        src_out = w_out[l].rearrange("(k p) n -> p k n", p=128)
        for pi, (a, b_) in enumerate(WA_PIECES):
            for k0 in range(0, 8, 4):
                dma('pool', win[:, k0:k0 + 4, a:b_], src_in[:, k0:k0 + 4, a:b_], (), [WK[h][pi + 5 * (k0 // 4)]],
                    ('W', h))
        for k in range(0, 8, 4):
            dma('pool', wout[:, k:k + 4, :], src_out[:, k:k + 4, :], (), [WK[h][10 + k // 4]], ('W', h))
        return win, wout

    def load_w_B(l, half, h):
        base = hoff(h)
        wup = view(base, BF16, [8, 2048])
        wdn = view(base + 8 * 2048 * 2, BF16, [16, D])
        src_up = w_up[l].rearrange("(k p) n -> p k n", p=128)
        src_dn = w_down[l].rearrange("(j p) n -> p j n", p=128)
        for k in range(8):
            dma('pool', wup[:, k, :], src_up[:, k, half * 2048:(half + 1) * 2048], (), [WK[h][k]], ('W', h))
        for j in range(0, 16, 4):
            dma('pool', wdn[:, j:j + 4, :], src_dn[:, half * 16 + j:half * 16 + j + 4, :], (), [WK[h][8 + j // 4]],
                ('W', h))
        return wup, wdn


    state = {}

    def units(lst):
        return [[it] for it in lst]

    def merge(*streams):
        keyed = []
        for si, st in enumerate(streams):
            us, lo, hi = st[0], st[1], st[2]
            early = st[3] if len(st) > 3 else 0
            n = len(us)
            for j, u in enumerate(us):
                k = lo + (j + 0.5) / n * (hi - lo)
                if j < early:
                    k = -1.0
                keyed.append((k, si, j, u))
        keyed.sort(key=lambda t: (t[0], t[1], t[2]))
        outl = []
        for _, _, _, u in keyed:
            outl.extend(u)
        return outl

    def rms_stats(buf, gi, bank, inplace=False):
        act(hsq, hT[buf], AF.Square, [('hT', buf)], ['hsq'])
        lst = [(psb(bank)[:, 0:T], ones_b, hsq[:, k, :], k == 0, k == 7) for k in range(8)]
        pe_mms(lst, ['hsq', 'ones_b'], [('ps', bank)])
        if S.rec is not None:
            state['early_mark'] = len(S.rec)
        act(rstd_b, psb(bank)[:, 0:T], AF.Ln, [('ps', bank)], ['rstd'], bias=float(D * EPS))
        act(rstd_b, rstd_b, AF.Exp, ['rstd'], ['rstd'], scale=-0.5)
        for k in range(8):
            if inplace:
                stt(hT[buf][:, k, :], hT[buf][:, k, :], gvec[:, gi, k:k + 1], rstd_b, ALU.mult, ALU.mult,
                    [('hT', buf), 'rstd', 'gvec'], [('hT', buf)])
            else:
                stt(xnT[buf][:, k, :], hT[buf][:, k, :], gvec[:, gi, k:k + 1], rstd_b, ALU.mult, ALU.mult,
                    [('hT', buf), 'rstd', 'gvec'], [('xnT', buf)])


    def phase_A(l, hw, hsrc, hdst):
        hb = 1 - hw
        AKEY = lambda n: ('A', l, n)
        win, wout = state['wA']
        Hc = Carver(hoff(hb), hoff(hb) + HB)
        dg = Hc.get(BF16, [2, 31, 128])
        biasb = Hc.get(BF16, [6, 640])
        kT = Hc.get(BF16, [3, 1024])
        vr = Hc.get(BF16, [8, 6, 65])
        pT = [Hc.get(BF16, [640]) for _ in range(2)]
        sig = [Hc.get(F32, [T]) for _ in range(2)]
        cy = Hc.get(F32, [2, T])
        cysq = Hc.get(F32, [2, T])
        cm = Hc.get(F32, [T])
        cmsq = Hc.get(F32, [T])
        cvar = Hc.get(F32, [T])
        crs = Hc.get(F32, [T])
        cd = Hc.get(F32, [2, T])
        cpar = Hc.get(F32, [3, 2])
        wT = Hc.get(F32, [2, 31])
        wtmp = Hc.get(F32, [256], parts=32)
        g_e = Hc.get(F32, [192])
        g_sp = Hc.get(F32, [192])
        g_ed = Hc.get(F32, [192])
        kdec = Hc.get(BF16, [192])
        Sst = Hc.get(F32, [4, 96], parts=48)
        Sbf = [Hc.get(BF16, [4, 96], parts=48) for _ in range(2)]
        dec = Hc.get(F32, [4, 2], parts=48)
        gsq = Hc.get(F32, [384])
        gms = Hc.get(F32, [4])
        gon = Hc.get(BF16, [384])
        wg = Hc.get(F32, [192], parts=32)
        rden = Hc.get(F32, [3])
        aob = Hc.get(BF16, [384])
        gsg = Hc.get(F32, [384])
        ncp = Hc.get(F32, [2, 2])
        a_keys = [AKEY(n) for n in (['dg', 'bias'] + ['kT%d' % q for q in range(8)] + ['v%d' % q for q in range(8)] +
                                    ['pT0', 'pT1', 'sig0', 'sig1', 'cy', 'cysq', 'cm', 'cmsq', 'cvar', 'crs', 'cd',
                                     'cpar', 'wT', 'wtmp', 'g_e', 'g_sp', 'g_ed', 'kdec', 'S0', 'S1', 'S2', 'S3',
                                     'Sbf0', 'Sbf1', 'dec', 'gsq', 'gms', 'gon', 'wg', 'rden', 'aob', 'vones', 'gsg', 'ncp'])]
        S.fence(WK[hb], a_keys)
        S.fence(PB_KEYS, PA_KEYS)

        for i, src in enumerate((b_dw[l], conv_ln_g[l], conv_ln_b[l])):
            sv_ = src.rearrange("(b p o) -> b p o", p=128, o=1)
            for blk in range(2):
                dma(sp_q, cpar[:, i, blk:blk + 1], sv_[blk], (), [AKEY('cpar')], AKEY('cpar'))
        ts('dve', ncp, cpar[:, 1:3, :], -1.0, None, ALU.mult, None, [AKEY('cpar')], [AKEY('ncp')])
        dma(sp_q, wtmp[0:31, :], w_dw[l], (), [AKEY('wtmp')], AKEY('wtmp'))
        for blk in range(2):
            pe_tr([(psb(6)[:, blk * 32:blk * 32 + 31], wtmp[0:31, blk * 128:(blk + 1) * 128], ident_f[0:31, 0:31])],
                  [AKEY('wtmp'), 'cst'], [('ps', 6)])
        cp('dve', wT, psb(6)[:, 0:64].rearrange("p (b j) -> p b j", b=2)[:, :, 0:31], [('ps', 6)], [AKEY('wT')])
        for blk in range(2):
            tt('dve', dg[:, blk, :, :], ident_b.unsqueeze(1).to_broadcast([128, 31, 128]),
               wT[:, blk, :].unsqueeze(2).to_broadcast([128, 31, 128]), ALU.mult, [AKEY('wT'), 'ident_b'], [AKEY('dg')])
        memset('dve', hglu[0][:, :, 0:32], 0.0, [('hglu', 0)])
        dma(sp_q, wg[0:16, :], w_gla_gate[l], (), [AKEY('wg')], AKEY('wg'))
        dma(sp_q, wg[16:17, :], b_gla_gate[l].rearrange("(o n) -> o n", o=1), (), [AKEY('wg')], AKEY('wg'))
        dma(sp_q, gnb, gla_norm[l].partition_broadcast(128), (), ['gnb'], 'gnb')
        ts('dve', gnb, gnb, float(np.sqrt(96.0)), None, ALU.mult, None, ['gnb'], ['gnb'])
        for p_ in range(2):
            memset('dve', lrT[p_], 1.0, [('lrT', p_)])
        memset('dve', Sst, 0.0, [AKEY('S%d' % h) for h in range(4)])
        memset('dve', vr[:, :, :, 64:65], 1.0, [AKEY('vones')])
        dma(sp_q, ext[:, 0:256], rel_bias[l][:, 1:257], (), ['ext'], 'ext')
        dma(sp_q, btmp[0:6, 512:513], rel_bias[l][:, 256:257], (), ['btmp'], 'btmp',
            allow_slow_non_contiguous=True)
        cp('dve', btmp[0:6, 0:512], btmp[0:6, 512:513].to_broadcast([6, 512]), ['btmp'], ['btmp'])
        dma(sp_q, ext[:, 256:768], btmp[0:6, 0:512], ['btmp'], ['ext'], 'ext')
        for h in range(6):
            src = bass.AP(tensor=ext.tensor, offset=ext.offset + h * 768, ap=[[1, 128], [1, 640]])
            dma(sp_q, btmp, src, ['ext'], ['btmp'], 'btmp')
            pe_mms([(psb(4)[:, 0:512], J_f, btmp[:, 0:512], True, True),
                    (psb(5)[:, 0:128], J_f, btmp[:, 512:640], True, True)], ['btmp', 'cst'],
                   [('ps', 4), ('ps', 5)])
            cp('dve', biasb[:, h, 0:512], psb(4)[:, 0:512], [('ps', 4)], [AKEY('bias')])
            cp('dve', biasb[:, h, 512:640], psb(5)[:, 0:128], [('ps', 5)], [AKEY('bias')])
        memset('dve', biasb[64:128, :, 0:64], NEG, [AKEY('bias')])
        memset('dve', biasb[0:64, :, 576:640], NEG, [AKEY('bias')])
        act(biasb, biasb, AF.Exp, [AKEY('bias')], [AKEY('bias')])

        def load_x(i):
            dma(sp_q, otile, x[i * T:(i + 1) * T, :].rearrange("(s p) d -> p s d", p=128), (), ['otile'], 'otile')

        def stage_P(i):
            par = i % 2
            S.record()
            if l == 0:
                for s in range(2):
                    for kk in range(0, 8, 4):
                        b = kk // 4
                        pe_tr([(psb(b)[:, j * 128:(j + 1) * 128], otile[:, s, (kk + j) * 128:(kk + j + 1) * 128], ident_f)
                               for j in range(4)], ['otile', 'cst'], [('ps', b)])
                        cp('act' if kk == 0 else 'dve', hT[par][:, kk:kk + 4, s * 128:(s + 1) * 128],
                           psb(b).rearrange("p (j t) -> p j t", j=4), [('ps', b)], [('hT', par)])
                if i + 1 < ntiles:
                    load_x(i + 1)
            else:
                dma(sp_q, hT[par].rearrange("p k t -> p (k t)"), hsrc[i], [('hs', id(hsrc), i)], [('hT', par)],
                    ('hT', par))
            rms_stats(par, 2 * l, 0)
            XN = ('xnT', par)
            xn = xnT[par]
            cnt = [0]

            def nb():
                cnt[0] += 1
                return cnt[0] % 2

            def fm(cols, m, pbase=0):
                b = nb()
                pe_mms([(psb(b)[pbase:pbase + m, 0:T], win[:, k, cols:cols + m], xn[:, k, :], k == 0, k == 7)
                        for k in range(8)], [XN] + wa_keys(hw, cols, cols + m), [('ps', b)])
                return b
            b = fm(LR, 128)
            cp('dve', lrT[par][0:16, :], psb(b)[0:16, 0:T], [('ps', b)], [('lrT', par)])
            for h in range(4):
                b = fm(GQ + 48 * h, 48)
                ts('dve', gqT[par][h], psb(b)[0:48, 0:T], float(48 ** -0.5), None, ALU.mult, None, [('ps', b)],
                   [('gqT', par, h)])
            for blk in range(2):
                bb = fm(CB + 128 * blk, 128)
                act(sig[blk], psb(bb)[:, 0:T], AF.Exp, [('ps', bb)], [AKEY('sig%d' % blk)], scale=-1.0)
                act(sig[blk], sig[blk], AF.Ln, [AKEY('sig%d' % blk)], [AKEY('sig%d' % blk)], bias=1.0)
                act(sig[blk], sig[blk], AF.Exp, [AKEY('sig%d' % blk)], [AKEY('sig%d' % blk)], scale=-1.0)
            slot0 = (2 * i) % 8
            for pr in range(3):
                b = fm(AQ + 128 * pr, 128)
                ts('dve', aqT[par][:, pr, :], psb(b)[:, 0:T], 0.125, None, ALU.mult, None, [('ps', b)], [('aqT', par)])
            for pr in range(3):
                b = fm(AK + 128 * pr, 128)
                cp('dve', kT[:, pr, slot0 * 128:slot0 * 128 + T], psb(b)[:, 0:T], [('ps', b)],
                   [AKEY('kT%d' % slot0), AKEY('kT%d' % (slot0 + 1))])
            for blk in range(2):
                ba = fm(CA + 128 * blk, 128)
                tt('dve', hglu[par][:, blk, 32:32 + T], psb(ba)[:, 0:T], sig[blk], ALU.mult,
                   [('ps', ba), AKEY('sig%d' % blk)], [('hglu', par)])
            if i > 0:
                cp('pool', hglu[par][:, :, 0:32], hglu[1 - par][:, :, T:T + 32], [('hglu', 1 - par)], [('hglu', par)])
            for s in range(2):
                g = 2 * i + s
                slot = g % 8

                def tm(cols, n):
                    b = nb()
                    pe_mms([(psb(b)[:, 0:n], xn[:, k, s * 128:(s + 1) * 128], win[:, k, cols:cols + n], k == 0, k == 7)
                            for k in range(8)], [XN] + wa_keys(hw, cols, cols + n), [('ps', b)])
                    return b
                b = tm(GG, 384)
                cp('dve', gla_g[par][:, s, :], psb(b)[:, 0:384], [('ps', b)], [('gla_g', par, s)])
                act(gsg, gla_g[par][:, s, :], AF.Exp, [('gla_g', par, s)], [AKEY('gsg')], scale=-1.0)
                act(gsg, gsg, AF.Ln, [AKEY('gsg')], [AKEY('gsg')], bias=1.0)
                act(gsg, gsg, AF.Exp, [AKEY('gsg')], [AKEY('gsg')], scale=-1.0)
                tt('pool', gla_g[par][:, s, :], gla_g[par][:, s, :], gsg, ALU.mult, [('gla_g', par, s), AKEY('gsg')],
                   [('gla_g', par, s)])
                b = tm(GK, 192)
                cp('dve', gla_k[par][:, s, :], psb(b)[:, 0:192], [('ps', b)], [('gla_k', par, s)])
                b = tm(GV, 384)
                cp('act', gla_v[par][:, s, :], psb(b)[:, 0:384], [('ps', b)], [('gla_v', par, s)])
                b = tm(AV, 384)
                cp('dve', vr[:, slot, :, 0:64], psb(b)[:, 0:384].rearrange("p (h d) -> p h d", h=6), [('ps', b)],
                   [AKEY('v%d' % slot)])
            return S.stop()

        def stage_G(i):
            par = i % 2
            S.record()
            B2 = [('ps', 2)]
            for s in range(2):
                pe_mms([(psb(2)[:, 0:192], lrT[par][0:17, s * 128:(s + 1) * 128], wg[0:17, :], True, True)],
                       [('lrT', par), AKEY('wg')], B2)
                act(g_e, psb(2)[:, 0:192], AF.Exp, B2, [AKEY('g_e')], scale=-1.0)
                act(g_sp, g_e, AF.Ln, [AKEY('g_e')], [AKEY('g_sp')], bias=1.0)
                pe_mms([(psb(2)[:, 0:192], tri_f, g_sp, True, True)] +
                       [(psb(2)[0:48, 192 + 2 * h:192 + 2 * h + 2], g_sp[:, 48 * h:48 * h + 48], ind_f, True, True)
                        for h in range(4)], [AKEY('g_sp'), 'cst'], B2)
                act(g_ed, psb(2)[:, 0:192], AF.Exp, B2, [AKEY('g_ed')])
                act(dec, psb(2)[0:48, 192:200].rearrange("p (h c) -> p h c", h=4), AF.Exp, B2, [AKEY('dec')])
                tt('dve', kdec, gla_k[par][:, s, :], g_ed, ALU.mult, [('gla_k', par, s), AKEY('g_ed')], [AKEY('kdec')])
                for c in range(2):
                    sb = c
                    pe_mms([(psb(2)[0:48, 96 * h:96 * h + 96], kdec[c * 64:(c + 1) * 64, 48 * h:48 * h + 48],
                             gla_v[par][c * 64:(c + 1) * 64, s, 96 * h:96 * h + 96], True, True) for h in range(4)],
                           [AKEY('kdec'), ('gla_v', par, s)], B2)
                    for h in range(4):
                        stt(Sst[:, h, :], Sst[:, h, :], dec[:, h, c:c + 1], psb(2)[0:48, 96 * h:96 * h + 96],
                            ALU.mult, ALU.add, [AKEY('S%d' % h), AKEY('dec'), ('ps', 2)], [AKEY('S%d' % h)])
                    cp('act', Sbf[sb], Sst, [AKEY('S%d' % h) for h in range(4)], [AKEY('Sbf%d' % sb)])
                    pe_mms([(psb(2)[c * 64:(c + 1) * 64, 96 * h:96 * h + 96],
                             gqT[par][h][:, s * 128 + c * 64:s * 128 + c * 64 + 64], Sbf[sb][:, h, :], True, True)
                            for h in range(4)], [('gqT', par, h) for h in range(4)] + [AKEY('Sbf%d' % sb)], B2)
                    cp('act', go[c * 64:(c + 1) * 64, :], psb(2)[c * 64:(c + 1) * 64, 0:384], B2, ['go'])
                tt('pool', gsq, go, go, ALU.mult, ['go'], [AKEY('gsq')])

                def red(e, gms=gms, gsq=gsq):
                    return e.tensor_reduce(out=gms, in_=gsq.rearrange("p (h v) -> p h v", h=4), axis=AX.X, op=ALU.add)
                S.add('dve', red, [AKEY('gsq')], [AKEY('gms')])
                act(gms, gms, AF.Ln, [AKEY('gms')], [AKEY('gms')], bias=float(96 * EPS))
                act(gms, gms, AF.Exp, [AKEY('gms')], [AKEY('gms')], scale=-0.5)
                tt('pool', go, go, gnb, ALU.mult, ['go', 'gnb'], ['go'])
                tt('pool', go, go, gla_g[par][:, s, :], ALU.mult, ['go', ('gla_g', par, s)], ['go'])
                tt('dve', gon.rearrange("p (h v) -> p h v", h=4), go.rearrange("p (h v) -> p h v", h=4),
                   gms.unsqueeze(2).to_broadcast([128, 4, 96]), ALU.mult, ['go', AKEY('gms')], [AKEY('gon')])
                pe_tr([(psbf(2)[:, j * 128:(j + 1) * 128], gon[:, j * 128:(j + 1) * 128], ident_b) for j in range(3)],
                      [AKEY('gon'), 'ident_b'], B2)
                cp('dve', mixT[:, 0:3, s * 128:(s + 1) * 128], psbf(2)[:, 0:384].rearrange("p (j t) -> p j t", j=3),
                   B2, [('mixT', 'g', s)])
            B3 = [('ps', 2)]
            pe_mms([(psb(2)[:, blk * T:(blk + 1) * T], dg[:, blk, j, :], hglu[par][:, blk, 2 + j:2 + j + T],
                     j == 0, j == 30) for blk in range(2) for j in range(31)], [AKEY('dg'), ('hglu', par)], B3)
            for blk in range(2):
                act(cy[:, blk, :], psb(2)[:, blk * T:(blk + 1) * T], AF.Identity, B3 + [AKEY('cpar')],
                    [AKEY('cy')], bias=cpar[:, 0, blk:blk + 1])
                act(cysq[:, blk, :], psb(2)[:, blk * T:(blk + 1) * T], AF.Square, B3 + [AKEY('cpar')],
                    [AKEY('cysq')], bias=cpar[:, 0, blk:blk + 1])
            pe_mms([(psb(2)[:, 0:T], ones_f, cy[:, 0, :], True, False), (psb(2)[:, 0:T], ones_f, cy[:, 1, :], False, True),
                    (psb(2)[:, T:2 * T], ones_f, cysq[:, 0, :], True, False),
                    (psb(2)[:, T:2 * T], ones_f, cysq[:, 1, :], False, True)],
                   [AKEY('cy'), AKEY('cysq'), 'cst'], B3)
            ts('dve', cm, psb(2)[:, 0:T], 1.0 / 256.0, None, ALU.mult, None, B3, [AKEY('cm')])
            tt('dve', cmsq, cm, cm, ALU.mult, [AKEY('cm')], [AKEY('cmsq')])
            stt(cvar, psb(2)[:, T:2 * T], 1.0 / 256.0, cmsq, ALU.mult, ALU.subtract, B3 + [AKEY('cmsq')],
                [AKEY('cvar')])
            act(crs, cvar, AF.Ln, [AKEY('cvar')], [AKEY('crs')], bias=float(EPS))
            act(crs, crs, AF.Exp, [AKEY('crs')], [AKEY('crs')], scale=-0.5)
            for blk in range(2):
                tt('pool', cd[:, blk, :], cy[:, blk, :], cm, ALU.subtract, [AKEY('cy'), AKEY('cm')], [AKEY('cd')])
                tt('pool', cd[:, blk, :], cd[:, blk, :], crs, ALU.mult, [AKEY('cd'), AKEY('crs')], [AKEY('cd')])
                act(cysq[:, blk, :], cd[:, blk, :], AF.Exp, [AKEY('cd'), AKEY('ncp')], [AKEY('cysq')],
                    scale=ncp[:, 0, blk:blk + 1], bias=ncp[:, 1, blk:blk + 1])
                act(cysq[:, blk, :], cysq[:, blk, :], AF.Ln, [AKEY('cysq')], [AKEY('cysq')], bias=1.0)
                act(cysq[:, blk, :], cysq[:, blk, :], AF.Exp, [AKEY('cysq')], [AKEY('cysq')], scale=-1.0)
                ts('pool', cd[:, blk, :], cd[:, blk, :], cpar[:, 1, blk:blk + 1], cpar[:, 2, blk:blk + 1], ALU.mult,
                   ALU.add, [AKEY('cd'), AKEY('cpar')], [AKEY('cd')])
                tt('dve', mixT[:, 3 + blk, :], cd[:, blk, :], cysq[:, blk, :], ALU.mult, [AKEY('cd'), AKEY('cysq')],
                   [('mixT', 'c', blk)])
            return S.stop()

        def stage_Att(i):
            par = i % 2
            S.record()

            def scores(s, h):
                g = 2 * i + s
                nj = min(5, g + 1)
                pr, p0 = h // 2, 64 * (h % 2)
                sb = h % 2
                bA = 4 if sb == 0 else 6
                bB = bA + 1
                kB = ('ps', bB)
                cB = 0
                q_ap = aqT[par][p0:p0 + 64, pr, s * 128:(s + 1) * 128]
                lst = []
                for j in range(min(nj, 4)):
                    sl = (g - j) % 8
                    lst.append((psb(bA)[:, j * 128:(j + 1) * 128], kT[p0:p0 + 64, pr, sl * 128:(sl + 1) * 128],
                                q_ap, True, True))
                rd = [('aqT', par)] + [AKEY('kT%d' % ((g - j) % 8)) for j in range(nj)]
                wr = [('ps', bA)]
                if nj == 5:
                    sl = (g - 4) % 8
                    lst.append((psb(bB)[:, cB:cB + 128], kT[p0:p0 + 64, pr, sl * 128:(sl + 1) * 128], q_ap, True, True))
                    wr.append(kB)
                pe_mms(lst, rd, wr)
                na = min(nj, 4) * 128
                act(pT[sb][:, 0:na], psb(bA)[:, 0:na], AF.Exp, [('ps', bA)], [AKEY('pT%d' % sb)])
                if nj == 5:
                    act(pT[sb][:, 512:640], psb(bB)[:, cB:cB + 128], AF.Exp, [kB], [AKEY('pT%d' % sb)])
                tt('dve', pT[sb][:, 0:nj * 128], pT[sb][:, 0:nj * 128], biasb[:, h, 0:nj * 128], ALU.mult,
                   [AKEY('pT%d' % sb), AKEY('bias')], [AKEY('pT%d' % sb)])

            def pv(s, h):
                g = 2 * i + s
                nj = min(5, g + 1)
                sb = h % 2
                hb3, hh = h // 3, h % 3
                pe_mms([(psb(3)[:, 65 * hh:65 * hh + 65], pT[sb][:, j * 128:(j + 1) * 128],
                         vr[:, (g - j) % 8, h, :], j == 0, j == nj - 1) for j in range(nj)],
                       [AKEY('pT%d' % sb), AKEY('vones')] + [AKEY('v%d' % ((g - j) % 8)) for j in range(nj)],
                       [('ps', 3)])
                if hh == 2:
                    o3 = psb(3)[:, 0:195].rearrange("p (h d) -> p h d", h=3)

                    def rcp(e, rden=rden, o3=o3):
                        return e.reciprocal(out=rden.unsqueeze(2), in_=o3[:, :, 64:65])
                    S.add('dve', rcp, [('ps', 3)], [AKEY('rden')])
                    tt('dve', aob[:, hb3 * 192:(hb3 + 1) * 192].rearrange("p (h d) -> p h d", h=3), o3[:, :, 0:64],
                       rden.unsqueeze(2).to_broadcast([128, 3, 64]), ALU.mult, [('ps', 3), AKEY('rden')],
                       [AKEY('aob')])
                if h == 5:
                    pe_tr([(psbf(3)[:, j * 128:(j + 1) * 128], aob[:, j * 128:(j + 1) * 128], ident_b)
                           for j in range(3)], [AKEY('aob'), 'ident_b'], [('ps', 3)])
                    cp('dve', mixT[:, 5:8, s * 128:(s + 1) * 128],
                       psbf(3)[:, 0:384].rearrange("p (j t) -> p j t", j=3), [('ps', 3)], [('mixT', 'a', s)])

            pairs = [(s, h) for s in range(2) for h in range(6)]
            scores(*pairs[0])
            for k in range(len(pairs)):
                if k + 1 < len(pairs):
                    scores(*pairs[k + 1])
                pv(*pairs[k])
            return S.stop()

        MIXK = [('mixT', 'g', 0), ('mixT', 'g', 1), ('mixT', 'c', 0), ('mixT', 'c', 1), ('mixT', 'a', 0), ('mixT', 'a', 1)]

        def stage_O(i):
            par = i % 2
            S.record()
            for m in range(8):
                b = 2 + (m % 2)
                pe_mms([(psb(b)[:, 0:T], wout[:, k, m * 128:(m + 1) * 128], mixT[:, k, :], k == 0, k == 7)
                        for k in range(8)], MIXK + WK[hw][10:12], [('ps', b)])
                tt('dve', hT[par][:, m, :], hT[par][:, m, :], psb(b)[:, 0:T], ALU.add, [('hT', par), ('ps', b)],
                   [('hT', par)])
            dma(sp_q, hdst[i], hT[par].rearrange("p k t -> p (k t)"), [('hT', par)], [('hs', id(hdst), i)],
                ('hTst', par))
            return S.stop()

        if l == 0:
            load_x(0)
        S.replay(stage_P(0))
        for i in range(ntiles):
            pn = units(stage_P(i + 1)) if i + 1 < ntiles else []
            n_early = state.get('early_mark', 0) if pn else 0
            gu = units(stage_G(i))
            au = units(stage_Att(i))
            S.replay(merge((pn, 0.25, 1.0, n_early), (gu, 0.0, 0.9), (au, 0.0, 0.9)))
            S.replay(stage_O(i))
        return a_keys

    def phase_B(l, half, hw, hsrc, hdst, final=False):
        wup, wdn = state['wB']
        S.fence(PA_KEYS, PB_KEYS)

        def stage_X(i):
            par = i % 2
            dma(sp_q, hT[par].rearrange("p k t -> p (k t)"), hsrc[i], [('hs', id(hsrc), i)], [('hT', par)], ('hT', par))
            if half == 1:
                dma(sp_q, xnT[par].rearrange("p k t -> p (k t)"), xs[i], [('xs', i)], [('xnT', par)], ('xnT', par))
            else:
                rms_stats(par, 2 * l + 1, 0)
                dma(sp_q, xs[i], xnT[par].rearrange("p k t -> p (k t)"), [('xnT', par)], [('xs', i)], ('xnTst', par))

        def stage_U(i):
            par = i % 2
            S.record()
            for j in range(16):
                b = j % 4
                pe_mms([(psb(b)[:, 0:T], wup[:, k, j * 128:(j + 1) * 128], xnT[par][:, k, :], k == 0, k == 7)
                        for k in range(8)], [('xnT', par)] + WK[hw], [('ps', b)])
                r = relu_t[j % 2]
                act(r, psb(b)[:, 0:T], AF.Relu, [('ps', b)], [('relu', j % 2)])
                tt('pool' if j % 2 == 0 else 'dve', hidT[par][:, j, :], r, r, ALU.mult, [('relu', j % 2)],
                   [('hid', par, j)])
            return S.stop()

        def stage_D(i):
            par = i % 2
            for m in range(8):
                b = 4 + (m % 4)
                pe_mms([(psb(b)[:, 0:T], wdn[:, j, m * 128:(m + 1) * 128], hidT[par][:, j, :], j == 0, j == 15)
                        for j in range(16)], [('hid', par, j) for j in range(16)] + WK[hw], [('ps', b)])
                tt('dve', hT[par][:, m, :], hT[par][:, m, :], psb(b)[:, 0:T], ALU.add, [('hT', par), ('ps', b)],
                   [('hT', par)])
            if not final:
                dma(sp_q, hdst[i], hT[par].rearrange("p k t -> p (k t)"), [('hT', par)], [('hs', id(hdst), i)],
                    ('hTst', par))

        def stage_F(i):
            par = i % 2
            S.record()
            rms_stats(par, 4, 4, inplace=True)
            for s in range(2):
                for kk in range(0, 8, 4):
                    b = 5 + (kk // 4)
                    pe_tr([(psb(b)[:, j * 128:(j + 1) * 128], hT[par][:, kk + j, s * 128:(s + 1) * 128], ident_f)
                           for j in range(4)], [('hT', par), 'cst'], [('ps', b)])
                    cp('act' if kk == 0 else 'dve', otile[:, s, kk * 128:(kk + 4) * 128], psb(b), [('ps', b)],
                       ['otile'])
            dma(sp_q, out[i * T:(i + 1) * T, :].rearrange("(s p) d -> p s d", p=128), otile, ['otile'], ['out'],
                'otile_st')
            return S.stop()

        stage_X(0)
        S.replay(stage_U(0))
        for i in range(ntiles):
            if i + 1 < ntiles:
                stage_X(i + 1)
            stage_D(i)
            un = units(stage_U(i + 1)) if i + 1 < ntiles else []
            fu = units(stage_F(i)) if final else []
            S.replay(merge((un, 0.0, 1.0), (fu, 0.0, 1.0)))

    def dump_dbg(hsbuf):
        dbg = nc.dram_tensor("dbg", [NT, 128, 8 * T], F32, kind="ExternalOutput").ap()
        for i in range(ntiles):
            dma(sp_q, hT[0].rearrange("p k t -> p (k t)"), hsbuf[i], [('hs', id(hsbuf), i)], [('hT', 0)], ('hT', 0))
            dma(sp_q, dbg[i], hT[0].rearrange("p k t -> p (k t)"), [('hT', 0)], ['dbg'], 'dbg')

    cur = 0
    wA = load_w_A(0, 0)
    for l in range(DEPTH):
        hw = l % 2
        state['wA'] = wA
        if l == 0:
            akeys = phase_A(l, hw, None, hs[0])
            cur = 0
        else:
            akeys = phase_A(l, hw, hs[cur], hs[1 - cur])
            cur = 1 - cur
        if stop_after == (l, 'A'):
            dump_dbg(hs[cur])
            break
        S.fence(akeys, WK[1 - hw])
        wB1 = load_w_B(l, 0, 1 - hw)
        wB2 = load_w_B(l, 1, hw)
        state['wB'] = wB1
        phase_B(l, 0, 1 - hw, hs[cur], hs[1 - cur])
        cur = 1 - cur
        if stop_after == (l, 'B1'):
            dump_dbg(hs[cur])
            break
        if l + 1 < DEPTH:
            wA = load_w_A(l + 1, 1 - hw)
        state['wB'] = wB2
        fin = (l == DEPTH - 1 and stop_after is None)
        phase_B(l, 1, hw, hs[cur], hs[1 - cur], final=fin)
        cur = 1 - cur
        if stop_after == (l, 'B2'):
            dump_dbg(hs[cur])
            break

    S.emit(nc, es)
    es.close()
    return nc


_NC_CACHE = {}


def kernel(**inputs):
    key = 'full'
    if key not in _NC_CACHE:
        _NC_CACHE[key] = build()
    nc = _NC_CACHE[key]
    cstv = make_consts()
    names = ["norm_mix", "w_in", "w_gla_gate", "b_gla_gate", "gla_norm", "w_dw", "b_dw", "conv_ln_g", "conv_ln_b",
             "rel_bias", "w_out", "norm_ffn", "w_up", "w_down", "norm_final"]
    shared = {n: np.ascontiguousarray(np.asarray(inputs[n], dtype=np.float32)) for n in names}
    xfull = np.asarray(inputs["x"], dtype=np.float32)
    in_maps = []
    for c in range(8):
        m = dict(shared)
        m["x"] = np.ascontiguousarray(xfull[c])
        m["cst"] = cstv
        in_maps.append(m)
    res = run_bass_kernel_spmd(nc, in_maps, core_ids=list(range(8)))
    return np.stack([np.asarray(r["out"], dtype=np.float32) for r in res.results], axis=0)
```
